# Optimizing a Trainium2 kernel written in Bass

```python
import math
import jax
import jax.numpy as jnp
from jax import lax
import numpy as np

D_MODEL = 2048
BATCH = 8
SEQ = 2048
DEPTH = 2

GRID_W = 64
CTX_LEN = 256
N_MOD = 9
FFN_DIM = 5632
A_HEADS = 8
A_HEAD_DIM = 64
A_VAL_DIM = 2 * A_HEAD_DIM
B_HEADS = 8
B_HEAD_DIM = 128
WIN_R = 8
WIN_C = 16
C_HEADS = 8
C_KEY_DIM = 64
C_VAL_DIM = 128
RET_CHUNK = 64
BRANCH_W = 1024
N_BRANCH = 3
ATTN_BLOCK = 128
ROPE_BASE = 10000.0
EPS = 1e-6
SPLIT_SIZES = (
    A_HEADS * 2 * A_HEAD_DIM,
    A_HEADS * 2 * A_HEAD_DIM,
    A_HEADS * A_VAL_DIM,
    B_HEADS * B_HEAD_DIM,
    B_HEADS * B_HEAD_DIM,
    B_HEADS * B_HEAD_DIM,
    C_HEADS * C_KEY_DIM,
    C_HEADS * C_KEY_DIM,
    C_HEADS * C_VAL_DIM,
    C_HEADS * C_VAL_DIM,
    N_BRANCH * D_MODEL,
)
IN_COLS = sum(SPLIT_SIZES)

kernel_name = 'hybrid_diffusion_prefix_trunk'


def rmsnorm(x, g):
    x32 = x.astype(jnp.float32)
    y = x32 * lax.rsqrt(jnp.mean(x32 * x32, axis=-1, keepdims=True) + EPS)
    return (y * g.astype(jnp.float32)).astype(x.dtype)


def adaln_in(t, gain, shift, scale):
    return rmsnorm(t, gain) * (1 + scale) + shift


def swiglu(u, w_in, w_out):
    a, b = jnp.split(u @ w_in, 2, axis=-1)
    return (jax.nn.silu(a) * b) @ w_out


def half_ffn_sublayer(t, pre_g, post_g, shift, scale, gate, w1, w2):
    u = adaln_in(t, pre_g, shift, scale)
    return t + 0.5 * gate * rmsnorm(swiglu(u, w1, w2), post_g)


def to_heads(t, n_heads):
    b, l, _ = t.shape
    return t.reshape(b, l, n_heads, -1).transpose(0, 2, 1, 3)


def from_heads(t):
    b, h, l, d = t.shape
    return t.transpose(0, 2, 1, 3).reshape(b, l, h * d)


def rope_1d(t, pos):
    half = t.shape[-1] // 2
    inv = ROPE_BASE ** (-jnp.arange(half, dtype=jnp.float32) / half)
    ang = pos[:, None] * inv[None, :]
    cos = jnp.cos(ang).astype(t.dtype)
    sin = jnp.sin(ang).astype(t.dtype)
    t1, t2 = t[..., :half], t[..., half:]
    return jnp.concatenate([t1 * cos - t2 * sin, t1 * sin + t2 * cos], axis=-1)


def axial_rope(t, prow, pcol):
    half = t.shape[-1] // 2
    return jnp.concatenate([rope_1d(t[..., :half], prow), rope_1d(t[..., half:], pcol)], axis=-1)


def softmax32(s):
    return jax.nn.softmax(s.astype(jnp.float32), axis=-1)


def diff_attention(qc, kc, vc, ql, kl, vl, lam, norm_g, lambda_init, with_ctx_out):
    scale = A_HEAD_DIM ** -0.5

    def attend(q, k, v):
        s1 = jnp.einsum('bhqd,bhkd->bhqk', q[..., :A_HEAD_DIM], k[..., :A_HEAD_DIM]) * scale
        s2 = jnp.einsum('bhqd,bhkd->bhqk', q[..., A_HEAD_DIM:], k[..., A_HEAD_DIM:]) * scale
        p = softmax32(s1) - lam * softmax32(s2)
        return jnp.einsum('bhqk,bhkv->bhqv', p.astype(v.dtype), v)

    def finish(o):
        return from_heads(rmsnorm(o, norm_g) * (1.0 - lambda_init))

    b, h, l, dq = ql.shape
    nb = l // ATTN_BLOCK
    k_all = jnp.concatenate([kc, kl], axis=2)
    v_all = jnp.concatenate([vc, vl], axis=2)
    q_blocks = ql.reshape(b, h, nb, ATTN_BLOCK, dq).transpose(2, 0, 1, 3, 4)
    ol = lax.map(lambda qb: attend(qb, k_all, v_all), q_blocks)
    ol = ol.transpose(1, 2, 0, 3, 4).reshape(b, h, l, -1)
    oc = finish(attend(qc, kc, vc)) if with_ctx_out else None
    return oc, finish(ol)


def neighbourhood_attention(qc, kc, vc, ql, kl, vl, rpb, rows, with_ctx_out):
    b, h, l, d = ql.shape
    scale = d ** -0.5
    wr = min(WIN_R, rows)
    ncb = GRID_W // WIN_C
    band = 2 * WIN_C
    qcol = jnp.arange(GRID_W).reshape(ncb, WIN_C)
    kcol = (jnp.clip(jnp.arange(ncb) * WIN_C - WIN_C // 2, 0, GRID_W - band)[:, None]
            + jnp.arange(band)[None, :])
    cstart = jnp.clip(qcol - WIN_C // 2, 0, GRID_W - WIN_C)
    col_ok = ((kcol[:, None, :] >= cstart[:, :, None])
              & (kcol[:, None, :] < cstart[:, :, None] + WIN_C))
    dc_idx = jnp.clip(kcol[:, None, :] - qcol[:, :, None] + WIN_C - 1, 0, 2 * WIN_C - 2)
    qg, kg, vg = (t.reshape(b, h, rows, GRID_W, d) for t in (ql, kl, vl))

    def row_block(r):
        rs = jnp.clip(r - wr // 2, 0, rows - wr)
        krow = lax.dynamic_slice_in_dim(kg, rs, wr, axis=2)[:, :, :, kcol]
        vrow = lax.dynamic_slice_in_dim(vg, rs, wr, axis=2)[:, :, :, kcol]
        qb = lax.dynamic_index_in_dim(qg, r, axis=2, keepdims=False).reshape(b, h, ncb, WIN_C, d)
        dr_idx = rs + jnp.arange(wr) - r + WIN_R - 1
        bias = rpb[:, dr_idx[None, None, :, None], dc_idx[:, :, None, :]]
        s_lat = (jnp.einsum('bhjqd,bhrjkd->bhjqrk', qb, krow).astype(jnp.float32) * scale
                 + bias.astype(jnp.float32))
        s_lat = jnp.where(col_ok[:, :, None, :], s_lat, -jnp.inf).reshape(b, h, ncb, WIN_C, wr * band)
        s_ctx = jnp.einsum('bhjqd,bhkd->bhjqk', qb, kc).astype(jnp.float32) * scale
        p = softmax32(jnp.concatenate([s_lat, s_ctx], axis=-1)).astype(vl.dtype)
        p_lat = p[..., :wr * band].reshape(b, h, ncb, WIN_C, wr, band)
        o = (jnp.einsum('bhjqrk,bhrjkd->bhjqd', p_lat, vrow)
             + jnp.einsum('bhjqk,bhkd->bhjqd', p[..., wr * band:], vc))
        return o.reshape(b, h, GRID_W, d)

    ol = lax.map(row_block, jnp.arange(rows))
    ol = from_heads(ol.transpose(1, 2, 0, 3, 4).reshape(b, h, l, d))
    oc = None
    if with_ctx_out:
        pc = softmax32(jnp.einsum('bhqd,bhkd->bhqk', qc, kc) * scale).astype(vc.dtype)
        oc = from_heads(jnp.einsum('bhqk,bhkd->bhqd', pc, vc))
    return oc, ol


def retention_chunkwise(q, k, v, log_g, s0):
    b, h, l, dk = q.shape
    dv = v.shape[-1]
    n = l // RET_CHUNK
    idx = jnp.arange(RET_CHUNK, dtype=jnp.float32)
    lg = log_g[:, None]
    dist = idx[:, None] - idx[None, :]
    intra = jnp.where(dist >= 0, jnp.exp(lg[:, :, None] * jnp.maximum(dist, 0.0)), 0.0)
    q_dec = jnp.exp(lg * (idx + 1.0))[..., None]
    k_dec = jnp.exp(lg * (RET_CHUNK - 1.0 - idx))[..., None]
    c_dec = jnp.exp(lg * RET_CHUNK)[..., None]

    def chunks(t):
        return t.reshape(b, h, n, RET_CHUNK, -1).transpose(2, 0, 1, 3, 4)

    def step(s, inp):
        qi, ki, vi = inp
        att = jnp.einsum('bhqd,bhkd->bhqk', qi, ki) * intra
        o = (jnp.einsum('bhqk,bhkv->bhqv', att, vi)
             + jnp.einsum('bhqd,bhdv->bhqv', qi * q_dec, s))
        s = s * c_dec + jnp.einsum('bhkd,bhkv->bhdv', ki * k_dec, vi)
        return s, o

    s, o = lax.scan(step, s0, (chunks(q), chunks(k), chunks(v)))
    return o.transpose(1, 2, 0, 3, 4).reshape(b, h, l, dv), s


def retention(qc, kc, vc, ql, kl, vl, decay_logit, norm_g, with_ctx_out):
    log_g = jax.nn.log_sigmoid(decay_logit.astype(jnp.float32))
    qc, kc, vc, ql, kl, vl = (t.astype(jnp.float32) for t in (qc, kc, vc, ql, kl, vl))
    flip = lambda t: jnp.flip(t, axis=2)
    b, h, _, dk = ql.shape
    s0 = jnp.zeros((b, h, dk, vl.shape[-1]), jnp.float32)
    oc_f, sc_f = retention_chunkwise(qc, kc, vc, log_g[0], s0)
    oc_b, sc_b = retention_chunkwise(flip(qc), flip(kc), flip(vc), log_g[1], s0)
    ol_f, _ = retention_chunkwise(ql, kl, vl, log_g[0], sc_f)
    ol_b, _ = retention_chunkwise(flip(ql), flip(kl), flip(vl), log_g[1], sc_b)
    finish = lambda o: from_heads(rmsnorm(o, norm_g))
    oc = finish(oc_f + flip(oc_b)) if with_ctx_out else None
    return oc, finish(ol_f + flip(ol_b))


def token_mixer(uc, ul, w_in, diff_lambda, diff_norm, na_rpb, ret_decay, ret_norm, w_branch, w_out,
                lambda_init, with_ctx_out):
    b, l, _ = ul.shape
    rows = l // GRID_W
    pos = jnp.arange(l)
    prow = (pos // GRID_W).astype(jnp.float32)
    pcol = (pos % GRID_W).astype(jnp.float32)
    offsets = [int(o) for o in np.cumsum(SPLIT_SIZES)[:-1]]
    aq, ak, av, bq, bk, bv, cq, ck, cv, cg, gates = jnp.split(ul @ w_in, offsets, axis=-1)
    aqc, akc, avc, bqc, bkc, bvc, cqc, ckc, cvc, cgc, gatesc = jnp.split(uc @ w_in, offsets, axis=-1)
    rope = lambda t: axial_rope(t, prow, pcol)
    rope2 = lambda t: jnp.concatenate([rope(t[..., :A_HEAD_DIM]), rope(t[..., A_HEAD_DIM:])], axis=-1)

    lv = diff_lambda.astype(jnp.float32)
    lam = jnp.exp(jnp.sum(lv[0] * lv[1])) - jnp.exp(jnp.sum(lv[2] * lv[3])) + lambda_init
    oa_c, oa_l = diff_attention(
        to_heads(aqc, A_HEADS), to_heads(akc, A_HEADS), to_heads(avc, A_HEADS),
        rope2(to_heads(aq, A_HEADS)), rope2(to_heads(ak, A_HEADS)), to_heads(av, A_HEADS),
        lam, diff_norm, lambda_init, with_ctx_out)

    ob_c, ob_l = neighbourhood_attention(
        to_heads(bqc, B_HEADS), to_heads(bkc, B_HEADS), to_heads(bvc, B_HEADS),
        to_heads(bq, B_HEADS), to_heads(bk, B_HEADS), to_heads(bv, B_HEADS),
        na_rpb, rows, with_ctx_out)

    kscale = C_KEY_DIM ** -0.5
    or_c, or_l = retention(
        to_heads(cqc, C_HEADS), to_heads(ckc, C_HEADS) * kscale, to_heads(cvc, C_HEADS),
        rope(to_heads(cq, C_HEADS)), rope(to_heads(ck, C_HEADS)) * kscale, to_heads(cv, C_HEADS),
        ret_decay, ret_norm, with_ctx_out)

    def merge(oa, ob, orr, g_ret, g_lin):
        yr = jax.nn.silu(g_ret) * orr.astype(g_ret.dtype)
        ga, gb, gr = jnp.split(jax.nn.sigmoid(g_lin), N_BRANCH, axis=-1)
        merged = ga * (oa @ w_branch[0]) + gb * (ob @ w_branch[1]) + gr * (yr @ w_branch[2])
        return merged @ w_out

    yl = merge(oa_l, ob_l, or_l, cg, gates)
    yc = merge(oa_c, ob_c, or_c, cgc, gatesc) if with_ctx_out else None
    return yc, yl


def setup_inputs(seed: int = 0) -> dict:
    key = jax.random.key(seed)
    ks = jax.random.split(key, 18)
    f32 = jnp.float32

    def normal(k, shape):
        return jax.random.normal(k, shape, f32)

    def dense(k, shape, fan_in, gain=1.0):
        return normal(k, shape) * (gain * fan_in ** -0.5)

    def near_one(k, shape):
        return 1.0 + 0.02 * normal(k, shape)

    gamma0 = 1.0 - 2.0 ** (-5.0 - jnp.arange(C_HEADS, dtype=f32))
    decay_logit = jnp.log(gamma0) - jnp.log1p(-gamma0)
    return {
        'x': normal(ks[0], (BATCH, SEQ, D_MODEL)),
        'c': normal(ks[1], (BATCH, D_MODEL)),
        'ctx': normal(ks[2], (BATCH, CTX_LEN, D_MODEL)),
        'c_ctx': normal(ks[3], (D_MODEL,)),
        'w_mod': dense(ks[4], (DEPTH, D_MODEL, N_MOD * D_MODEL), D_MODEL, 0.5),
        'b_mod': 0.01 * normal(ks[5], (DEPTH, N_MOD * D_MODEL)),
        'pre_norm': near_one(ks[6], (DEPTH, 3, D_MODEL)),
        'post_norm': near_one(ks[7], (DEPTH, 3, D_MODEL)),
        'ffn_w_in': dense(ks[8], (DEPTH, 2, D_MODEL, 2 * FFN_DIM), D_MODEL),
        'ffn_w_out': dense(ks[9], (DEPTH, 2, FFN_DIM, D_MODEL), FFN_DIM),
        'w_in': dense(ks[10], (DEPTH, D_MODEL, IN_COLS), D_MODEL),
        'diff_lambda': 0.1 * normal(ks[11], (DEPTH, 4, A_HEAD_DIM)),
        'diff_norm': near_one(ks[12], (DEPTH, A_VAL_DIM)),
        'na_rpb': 0.1 * normal(ks[13], (DEPTH, B_HEADS, 2 * WIN_R - 1, 2 * WIN_C - 1)),
        'ret_decay': decay_logit + 0.05 * normal(ks[14], (DEPTH, 2, C_HEADS)),
        'ret_norm': near_one(ks[15], (DEPTH, C_VAL_DIM)),
        'w_branch': dense(ks[16], (DEPTH, N_BRANCH, BRANCH_W, D_MODEL), BRANCH_W),
        'w_out': dense(ks[17], (DEPTH, D_MODEL, D_MODEL), D_MODEL),
    }


def reference(x, c, ctx, c_ctx, w_mod, b_mod, pre_norm, post_norm, ffn_w_in, ffn_w_out, w_in,
              diff_lambda, diff_norm, na_rpb, ret_decay, ret_norm, w_branch, w_out):
    b, _, d = x.shape
    h, hc = x, ctx
    for l in range(DEPTH):
        with_ctx_out = l < DEPTH - 1
        lambda_init = 0.8 - 0.6 * math.exp(-0.3 * l)
        mod = (jax.nn.silu(c) @ w_mod[l] + b_mod[l]).reshape(b, N_MOD, d).transpose(1, 0, 2)[:, :, None, :]
        modc = (jax.nn.silu(c_ctx) @ w_mod[l] + b_mod[l]).reshape(N_MOD, d)
        h = half_ffn_sublayer(h, pre_norm[l, 0], post_norm[l, 0], mod[0], mod[1], mod[2],
                              ffn_w_in[l, 0], ffn_w_out[l, 0])
        hc = half_ffn_sublayer(hc, pre_norm[l, 0], post_norm[l, 0], modc[0], modc[1], modc[2],
                               ffn_w_in[l, 0], ffn_w_out[l, 0])
        yc, yl = token_mixer(adaln_in(hc, pre_norm[l, 1], modc[3], modc[4]),
                             adaln_in(h, pre_norm[l, 1], mod[3], mod[4]),
                             w_in[l], diff_lambda[l], diff_norm[l], na_rpb[l], ret_decay[l], ret_norm[l],
                             w_branch[l], w_out[l], lambda_init, with_ctx_out)
        h = h + mod[5] * rmsnorm(yl, post_norm[l, 1])
        h = half_ffn_sublayer(h, pre_norm[l, 2], post_norm[l, 2], mod[6], mod[7], mod[8],
                              ffn_w_in[l, 1], ffn_w_out[l, 1])
        if with_ctx_out:
            hc = hc + modc[5] * rmsnorm(yc, post_norm[l, 1])
            hc = half_ffn_sublayer(hc, pre_norm[l, 2], post_norm[l, 2], modc[6], modc[7], modc[8],
                                   ffn_w_in[l, 1], ffn_w_out[l, 1])
    return h
```

```python
import math
import numpy as np
import concourse.bass as bass
import concourse.mybir as mybir
from concourse.bass_utils import run_bass_kernel_spmd

F32 = mybir.dt.float32
BF16 = mybir.dt.bfloat16
AF = mybir.ActivationFunctionType
ALU = mybir.AluOpType
AX = mybir.AxisListType

ENGS = ("pe", "act", "dve", "pool", "sp")
EPS = 1e-6
D = 2048
T = 2304
NT = 18
BIGZ = 1.0e6


class Op:
    __slots__ = ("eng", "fn", "deps", "dma", "signal", "sigval", "gidx", "dmaval")

    def __init__(self, eng, fn, dma):
        self.eng = eng
        self.fn = fn
        self.dma = dma
        self.deps = None
        self.signal = False
        self.sigval = 0
        self.dmaval = 0
        self.gidx = 0


class Sched:
    def __init__(self, nc, same_engine_sync=True):
        self.nc = nc
        self.ops = []
        self.last_w = {}
        self.readers = {}
        self.last_dma = {}
        self.dma_cnt = {}
        self.same_engine_sync = same_engine_sync
        self.last_op_eng = {e: None for e in ENGS}

    def add(self, eng, fn, reads=(), writes=(), dma=None):
        op = Op(eng, fn, dma)
        op.gidx = len(self.ops)
        deps = {}
        psr = [b for b in reads if isinstance(b, tuple) and b[0] == "ps"]
        if psr:
            writes = list(writes) + [b for b in psr if b not in writes]

        def adddep(d):
            if d is None or d is op:
                return
            if d.dma is not None:
                k = ("dma", d.dma)
                if k not in deps or deps[k].dmaval < d.dmaval:
                    deps[k] = d
            else:
                k = d.eng
                if k not in deps or deps[k].gidx < d.gidx:
                    deps[k] = d

        for b in reads:
            adddep(self.last_w.get(b))
        for b in writes:
            adddep(self.last_w.get(b))
            for r in self.readers.get(b, ()):
                adddep(r)
        for b in reads:
            self.readers.setdefault(b, []).append(op)
        for b in writes:
            self.last_w[b] = op
            self.readers[b] = []
        if dma is not None:
            adddep(self.last_dma.get(dma))
            self.last_dma[dma] = op
            self.dma_cnt[dma] = self.dma_cnt.get(dma, 0) + 16
            op.dmaval = self.dma_cnt[dma]
        op.deps = list(deps.values())
        self.ops.append(op)
        self.last_op_eng[eng] = op
        return op

    def barrier(self):
        lasts = [o for o in self.last_op_eng.values() if o is not None]
        dmas = list(self.last_dma.values())
        for e in ENGS:
            op = Op(e, None, None)
            op.gidx = len(self.ops)
            op.deps = list(lasts + dmas)
            self.ops.append(op)
        self.last_w = {}
        self.readers = {}

    def emit(self):
        nc = self.nc
        for op in self.ops:
            for d in op.deps:
                if d.dma is None:
                    if d.eng == op.eng and (d.eng == "pe" or not self.same_engine_sync):
                        continue
                    d.signal = True
        cnt = {e: 0 for e in ENGS}
        for op in self.ops:
            if op.signal:
                cnt[op.eng] += 1
                op.sigval = cnt[op.eng]
        per_eng = {e: [o for o in self.ops if o.eng == e] for e in ENGS}
        sems_eng = {e: nc.alloc_semaphore(name=f"se_{e}") for e in ENGS}
        dkeys = []
        for o in self.ops:
            if o.dma is not None and o.dma not in dkeys:
                dkeys.append(o.dma)
        sems_dma = {k: nc.alloc_semaphore(name=f"sd_{i}") for i, k in enumerate(dkeys)}
        same = self.same_engine_sync
        stats = {e: [0, 0] for e in ENGS}

        def run(eng_name, eng):
            waited = {}
            for op in per_eng[eng_name]:
                for d in op.deps:
                    if d.dma is not None:
                        sem = sems_dma[d.dma]
                        val = d.dmaval
                        key = ("dma", d.dma)
                    else:
                        if d.eng == eng_name and (eng_name == "pe" or not same):
                            continue
                        sem = sems_eng[d.eng]
                        val = d.sigval
                        key = d.eng
                    if waited.get(key, 0) >= val:
                        continue
                    waited[key] = val
                    eng.wait_ge(sem, val)
                    stats[eng_name][1] += 1
                if op.fn is None:
                    continue
                inst = op.fn(eng)
                stats[eng_name][0] += 1
                if op.dma is not None:
                    inst.then_inc(sems_dma[op.dma], 16)
                elif op.signal:
                    inst.then_inc(sems_eng[eng_name], 1)

        with nc.Block() as block:
            @block.tensor
            def _(e):
                run("pe", e)

            @block.scalar
            def _(e):
                run("act", e)

            @block.vector
            def _(e):
                run("dve", e)

            @block.gpsimd
            def _(e):
                run("pool", e)

            @block.sync
            def _(e):
                run("sp", e)
        return stats


class Arena:
    def __init__(self, tensor, nwords):
        self.t = tensor
        self.n = nwords
        self.off = 0
        self.marks = []

    def alloc(self, shape, dtype):
        nel = int(np.prod(shape))
        esz = 4 if dtype == F32 else 2
        words = (nel * esz + 3) // 4
        words = (words + 7) // 8 * 8
        assert self.off + words <= self.n, f"SBUF arena overflow {self.off}+{words}>{self.n}"
        ap = self.t[:, self.off:self.off + words]
        self.off += words
        if dtype != F32:
            ap = ap.bitcast(dtype)
        ap = ap[:, 0:nel]
        if len(shape) == 2:
            ap = ap.rearrange("p (a b) -> p a b", b=shape[1])
        elif len(shape) == 3:
            ap = ap.rearrange("p (a b c) -> p a b c", b=shape[1], c=shape[2])
        elif len(shape) == 4:
            ap = ap.rearrange("p (a b c d) -> p a b c d", b=shape[1], c=shape[2], d=shape[3])
        return ap

    def push(self):
        self.marks.append(self.off)

    def pop(self):
        self.off = self.marks.pop()


def lat_T(j):
    return j if j < 4 else j + 2


CTX_TILES = (4, 5)
QBLK = [(0, 512, 0), (768, 512, 4), (1280, 512, 8), (1792, 512, 12)]
CTX_COL0 = 512
PARTS = [[(b * 768, 512), (b * 768 + 512, 256)] for b in range(3)]


def part_is_ctx(b, p):
    return b == 0 and p == 1


def lat_col(c):
    return c if c < 512 else c - 256


def nb_jlist(i):
    s = set()
    for r in (2 * i, 2 * i + 1):
        rs = min(max(r - 4, 0), 24)
        for kr in range(rs, rs + 8):
            s.add(kr // 2)
    return sorted(s)


def nb_cls(i):
    return {0: 0, 1: 1, 14: 3, 15: 4}.get(i, 2)


class Prog:
    def __init__(self, nlayers=2, dbg=(), phases=None, same_engine_sync=True):
        self.nl = nlayers
        self.dbg = set(dbg)
        self.phases = phases
        nc = bass.Bass("TRN2", target_bir_lowering=False)
        self.nc = nc
        self.S = Sched(nc, same_engine_sync=same_engine_sync)
        self.in_names = []
        self.fast_rcp = False
        self.mod_overlap = True

    def din(self, name, shape, dt=F32):
        self.in_names.append(name)
        return self.nc.dram_tensor(name, list(shape), dt, kind="ExternalInput").ap()

    def dscr(self, name, shape, dt):
        kind = "ExternalOutput" if name in self.dbg else "Internal"
        return self.nc.dram_tensor(name, list(shape), dt, kind=kind).ap()

    def dma(self, eng, out, in_, reads, writes, sem):
        return self.S.add(eng, lambda e: e.dma_start(out=out, in_=in_), reads, writes, dma=sem)

    def mm(self, out, lhsT, rhs, start, stop, reads, writes):
        return self.S.add("pe", lambda e: e.matmul(out, lhsT, rhs, start=start, stop=stop), reads, writes)

    def act(self, out, in_, func, reads, writes, scale=None, bias=None):
        kw = {}
        if scale is not None:
            kw["scale"] = scale
        if bias is not None:
            kw["bias"] = bias
        return self.S.add("act", lambda e: e.activation(out=out, in_=in_, func=func, **kw), reads, writes)

    def tt(self, out, in0, in1, op, reads, writes, eng="dve"):
        return self.S.add(eng, lambda e: e.tensor_tensor(out=out, in0=in0, in1=in1, op=op), reads, writes)

    def stt(self, out, in0, scalar, in1, op0, op1, reads, writes):
        return self.S.add("dve", lambda e: e.scalar_tensor_tensor(out=out, in0=in0, scalar=scalar, in1=in1,
                                                                   op0=op0, op1=op1), reads, writes)

    def ts(self, out, in0, s1, op0, reads, writes, s2=None, op1=None, eng="dve"):
        if op1 is None:
            return self.S.add(eng, lambda e: e.tensor_scalar(out=out, in0=in0, scalar1=s1, scalar2=None, op0=op0),
                              reads, writes)
        return self.S.add(eng, lambda e: e.tensor_scalar(out=out, in0=in0, scalar1=s1, scalar2=s2, op0=op0, op1=op1),
                          reads, writes)

    def recip(self, out, in_, reads, writes):
        return self.S.add("dve", lambda e: e.reciprocal(out=out, in_=in_), reads, writes)

    def rcp(self, out, in_, scratch, reads, writes):
        if self.fast_rcp:
            return self.S.add("dve", lambda e: e.reciprocal_approx_accurate(out, in_, scratch), reads, writes)
        return self.S.add("dve", lambda e: e.reciprocal(out=out, in_=in_), reads, writes)

    def bank(self, b, n=512, c0=0):
        return self.ps[:, b * 512 + c0: b * 512 + c0 + n]

    def build(self):
        nc = self.nc
        NW = 52600
        self.declare_io()
        with nc.sbuf_tensor("arena", [128, NW], F32) as arena_t, \
             nc.psum_tensor("psum", [128, 8 * 512], F32) as ps:
            self.ps = ps
            self.ar = Arena(arena_t, NW)
            self.body()
            self.stats = self.S.emit()
        return nc

    def declare_io(self):
        L = self.nl
        self.xT = self.din("xT", [16, 128, T])
        self.cvec = self.din("cvec", [128, 16, 2])
        self.cosT = self.din("cosT", [128, 2048])
        self.sinT = self.din("sinT", [128, 2048])
        self.permM = self.din("permM", [128, 128])
        self.retz = self.din("retz", [128, 10, 512])
        self.rows128 = self.din("rows128", [128, 20])
        self.identM = self.din("identM", [128, 128])
        self.W = []
        for l in range(L):
            w = {}
            w["wmod"] = self.din(f"wmod{l}", [36, 128, 4, 16, 128])
            w["bmod"] = self.din(f"bmod{l}", [128, 144])
            w["pre"] = self.din(f"pre{l}", [128, 3, 16])
            w["post"] = self.din(f"post{l}", [128, 3, 16])
            w["fwin"] = [self.din(f"fwin{l}_{i}", [44, 128, 2, 16, 128]) for i in range(2)]
            w["fwout"] = [self.din(f"fwout{l}_{i}", [16, 128, 44, 128]) for i in range(2)]
            w["win"] = self.din(f"win{l}", [30, 128, 8192])
            w["wbr"] = self.din(f"wbr{l}", [16, 128, 3, 8, 128])
            w["wo"] = self.din(f"wo{l}", [16, 128, 16, 128])
            w["dlam"] = self.din(f"dlam{l}", [128, 256])
            w["dnorm"] = self.din(f"dnorm{l}", [128, 1])
            w["dnrow"] = self.din(f"dnrow{l}", [128, 128])
            w["rnorm"] = self.din(f"rnorm{l}", [128, 1])
            w["rdecay"] = self.din(f"rdecay{l}", [128, 16])
            w["rpbx"] = self.din(f"rpbx{l}", [8, 128, 25, 128])
            self.W.append(w)
        self.outT = self.nc.dram_tensor("outT", [16, 128, 2048], F32, kind="ExternalOutput").ap()
        self.hT = self.dscr("hT", [16, 128, T], F32)
        self.yscr = self.dscr("yscr", [2, 16, 128, 768], F32)
        for nm in ("qaT", "kaT", "qbT", "kbT", "cgT", "oaT", "obT", "yrT"):
            setattr(self, nm, self.dscr(nm, [8, 128, T], BF16))
        for nm in ("qcT", "kcT"):
            setattr(self, nm, self.dscr(nm, [4, 128, T], BF16))
        for nm in ("va", "vb", "vc"):
            setattr(self, nm, self.dscr(nm, [NT, 128, 1024], BF16))
        self.gT = self.dscr("gT", [48, 128, T], BF16)
        self.modout = self.dscr("modout", [128, 144, 2], F32) if "modout" in self.dbg else None

    def body(self):
        ar = self.ar
        S = self.S
        self.ones = ar.alloc([128], BF16)
        self.epsT = ar.alloc([1], F32)
        self.oneT = ar.alloc([1], F32)
        self.out_keys = []
        self.csil = ar.alloc([16, 2], F32)
        self.modT = ar.alloc([144, 2], F32)
        self.modN = ar.alloc([144, 2], F32)
        self.mod_ready = set()
        self.bmodS = ar.alloc([144], F32)
        self.preS = ar.alloc([3, 16], F32)
        self.postS = ar.alloc([3, 16], F32)
        self.Avec = ar.alloc([3, 16, 2], F32)
        self.Gvec = ar.alloc([3, 16, 2], F32)
        self.lamS = ar.alloc([8], F32)
        self.dnS = ar.alloc([2], F32)
        self.lgS = ar.alloc([16], F32)
        self.cft = ar.alloc([16, 20], F32)
        S.add("dve", lambda e: e.memset(self.ones, 1.0), writes=["ones"])
        S.add("dve", lambda e: e.memset(self.epsT, EPS), writes=["eps"])
        S.add("dve", lambda e: e.memset(self.oneT, 1.0), writes=["one"])
        self.dma("sp", self.csil, self.cvec, [], ["csil"], "ld0")
        self.csilb = ar.alloc([16, 2], BF16)
        self.act(self.csilb, self.csil, AF.Silu, ["csil"], ["csilb"])
        ph = self.phases
        h_in = self.xT
        for l in range(self.nl):
            last = (l == self.nl - 1) and not getattr(self, 'force_not_last', False)
            if ph is None or "mod" in ph:
                self.phase_mod(l)
                S.barrier()
            if ph is None or "ffn1" in ph:
                self.ffn_sublayer(l, 0, 0, h_in, self.hT, skip_ctx=False, final=False)
                S.barrier()
            h_in = self.hT
            if ph is None or "mix_in" in ph:
                self.mixer_inproj(l)
                S.barrier()
            if ph is None or "attA" in ph:
                self.attn_a(l, with_ctx=not last)
                S.barrier()
            if ph is None or "attB" in ph:
                self.attn_b(l, with_ctx=not last)
                S.barrier()
            if ph is None or "ret" in ph:
                self.retention(l, with_ctx=not last)
                S.barrier()
            if ph is None or "merge" in ph:
                self.merge(l, skip_ctx=last)
                S.barrier()
            if ph is None or "ffn2" in ph:
                self.ffn_sublayer(l, 1, 2, self.hT, self.outT if last else self.hT, skip_ctx=last, final=last)
                S.barrier()
        S.add("sp", None, reads=list(self.out_keys))

    def phase_mod(self, l):
        ar, S, W = self.ar, self.S, self.W[l]
        ar.push()
        B = 7
        if l not in self.mod_ready:
            wb = [ar.alloc([4, 16, 128], BF16) for _ in range(3)]
            for g in range(36):
                buf = wb[g % 3]
                self.dma("pool", buf, W["wmod"][g], [], [("wm", g % 3)], ("wm", g % 3))
                for mi in range(4):
                    mc = g * 4 + mi
                    for kc in range(16):
                        self.mm(self.ps[:, B * 512 + mc * 2: B * 512 + mc * 2 + 2], buf[:, mi, kc, :],
                                self.csilb[:, kc, :], kc == 0, kc == 15, [("wm", g % 3), "csilb"], [("ps", B)])
        self.dma("sp", self.bmodS, W["bmod"], [], ["bmod"], "ld0")
        self.dma("sp", self.preS, W["pre"], [], ["pre"], "ld1")
        self.dma("sp", self.postS, W["post"], [], ["post"], "ld2")
        if l in self.mod_ready:
            for j in range(2):
                self.tt(self.modT[:, :, j], self.modN[:, :, j], self.bmodS, ALU.add, ["modN", "bmod"], ["modT"])
        else:
            psv = self.bank(B, 288).rearrange("p (a b) -> p a b", b=2)
            for j in range(2):
                self.tt(self.modT[:, :, j], psv[:, :, j], self.bmodS, ALU.add, [("ps", B), "bmod"], ["modT"])
        if self.modout is not None:
            self.dma("sp", self.modout, self.modT, ["modT"], ["modout"], "ld0")
        for s in range(3):
            for j in range(2):
                sc = self.modT[:, (3 * s + 1) * 16:(3 * s + 2) * 16, j]
                gt = self.modT[:, (3 * s + 2) * 16:(3 * s + 3) * 16, j]
                self.stt(self.Avec[:, s, :, j], sc, 1.0, self.preS[:, s, :], ALU.add, ALU.mult,
                         ["modT", "pre"], ["Avec"])
                self.stt(self.Gvec[:, s, :, j], gt, 0.5 if s != 1 else 1.0, self.postS[:, s, :], ALU.mult, ALU.mult,
                         ["modT", "post"], ["Gvec"])
        lam_init = 0.8 - 0.6 * math.exp(-0.3 * l)
        dl = ar.alloc([256], F32)
        pr = ar.alloc([128], F32)
        sm = ar.alloc([2], F32)
        self.dma("sp", dl, W["dlam"], [], ["dl"], "ld0")
        dlv = dl.rearrange("p (a b) -> p a b", b=64)
        prv = pr.rearrange("p (a b) -> p a b", b=64)
        self.tt(prv[:, 0, :], dlv[:, 0, :], dlv[:, 1, :], ALU.mult, ["dl"], ["pr"])
        self.tt(prv[:, 1, :], dlv[:, 2, :], dlv[:, 3, :], ALU.mult, ["dl"], ["pr"])
        S.add("dve", lambda e: e.reduce_sum(out=sm, in_=prv, axis=AX.X), ["pr"], ["sm"])
        self.act(sm, sm, AF.Exp, ["sm"], ["sm"])
        self.stt(self.lamS[:, 0:1], sm[:, 1:2], -lam_init, sm[:, 0:1], ALU.add, ALU.subtract, ["sm"], ["lamS"])
        dn = ar.alloc([2], F32)
        self.dma("sp", dn[:, 0:1], W["dnorm"], [], ["dn"], "ld1")
        self.dma("sp", dn[:, 1:2], W["rnorm"], [], ["dn"], "ld2")
        self.ts(self.dnS[:, 0:1], dn[:, 0:1], 1.0 - lam_init, ALU.mult, ["dn"], ["dnS"])
        self.ts(self.dnS[:, 1:2], dn[:, 1:2], 1.0, ALU.mult, ["dn"], ["dnS"])
        rd = ar.alloc([16], F32)
        self.dma("sp", rd, W["rdecay"], [], ["rd"], "ld0")
        self.act(rd, rd, AF.Exp, ["rd"], ["rd"], scale=-1.0)
        self.act(rd, rd, AF.Ln, ["rd"], ["rd", "one"][:1], bias=self.oneT[:, 0:1])
        self.ts(self.lgS, rd, -1.0, ALU.mult, ["rd"], ["lgS"])
        r128 = ar.alloc([20], F32)
        self.dma("sp", r128, self.rows128, [], ["r128"], "ld1")
        for k in range(16):
            self.act(self.cft[:, k, :], r128, AF.Exp, ["r128", "lgS"], ["cft"], scale=self.lgS[:, k:k + 1])
        ar.pop()

    def mod_overlap_gen(self, ln, wbufs):
        W = self.W[ln]
        modNf = self.modN.rearrange("p a b -> p (a b)")
        for g in range(36 + 2):
            if g < 36:
                self.dma("pool", wbufs[g % 3], W["wmod"][g], [], [("wmo", g % 3)], ("wm", g % 3))
            if g >= 2:
                g2 = g - 2
                buf = wbufs[g2 % 3]
                c0 = 7 * 512 + 384 + (g2 % 8) * 8
                for mi in range(4):
                    for kc in range(16):
                        self.mm(self.ps[:, c0 + mi * 2: c0 + mi * 2 + 2], buf[:, mi, kc, :], self.csilb[:, kc, :],
                                kc == 0, kc == 15, [("wmo", g2 % 3), "csilb"], [("ps", 7)])
                self.S.add("dve", (lambda e, o=modNf[:, g2 * 8:g2 * 8 + 8], i=self.ps[:, c0:c0 + 8]:
                                   e.tensor_copy(out=o, in_=i)), [("ps", 7)], ["modN"])
            yield
        self.mod_ready.add(ln)

    def prep_gen(self, l, sidx, h_src, parts, uT, ucols, statbank, bufs):
        hb, sqb, tmpf, rst = bufs
        for (c0, n, j), uc0 in zip(parts, ucols):
            for ch in range(18):
                if ch < 16:
                    k = ch % 4
                    self.dma("sp", hb[k][:, :n], h_src[ch][:, c0:c0 + n], [("h", ch, c0)], [("hb", k)], ("hb", k))
                    self.act(sqb[k][:, :n], hb[k][:, :n], AF.Square, [("hb", k)], [("sqb", k)])
                if ch >= 2:
                    c2 = ch - 2
                    self.mm(self.bank(statbank, n), self.ones, sqb[c2 % 4][:, :n], c2 == 0, c2 == 15,
                            [("sqb", c2 % 4), "ones"], [("ps", statbank)])
                yield
            self.act(rst[:, :n], self.bank(statbank, n), AF.Sqrt, [("ps", statbank), "eps"], ["rst"],
                     scale=1.0 / D, bias=self.epsT[:, 0:1])
            self.recip(rst[:, :n], rst[:, :n], ["rst"], ["rst"])
            for ch in range(16):
                k = ch % 4
                self.dma("sp", hb[k][:, :n], h_src[ch][:, c0:c0 + n], [("h", ch, c0)], [("hb", k)], ("hb", k))
                self.tt(tmpf[ch % 2][:, :n], hb[k][:, :n], rst[:, :n], ALU.mult, [("hb", k), "rst"], [("tmpf", ch % 2)])
                self.act(uT[:, ch, uc0:uc0 + n], tmpf[ch % 2][:, :n], AF.Identity,
                         [("tmpf", ch % 2), "Avec", "modT"], [("uT", ch)],
                         scale=self.Avec[:, sidx, ch, j:j + 1], bias=self.modT[:, 3 * sidx * 16 + ch, j:j + 1])
                yield

    def alloc_prep_bufs(self):
        ar = self.ar
        hb = [ar.alloc([512], F32) for _ in range(4)]
        sqb = [ar.alloc([512], BF16) for _ in range(4)]
        tmpf = [ar.alloc([512], F32) for _ in range(2)]
        rst = ar.alloc([512], F32)
        return hb, sqb, tmpf, rst

    def alloc_post_bufs(self):
        ar = self.ar
        d = {}
        d["ysb"] = [ar.alloc([768], F32) for _ in range(2)]
        d["ysq"] = [ar.alloc([768], BF16) for _ in range(2)]
        d["yl"] = [ar.alloc([768], F32) for _ in range(2)]
        d["hl"] = [ar.alloc([768], F32) for _ in range(2)]
        d["ho"] = [ar.alloc([768], F32) for _ in range(2)]
        d["rstp"] = ar.alloc([768], F32)
        return d

    def post_evac(self, pb, blk, oc, parts, ybanks, statbanks, slot):
        ysb, ysq = pb["ysb"][oc % 2], pb["ysq"][oc % 2]
        off = 0
        for pi, (c0, n, j, on) in enumerate(parts):
            if on:
                yb = self.bank(ybanks[pi], n)
                self.act(ysq[:, off:off + n], yb, AF.Square, [("ps", ybanks[pi])], [("ysq", oc % 2, pi)])
                self.mm(self.bank(statbanks[pi], n), self.ones, ysq[:, off:off + n], oc == 0, oc == 15,
                        [("ysq", oc % 2, pi), "ones"], [("ps", statbanks[pi])])
                self.S.add("dve", (lambda e, o=ysb[:, off:off + n], i=yb: e.tensor_copy(out=o, in_=i)),
                           [("ps", ybanks[pi])], [("ysb", oc % 2, pi)])
            off += n
        ntot = off
        act_parts = [pi for pi, p in enumerate(parts) if p[3]]
        self.dma("sp", self.yscr[slot][oc][:, :ntot], ysb[:, :ntot], [("ysb", oc % 2, pi) for pi in act_parts],
                 [("yscr", slot, oc)], ("ysb", oc % 2))

    def post_gen(self, pb, l, sidx, blk, parts, statbanks, slot, h_in, h_out, final):
        rstp = pb["rstp"]
        off = 0
        offs = []
        for pi, (c0, n, j, on) in enumerate(parts):
            offs.append(off)
            if on:
                self.act(rstp[:, off:off + n], self.bank(statbanks[pi], n), AF.Sqrt, [("ps", statbanks[pi]), "eps"],
                         [("rstp", pi)], scale=1.0 / D, bias=self.epsT[:, 0:1])
                self.recip(rstp[:, off:off + n], rstp[:, off:off + n], [("rstp", pi)], [("rstp", pi)])
            off += n
        ntot = off
        for oc in range(16):
            k = oc % 2
            yl, hl, ho = pb["yl"][k], pb["hl"][k], pb["ho"][k]
            self.dma("sp", yl[:, :ntot], self.yscr[slot][oc][:, :ntot], [("yscr", slot, oc)], [("yl", k)], ("yl", k))
            for pi, (c0, n, j, on) in enumerate(parts):
                if not on:
                    continue
                o = offs[pi]
                self.dma("sp", hl[:, o:o + n], h_in[oc][:, c0:c0 + n], [("h", oc, c0)], [("hl", k, pi)], ("hl", k, pi))
                self.tt(yl[:, o:o + n], yl[:, o:o + n], rstp[:, o:o + n], ALU.mult, [("yl", k), ("rstp", pi)],
                        [("yl", k)])
                self.stt(ho[:, o:o + n], yl[:, o:o + n], self.Gvec[:, sidx, oc, j:j + 1], hl[:, o:o + n],
                         ALU.mult, ALU.add, [("yl", k), ("hl", k, pi), "Gvec"], [("ho", k, pi)])
                if final:
                    lc = lat_col(c0)
                    self.out_keys.append(("OUT", oc, c0))
                    self.dma("sp", h_out[oc][:, lc:lc + n], ho[:, o:o + n], [("ho", k, pi)], [("OUT", oc, c0)],
                             ("ho", k, pi))
                else:
                    self.dma("sp", h_out[oc][:, c0:c0 + n], ho[:, o:o + n], [("ho", k, pi)], [("h", oc, c0)],
                             ("ho", k, pi))
            yield

    def post_pass(self, *a):
        for _ in self.post_gen(*a):
            pass

    def ffn_sublayer(self, l, i, sidx, h_in, h_out, skip_ctx, final):
        ar, S, W = self.ar, self.S, self.W[l]
        ar.push()
        uT = ar.alloc([16, 768], BF16)
        gT = ar.alloc([44, 768], BF16)
        wbuf = [ar.alloc([2, 16, 128], BF16) for _ in range(3)]
        wobuf = [ar.alloc([44, 128], BF16) for _ in range(2)]
        sa = [ar.alloc([512], BF16) for _ in range(2)]
        pbufs = self.alloc_prep_bufs()
        pb = self.alloc_post_bufs()

        def parts_of(b):
            return [(c0, n, 1 if part_is_ctx(b, p) else 0, not (skip_ctx and part_is_ctx(b, p)))
                    for p, (c0, n) in enumerate(PARTS[b])]

        def prep_for(b, statbank):
            ps_ = [(c0, n, j) for (c0, n, j, on) in parts_of(b) if on]
            uc = [0 if idx == 0 else 512 for idx, (c0, n, j, on) in enumerate(parts_of(b)) if on]
            return self.prep_gen(l, sidx, h_in, ps_, uT, uc, statbank, pbufs)

        for _ in prep_for(0, 7):
            pass
        wcount = 0
        postg = iter(())
        for b in range(3):
            parts = parts_of(b)
            slot = b % 2
            for fc in range(44):
                next(postg, None)
                wb = wbuf[wcount % 3]
                wk = ("wbuf", wcount % 3)
                self.dma("pool", wb, W["fwin"][i][fc], [], [wk], wk)
                wcount += 1
                st = fc % 2
                banks = (st * 3, st * 3 + 1, st * 3 + 2)
                for half in range(2):
                    for kc in range(16):
                        self.mm(self.bank(banks[half]), wb[:, half, kc, :], uT[:, kc, 0:512], kc == 0, kc == 15,
                                [wk, ("uT", kc)], [("ps", banks[half])])
                if parts[1][3]:
                    for half in range(2):
                        for kc in range(16):
                            self.mm(self.bank(banks[2], 256, half * 256), wb[:, half, kc, :], uT[:, kc, 512:768],
                                    kc == 0, kc == 15, [wk, ("uT", kc)], [("ps", banks[2])])
                self.act(sa[0], self.bank(banks[0]), AF.Silu, [("ps", banks[0])], [("sa", 0)])
                self.tt(gT[:, fc, 0:512], sa[0], self.bank(banks[1]), ALU.mult, [("sa", 0), ("ps", banks[1])],
                        [("gT", fc)])
                if parts[1][3]:
                    self.act(sa[1][:, :256], self.bank(banks[2], 256, 0), AF.Silu, [("ps", banks[2])], [("sa", 1)])
                    self.tt(gT[:, fc, 512:768], sa[1][:, :256], self.bank(banks[2], 256, 256), ALU.mult,
                            [("sa", 1), ("ps", banks[2])], [("gT", fc)])
            pg = prep_for(b + 1, 5) if b < 2 else iter(())
            for oc in range(16):
                for _ in range(5):
                    next(pg, None)
                wo = wobuf[oc % 2]
                wok = ("wobuf", oc % 2)
                self.dma("pool", wo, W["fwout"][i][oc], [], [wok], wok)
                st = oc % 2
                ybanks = (st * 2, st * 2 + 1)
                for pi, (c0, n, j, on) in enumerate(parts):
                    if not on:
                        continue
                    u0 = 0 if pi == 0 else 512
                    for fc in range(44):
                        self.mm(self.bank(ybanks[pi], n), wo[:, fc, :], gT[:, fc, u0:u0 + n], fc == 0, fc == 43,
                                [wok, ("gT", fc)], [("ps", ybanks[pi])])
                self.post_evac(pb, b, oc, parts, ybanks, (6, 7), slot)
            for _ in pg:
                pass
            for _ in postg:
                pass
            postg = self.post_gen(pb, l, sidx, b, parts, (6, 7), slot, h_in, h_out, final)
        for _ in postg:
            pass
        ar.pop()

    def mixer_inproj(self, l):
        ar, S, W = self.ar, self.S, self.W[l]
        ar.push()
        uT = ar.alloc([16, T], BF16)
        wbuf = [ar.alloc([8192], BF16) for _ in range(2)]
        cbuf = [ar.alloc([T], BF16) for _ in range(4)]
        vbuf = [ar.alloc([512], BF16) for _ in range(3)]
        cosS = ar.alloc([2048], F32)
        sinS = ar.alloc([2048], F32)
        pm = ar.alloc([128], F32)
        tsb = [ar.alloc([512], F32) for _ in range(2)]
        ra = [ar.alloc([512], F32) for _ in range(2)]
        rb = [ar.alloc([512], F32) for _ in range(2)]
        pbufs = self.alloc_prep_bufs()
        self.dma("sp", cosS, self.cosT, [], ["cos"], "ld0")
        self.dma("sp", sinS, self.sinT, [], ["sin"], "ld1")
        self.dma("sp", pm, self.permM, [], ["pm"], "ld2")
        allparts = [(0, 512, 0), (512, 256, 1), (768, 512, 0), (1280, 512, 0), (1792, 512, 0)]
        for _ in self.prep_gen(l, 1, self.hT, allparts, uT, [p[0] for p in allparts], 7, pbufs):
            pass
        fparts = [(0, 512, 0), (512, 256, None), (768, 512, 4), (1280, 512, 8), (1792, 512, 12)]
        ccount = 0
        ecount = 0
        vcount = 0
        rcount = 0
        for g in range(30):
            wb = wbuf[g % 2]
            wk = ("wbuf", g % 2)
            self.dma("pool", wb, W["win"][g], [], [wk], wk)
            if g in (4, 5, 10, 11, 14, 15):
                dst = {4: self.va, 5: self.va, 10: self.vb, 11: self.vb, 14: self.vc, 15: self.vc}[g]
                hoff = (g % 2) * 512
                wv = wb.rearrange("p (k c) -> p k c", c=512)
                for tt_ in range(NT):
                    bk = ecount % 4
                    ecount += 1
                    for kc in range(16):
                        self.mm(self.bank(bk), uT[:, kc, tt_ * 128:(tt_ + 1) * 128], wv[:, kc, :], kc == 0, kc == 15,
                                [wk, ("uT", kc)], [("ps", bk)])
                    vb_ = vbuf[vcount % 3]
                    vk = ("vbuf", vcount % 3)
                    vcount += 1
                    self.act(vb_, self.bank(bk), AF.Copy, [("ps", bk)], [vk])
                    self.dma("sp", dst[tt_][:, hoff:hoff + 512], vb_, [vk], [("vdst", g, tt_)], vk)
                continue
            wf = wb.rearrange("p (c k f) -> p c k f", k=16, f=128)
            for ci in range(4):
                cc = g * 4 + ci
                if g < 2:
                    dst, mode, sc = self.qaT[cc], "rope", 1.0
                elif g < 4:
                    dst, mode, sc = self.kaT[cc - 8], "rope", 1.0
                elif g < 8:
                    dst, mode, sc = self.qbT[cc - 24], "copy", 1.0
                elif g < 10:
                    dst, mode, sc = self.kbT[cc - 32], "copy", 1.0
                elif g == 12:
                    dst, mode, sc = self.qcT[cc - 48], "rope", 1.0
                elif g == 13:
                    dst, mode, sc = self.kcT[cc - 52], "rope", 0.125
                elif g < 18:
                    dst, mode, sc = self.cgT[cc - 64], "silu", 1.0
                else:
                    dst, mode, sc = self.gT[cc - 72], "sigmoid", 1.0
                cb = cbuf[ccount % 4]
                ck = ("cbuf", ccount % 4)
                ccount += 1
                for (c0, n, lt0) in fparts:
                    bk = ecount % 4
                    ecount += 1
                    for kc in range(16):
                        self.mm(self.bank(bk, n), wf[:, ci, kc, :], uT[:, kc, c0:c0 + n], kc == 0, kc == 15,
                                [wk, ("uT", kc)], [("ps", bk)])
                    if mode == "rope" and lt0 is not None:
                        r = rcount % 2
                        rcount += 1
                        rbk = 4 + r
                        t0 = lt0 * 128
                        self.act(tsb[r], self.bank(bk), AF.Copy, [("ps", bk)], [("tsb", r)], scale=sc)
                        self.mm(self.bank(rbk), pm, tsb[r], True, True, [("tsb", r), "pm"], [("ps", rbk)])
                        self.tt(ra[r], tsb[r], cosS[:, t0:t0 + 512], ALU.mult, [("tsb", r), "cos"], [("ra", r)],
                                eng="pool")
                        self.tt(rb[r], self.bank(rbk), sinS[:, t0:t0 + 512], ALU.mult, [("ps", rbk), "sin"],
                                [("rb", r)])
                        self.tt(cb[:, c0:c0 + 512], ra[r], rb[r], ALU.add, [("ra", r), ("rb", r)], [ck])
                    else:
                        fn = {"rope": AF.Copy, "copy": AF.Copy, "silu": AF.Silu, "sigmoid": AF.Sigmoid}[mode]
                        self.act(cb[:, c0:c0 + n], self.bank(bk, n), fn, [("ps", bk)], [ck], scale=sc)
                self.dma("sp", dst, cb, [ck], [("fdst", cc)], ck)
        ar.pop()

    def attn_a(self, l, with_ctx):
        ar, S, W = self.ar, self.S, self.W[l]
        ar.push()
        q1 = [ar.alloc([T], BF16) for _ in range(2)]
        q2 = [ar.alloc([T], BF16) for _ in range(2)]
        kS = [ar.alloc([T], BF16) for _ in range(2)]
        vS = [ar.alloc([NT, 129], BF16) for _ in range(2)]
        E1 = [ar.alloc([512], BF16) for _ in range(3)]
        E2 = [ar.alloc([512], BF16) for _ in range(3)]
        cacc = [ar.alloc([3, 512], F32) for _ in range(2)]
        ident = ar.alloc([128], BF16)
        dnrow = ar.alloc([128], F32)
        sm = [ar.alloc([8], F32) for _ in range(2)]
        t1 = [ar.alloc([128], F32) for _ in range(2)]
        o_ = [ar.alloc([128], F32) for _ in range(2)]
        junkf = [ar.alloc([128], F32) for _ in range(2)]
        onb = [ar.alloc([128], BF16) for _ in range(2)]
        obuf = [ar.alloc([T], BF16) for _ in range(2)]
        self.dma("pool", ident, self.identM, [], ["ident"], "ld0")
        self.dma("sp", dnrow, W["dnrow"], [], ["dnrow"], "ld1")
        self.ts(dnrow, dnrow, 1.0 - (0.8 - 0.6 * math.exp(-0.3 * l)), ALU.mult, ["dnrow"], ["dnrow"])
        for i in range(2):
            S.add("dve", (lambda e, a=q1[i][64:128, :]: e.memset(a, 0.0)), writes=[("q1z", i)])
            S.add("dve", (lambda e, a=q2[i][0:64, :]: e.memset(a, 0.0)), writes=[("q2z", i)])
            S.add("dve", (lambda e, a=vS[i][:, :, 128:129]: e.memset(a, 1.0)), writes=[("vSo", i)])
        qblocks = [(c0, n, list(range(NT))) for (c0, n, _) in QBLK]
        if with_ctx:
            qblocks.append((CTX_COL0, 256, list(CTX_TILES)))
        slots1 = [(4, 0), (4, 129), (4, 258), (5, 0)]
        slots2 = [(5, 129), (5, 258), (6, 0), (6, 129)]
        scnt = 0
        ecnt = 0
        fcnt = 0
        tcnt = 0
        ps7b = self.bank(7).bitcast(BF16)
        tcnt_box = [0]

        def fin_gen(cb, c0, nq, ob, hb):
            ca = cacc[cb]
            ck = [("cacc", cb, 0), ("cacc", cb, 1), ("cacc", cb, 2)]
            for qt in range(nq):
                tb = tcnt_box[0] % 2
                tcnt_box[0] += 1
                b1, c1 = slots1[qt]
                b2, c2 = slots2[qt]
                A1 = ca[:, b1 - 4, c1:c1 + 129]
                A2 = ca[:, b2 - 4, c2:c2 + 129]
                smt = sm[tb]
                smk = ("sm", tb)
                self.recip(smt[:, 0:1], A1[:, 128:129], ck, [smk])
                self.recip(smt[:, 1:2], A2[:, 128:129], ck, [smk])
                self.tt(smt[:, 2:3], smt[:, 1:2], self.lamS[:, 0:1], ALU.mult, [smk, "lamS"], [smk])
                yield
                self.ts(t1[tb], A1[:, 0:128], smt[:, 0:1], ALU.mult, ck + [smk], [("t1", tb)])
                self.stt(o_[tb], A2[:, 0:128], smt[:, 2:3], t1[tb], ALU.mult, ALU.add, ck + [smk, ("t1", tb)],
                         [("o_", tb)])
                yield
                self.S.add("dve", (lambda e, o=junkf[tb], i=o_[tb], a=smt[:, 3:4]:
                                   e.scalar_tensor_tensor(out=o, in0=i, scalar=1.0, in1=i, op0=ALU.mult, op1=ALU.mult,
                                                          accum_out=a)),
                           [("o_", tb)], [("junk", tb), smk])
                yield
                self.act(smt[:, 4:5], smt[:, 3:4], AF.Ln, [smk, "eps"], [smk], scale=1.0 / 128,
                         bias=self.epsT[:, 0:1])
                self.act(smt[:, 5:6], smt[:, 4:5], AF.Exp, [smk], [smk], scale=-0.5)
                self.stt(onb[tb], o_[tb], smt[:, 5:6], dnrow, ALU.mult, ALU.mult, [("o_", tb), smk, "dnrow"],
                         [("onb", tb)])
                yield
                tv = ps7b[:, tb * 128:(tb + 1) * 128]
                self.S.add("pe", (lambda e, o=tv, i=onb[tb]: e.transpose(o, i, ident)), [("onb", tb), "ident"],
                           [("ps", 7)])
                yield
                qc = c0 + qt * 128
                self.S.add("dve", (lambda e, o=ob[:, qc:qc + 128], i=tv: e.tensor_copy(out=o, in_=i)),
                           [("ps", 7)], [("obuf", hb)])
                yield

        fin = iter(())
        modg = iter(())
        if l + 1 < self.nl and self.mod_overlap:
            mwb = [ar.alloc([4, 16, 128], BF16) for _ in range(3)]
            modg = self.mod_overlap_gen(l + 1, mwb)
        stepc = 0
        for h in range(8):
            hb = h % 2
            self.dma("sp", q1[hb][0:64, :], self.qaT[h][0:64, :], [], [("q1", hb)], ("q1", hb))
            self.dma("sp", q2[hb][64:128, :], self.qaT[h][64:128, :], [], [("q2", hb)], ("q2", hb))
            self.dma("sp", kS[hb], self.kaT[h], [], [("kS", hb)], ("kS", hb))
            self.dma("sp", vS[hb][:, :, 0:128], self.va[:, :, h * 128:(h + 1) * 128].rearrange("t p d -> p t d"),
                     [], [("vS", hb)], ("vS", hb))
            ob = obuf[hb]
            for (c0, n, ktl) in qblocks:
                nq = n // 128
                nk = len(ktl)
                pend = None
                first_in_bank = {}
                for idx in range(nk + 1):
                    cur = None
                    if idx < nk:
                        kt = ktl[idx]
                        sb = scnt % 2
                        scnt += 1
                        self.mm(self.bank(sb, n), kS[hb][:, kt * 128:(kt + 1) * 128], q1[hb][:, c0:c0 + n],
                                True, True, [("kS", hb), ("q1", hb), ("q1z", hb)], [("ps", sb)])
                        self.mm(self.bank(2 + sb, n), kS[hb][:, kt * 128:(kt + 1) * 128], q2[hb][:, c0:c0 + n],
                                True, True, [("kS", hb), ("q2", hb), ("q2z", hb)], [("ps", 2 + sb)])
                        eb = ecnt % 3
                        ecnt += 1
                        self.act(E1[eb][:, :n], self.bank(sb, n), AF.Exp, [("ps", sb)], [("E1", eb)], scale=0.125)
                        self.act(E2[eb][:, :n], self.bank(2 + sb, n), AF.Exp, [("ps", 2 + sb)], [("E2", eb)], scale=0.125)
                        cur = (kt, eb, idx)
                    if pend is not None:
                        kt_, eb_, i_ = pend
                        for (Eb, slots, ek) in ((E1[eb_], slots1, ("E1", eb_)), (E2[eb_], slots2, ("E2", eb_))):
                            for qt in range(nq):
                                bk, co = slots[qt]
                                st = (i_ == 0) and (bk not in first_in_bank)
                                if i_ == 0:
                                    first_in_bank[bk] = True
                                self.S.add("pe", (lambda e, o=self.bank(bk, 129, co), lt=Eb[:, qt * 128:(qt + 1) * 128],
                                                  r=vS[hb][:, kt_, :], st=st, sp=(i_ == nk - 1):
                                                  e.matmul(o, lt, r, start=st, stop=sp, skip_group_check=True)),
                                           [("vS", hb), ("vSo", hb), ek], [("ps", bk)])
                    pend = cur
                    next(fin, None)
                    next(fin, None)
                    stepc += 1
                    if stepc % 14 == 0:
                        next(modg, None)
                for _ in fin:
                    pass
                cb = fcnt % 2
                fcnt += 1
                ca = cacc[cb]
                for bi, bk in enumerate((4, 5, 6)):
                    if bi == 2 and nq < 3:
                        continue
                    self.S.add("dve", (lambda e, o=ca[:, bi, :], i=self.bank(bk): e.tensor_copy(out=o, in_=i)),
                               [("ps", bk)], [("cacc", cb, bi)])
                fin = fin_gen(cb, c0, nq, ob, hb)
            for _ in fin:
                pass
            fin = iter(())
            if with_ctx:
                self.dma("sp", self.oaT[h], ob, [("obuf", hb)], [("oaT", h)], ("obuf", hb))
            else:
                self.dma("sp", self.oaT[h][:, 0:512], ob[:, 0:512], [("obuf", hb)], [("oaT", h)], ("obuf", hb))
                self.dma("sp", self.oaT[h][:, 768:T], ob[:, 768:T], [("obuf", hb)], [("oaT", h)], ("obuf", hb))
        for _ in modg:
            pass
        ar.pop()

    def attn_b(self, l, with_ctx):
        ar, S, W = self.ar, self.S, self.W[l]
        ar.push()
        qS = [ar.alloc([T], BF16) for _ in range(2)]
        kS = [ar.alloc([T], BF16) for _ in range(2)]
        vS = [ar.alloc([NT, 128], BF16) for _ in range(2)]
        bias = [ar.alloc([25, 128], F32) for _ in range(2)]
        pre = [ar.alloc([5, 128], F32) for _ in range(2)]
        E = [ar.alloc([7, 128], BF16) for _ in range(2)]
        rr = [ar.alloc([256], F32) for _ in range(2)]
        rsc = [ar.alloc([256], F32) for _ in range(2)]
        obuf = [ar.alloc([T], BF16) for _ in range(2)]
        scale = 128 ** -0.5
        steps = []

        def loads(h):
            hb = h % 2
            self.dma("sp", qS[hb], self.qbT[h], [], [("qS", hb)], ("qS", hb))
            self.dma("sp", kS[hb], self.kbT[h], [], [("kS", hb)], ("kS", hb))
            self.dma("sp", vS[hb], self.vb[:, :, h * 128:(h + 1) * 128].rearrange("t p d -> p t d"),
                     [], [("vS", hb)], ("vS", hb))
            self.dma("sp", bias[hb], W["rpbx"][h], [], [("bias", hb)], ("bias", hb))

        def front(h, i, s2):
            hb = h % 2
            if i == 0:
                loads(h)
            qc0 = lat_T(i) * 128
            jl = nb_jlist(i)
            cls = nb_cls(i)
            nl_ = len(jl)
            tiles = [lat_T(j) for j in jl] + list(CTX_TILES)
            for sl, kt in enumerate(tiles):
                bk = s2 * 2 + (0 if sl < 4 else 1)
                cc = (sl if sl < 4 else sl - 4) * 128
                self.mm(self.bank(bk, 128, cc), kS[hb][:, kt * 128:(kt + 1) * 128], qS[hb][:, qc0:qc0 + 128],
                        True, True, [("kS", hb), ("qS", hb)], [("ps", bk)])
            b0 = self.bank(s2 * 2, 512).rearrange("p (a b) -> p a b", b=128)
            b1 = self.bank(s2 * 2 + 1, 512).rearrange("p (a b) -> p a b", b=128)
            n0 = min(nl_, 4)
            self.stt(pre[s2][:, 0:n0, :], b0[:, 0:n0, :], scale, bias[hb][:, cls * 5:cls * 5 + n0, :], ALU.mult, ALU.add,
                     [("ps", s2 * 2), ("bias", hb)], [("pre", s2)])
            if nl_ > 4:
                self.stt(pre[s2][:, 4:5, :], b1[:, 0:1, :], scale, bias[hb][:, cls * 5 + 4:cls * 5 + 5, :], ALU.mult,
                         ALU.add, [("ps", s2 * 2 + 1), ("bias", hb)], [("pre", s2)])
            self.act(E[s2][:, 0:nl_, :], pre[s2][:, 0:nl_, :], AF.Exp, [("pre", s2)], [("E", s2)])
            if nl_ <= 4:
                for ci in range(2):
                    sl = nl_ + ci
                    src = b0[:, sl:sl + 1, :] if sl < 4 else b1[:, sl - 4:sl - 3, :]
                    bkk = s2 * 2 + (0 if sl < 4 else 1)
                    self.act(E[s2][:, sl:sl + 1, :], src, AF.Exp, [("ps", bkk)], [("E", s2)], scale=scale)
            else:
                self.act(E[s2][:, 5:7, :], b1[:, 1:3, :], AF.Exp, [("ps", s2 * 2 + 1)], [("E", s2)], scale=scale)

        def back(h, i, s2):
            hb = h % 2
            ob = obuf[hb]
            qc0 = lat_T(i) * 128
            tiles = [lat_T(j) for j in nb_jlist(i)] + list(CTX_TILES)
            ns = len(tiles)
            ob_ = 4 + s2
            db_ = 6 + s2
            for sl, kt in enumerate(tiles):
                self.mm(self.bank(ob_, 128), vS[hb][:, kt, :], E[s2][:, sl, :], sl == 0, sl == ns - 1,
                        [("vS", hb), ("E", s2)], [("ps", ob_)])
            for sl, kt in enumerate(tiles):
                self.mm(self.bank(db_, 128), self.ones, E[s2][:, sl, :], sl == 0, sl == ns - 1,
                        ["ones", ("E", s2)], [("ps", db_)])
            self.rcp(rr[s2][:, :128], self.bank(db_, 128), rsc[s2][:, :128], [("ps", db_)], [("rr", s2)])
            self.tt(ob[:, qc0:qc0 + 128], self.bank(ob_, 128), rr[s2][:, :128], ALU.mult, [("ps", ob_), ("rr", s2)],
                    [("obuf", hb)])
            if i == 15 and not with_ctx:
                self.dma("sp", self.obT[h][:, 0:512], ob[:, 0:512], [("obuf", hb)], [("obT", h)], ("obuf", hb))
                self.dma("sp", self.obT[h][:, 768:T], ob[:, 768:T], [("obuf", hb)], [("obT", h)], ("obuf", hb))

        def front_c(h, s2):
            hb = h % 2
            n = 256
            for ci, kt in enumerate(CTX_TILES):
                self.mm(self.bank(s2 * 2 + ci, n), kS[hb][:, kt * 128:(kt + 1) * 128], qS[hb][:, CTX_COL0:CTX_COL0 + n],
                        True, True, [("kS", hb), ("qS", hb)], [("ps", s2 * 2 + ci)])
            Ev = E[s2].rearrange("p a b -> p (a b)")
            for ci in range(2):
                self.act(Ev[:, ci * 256:(ci + 1) * 256], self.bank(s2 * 2 + ci, n), AF.Exp, [("ps", s2 * 2 + ci)],
                         [("E", s2)], scale=scale)

        def back_c(h, s2):
            hb = h % 2
            ob = obuf[hb]
            n = 256
            Ev = E[s2].rearrange("p a b -> p (a b)")
            ob_ = 4 + s2
            db_ = 6 + s2
            for ci, kt in enumerate(CTX_TILES):
                self.mm(self.bank(ob_, n), vS[hb][:, kt, :], Ev[:, ci * 256:(ci + 1) * 256], ci == 0, ci == 1,
                        [("vS", hb), ("E", s2)], [("ps", ob_)])
            for ci, kt in enumerate(CTX_TILES):
                self.mm(self.bank(db_, n), self.ones, Ev[:, ci * 256:(ci + 1) * 256], ci == 0, ci == 1,
                        ["ones", ("E", s2)], [("ps", db_)])
            self.rcp(rr[s2][:, :n], self.bank(db_, n), rsc[s2][:, :n], [("ps", db_)], [("rr", s2)])
            self.tt(ob[:, CTX_COL0:CTX_COL0 + n], self.bank(ob_, n), rr[s2][:, :n], ALU.mult,
                    [("ps", ob_), ("rr", s2)], [("obuf", hb)])
            self.dma("sp", self.obT[h], ob, [("obuf", hb)], [("obT", h)], ("obuf", hb))

        cnt = 0
        for h in range(8):
            for i in range(16):
                s2 = cnt % 2
                cnt += 1
                steps.append((lambda h=h, i=i, s2=s2: front(h, i, s2), lambda h=h, i=i, s2=s2: back(h, i, s2)))
            if with_ctx:
                s2 = cnt % 2
                cnt += 1
                steps.append((lambda h=h, s2=s2: front_c(h, s2), lambda h=h, s2=s2: back_c(h, s2)))
        prev = None
        for fr, bk in steps:
            fr()
            if prev is not None:
                prev()
            prev = bk
        prev()
        ar.pop()

    def retention(self, l, with_ctx):
        ar, S = self.ar, self.S
        ar.push()
        qP = [[ar.alloc([T], BF16) for _ in range(2)] for _ in range(2)]
        kS = [ar.alloc([T], BF16) for _ in range(2)]
        vS = [ar.alloc([NT, 256], BF16) for _ in range(2)]
        for i in range(2):
            S.add("dve", (lambda e, a=qP[0][i][64:128, :]: e.memset(a, 0.0)), writes=[("qz", 0, i)])
            S.add("dve", (lambda e, a=qP[1][i][0:64, :]: e.memset(a, 0.0)), writes=[("qz", 1, i)])
        cg = [ar.alloc([T], BF16) for _ in range(2)]
        rz = ar.alloc([10, 512], F32)
        Dm = [ar.alloc([6, 512], F32) for _ in range(2)]
        dtmp = ar.alloc([512], F32)
        dctx = [ar.alloc([512], F32) for _ in range(2)]
        dctx2 = [ar.alloc([512], F32) for _ in range(2)]
        ssb = [ar.alloc([512], F32) for _ in range(2)]
        rs2 = ar.alloc([512], F32)
        rsc = ar.alloc([512], F32)
        att = [ar.alloc([512], BF16) for _ in range(4)]
        osb = ar.alloc([512], F32)
        osq = ar.alloc([512], BF16)
        rs = ar.alloc([512], F32)
        ybuf = [ar.alloc([T], BF16) for _ in range(2)]
        self.dma("sp", rz, self.retz, [], ["rz"], "ld0")
        qblocks = [(c0, n, j0, False) for (c0, n, j0) in QBLK]
        if with_ctx:
            qblocks.append((CTX_COL0, 256, 0, True))
        scnt = 0
        acnt = 0
        hcnt = 0
        osbs = [ar.alloc([512], F32) for _ in range(2)]
        osqs = [ar.alloc([512], BF16) for _ in range(2)]
        rfc = [0]

        def ret_fin(fb, n, hh, yb, c0, hb):
            yield
            yield
            self.mm(self.bank(6 + hh, n), self.ones, osqs[fb][:, :n], True, True, [("osq", fb), "ones"], [("ps", 6 + hh)])
            yield
            self.act(rs[:, :n], self.bank(6 + hh, n), AF.Ln, [("ps", 6 + hh), "eps"], ["rs"], scale=1.0 / 128,
                     bias=self.epsT[:, 0:1])
            self.act(rs2[:, :n], rs[:, :n], AF.Exp, ["rs"], ["rs2"], scale=-0.5)
            yield
            yield
            self.tt(osbs[fb][:, :n], osbs[fb][:, :n], rs2[:, :n], ALU.mult, [("osb", fb), "rs2"], [("osb", fb)])
            yield
            self.tt(yb[:, c0:c0 + n], osbs[fb][:, :n], cg[hb][:, c0:c0 + n], ALU.mult, [("osb", fb), ("cg", hb)],
                    [("ybuf", hb)])
            yield

        rfin = iter(())
        for m in range(4):
            mb = m % 2
            self.dma("sp", qP[0][mb][0:64, :], self.qcT[m][0:64, :], [], [("qS", 0, mb)], ("qS", 0, mb))
            self.dma("sp", qP[1][mb][64:128, :], self.qcT[m][64:128, :], [], [("qS", 1, mb)], ("qS", 1, mb))
            self.dma("sp", kS[mb], self.kcT[m], [], [("kS", mb)], ("kS", mb))
            self.dma("sp", vS[mb], self.vc[:, :, m * 256:(m + 1) * 256].rearrange("t p d -> p t d"),
                     [], [("vS", mb)], ("vS", mb))
            for hh in range(2):
                h = 2 * m + hh
                pb0 = hh * 64
                hb = hcnt % 2
                hcnt += 1
                self.dma("sp", cg[hb], self.cgT[h], [], [("cg", hb)], ("cg", hb))
                lgf = self.lgS[:, h:h + 1]
                lgb = self.lgS[:, 8 + h:9 + h]
                dm = Dm[hb]
                dk = ("Dm", hb)
                self.act(dm[:, 0, :], rz[:, 0, :], AF.Exp, ["rz", "lgS"], [dk], scale=lgf)
                self.act(dm[:, 1, :], rz[:, 1, :], AF.Exp, ["rz", "lgS"], [dk], scale=lgb)
                for a in range(4):
                    self.act(dm[:, 2 + a, :], rz[:, 2 + a, :], AF.Exp, ["rz", "lgS"], [dk], scale=lgf)
                    self.act(dtmp, rz[:, 6 + a, :], AF.Exp, ["rz", "lgS"], ["dtmp"], scale=lgb)
                    self.tt(dm[:, 2 + a, :], dm[:, 2 + a, :], dtmp, ALU.add, [dk, "dtmp"], [dk])
                yb = ybuf[hb]
                for (c0, n, j0, isctx) in qblocks:
                    if isctx:
                        ktl = [("cc", c) for c in range(2)]
                    else:
                        ktl = [("l", j) for j in range(16)] + [("c", c) for c in range(2)]
                    nk = len(ktl)
                    ob_ = 4 + (scnt // 100000) % 1
                    obank = 4 + hh
                    pendq = []
                    LAG = 3
                    for idx in range(nk + LAG):
                        cur = None
                        if idx < nk:
                            kind, jk = ktl[idx]
                            kt = lat_T(jk) if kind == "l" else CTX_TILES[jk]
                            sb = scnt % 4
                            scnt += 1
                            self.mm(self.bank(sb, n), kS[mb][:, kt * 128:(kt + 1) * 128],
                                    qP[hh][mb][:, c0:c0 + n], True, True,
                                    [("kS", mb), ("qS", hh, mb), ("qz", hh, mb)], [("ps", sb)])
                            ab = acnt % 4
                            acnt += 1
                            ak = ("att", ab)
                            sps = self.bank(sb, n)
                            alt = (kind == "l") and (acnt % 2 == 1)
                            if alt:
                                sf = ssb[(acnt // 2) % 2]
                                sfk = ("ssb", (acnt // 2) % 2)
                                if jk < j0:
                                    self.act(sf[:, :n], sps, AF.Copy, [("ps", sb), "cft"], [sfk],
                                             scale=self.cft[:, h, j0 - jk:j0 - jk + 1])
                                    dsel = dm[:, 0, :n]
                                elif jk > j0 + 3:
                                    self.act(sf[:, :n], sps, AF.Copy, [("ps", sb), "cft"], [sfk],
                                             scale=self.cft[:, 8 + h, jk - j0 - 4:jk - j0 - 3])
                                    dsel = dm[:, 1, :n]
                                else:
                                    self.act(sf[:, :n], sps, AF.Copy, [("ps", sb)], [sfk])
                                    dsel = dm[:, 2 + jk - j0, :n]
                                self.tt(att[ab][:, :n], sf[:, :n], dsel, ALU.mult, [sfk, dk], [ak], eng="pool")
                            elif kind == "l":
                                if jk < j0:
                                    self.stt(att[ab][:, :n], sps, self.cft[:, h, j0 - jk:j0 - jk + 1], dm[:, 0, :n],
                                             ALU.mult, ALU.mult, [("ps", sb), dk, "cft"], [ak])
                                elif jk > j0 + 3:
                                    self.stt(att[ab][:, :n], sps, self.cft[:, 8 + h, jk - j0 - 4:jk - j0 - 3], dm[:, 1, :n],
                                             ALU.mult, ALU.mult, [("ps", sb), dk, "cft"], [ak])
                                else:
                                    self.tt(att[ab][:, :n], sps, dm[:, 2 + jk - j0, :n], ALU.mult, [("ps", sb), dk], [ak])
                            elif kind == "c":
                                dc = dctx[acnt % 2]
                                dck = ("dctx", acnt % 2)
                                i1 = 2 + j0 - jk
                                i2 = 12 + jk - j0
                                dc2 = dctx2[acnt % 2]
                                self.act(dc, dm[:, 0, :], AF.Copy, [dk, "cft"], [dck], scale=self.cft[:, h, i1:i1 + 1])
                                self.act(dc2, dm[:, 1, :], AF.Copy, [dk, "cft"], [("dctx2", acnt % 2)],
                                         scale=self.cft[:, 8 + h, i2:i2 + 1])
                                self.tt(dc, dc, dc2, ALU.add, [dck, ("dctx2", acnt % 2)], [dck], eng="pool")
                                self.tt(att[ab][:, :n], sps, dc[:, :n], ALU.mult, [("ps", sb), dck], [ak])
                            else:
                                self.tt(att[ab][:, :n], sps, dm[:, 2 + jk, :n], ALU.mult, [("ps", sb), dk], [ak])
                            cur = (kt, ab, idx)
                        next(rfin, None)
                        if cur is not None:
                            pendq.append(cur)
                        if idx >= LAG and pendq:
                            kt_, ab_, i_ = pendq.pop(0)
                            self.mm(self.bank(obank, n), vS[mb][:, kt_, hh * 128:(hh + 1) * 128], att[ab_][:, :n],
                                    i_ == 0, i_ == nk - 1, [("vS", mb), ("att", ab_)], [("ps", obank)])
                    for _ in rfin:
                        pass
                    fb = rfc[0] % 2
                    rfc[0] += 1
                    self.act(osbs[fb][:, :n], self.bank(obank, n), AF.Copy, [("ps", obank), "dnS"], [("osb", fb)],
                             scale=self.dnS[:, 1:2])
                    self.act(osqs[fb][:, :n], self.bank(obank, n), AF.Square, [("ps", obank)], [("osq", fb)])
                    rfin = ret_fin(fb, n, hh, yb, c0, hb)
                for _ in rfin:
                    pass
                rfin = iter(())
                if with_ctx:
                    self.dma("sp", self.yrT[h], yb, [("ybuf", hb)], [("yrT", h)], ("ybuf", hb))
                else:
                    self.dma("sp", self.yrT[h][:, 0:512], yb[:, 0:512], [("ybuf", hb)], [("yrT", h)], ("ybuf", hb))
                    self.dma("sp", self.yrT[h][:, 768:T], yb[:, 768:T], [("ybuf", hb)], [("yrT", h)], ("ybuf", hb))
        ar.pop()

    def merge(self, l, skip_ctx):
        ar, S, W = self.ar, self.S, self.W[l]
        ar.push()
        obr = [ar.alloc([8, 768], BF16) for _ in range(3)]
        mT = ar.alloc([16, 768], BF16)
        wbr = [ar.alloc([3, 8, 128], BF16) for _ in range(3)]
        wo = [ar.alloc([16, 128], BF16) for _ in range(2)]
        gb = [ar.alloc([3, 768], BF16) for _ in range(2)]
        m1 = [ar.alloc([512], F32) for _ in range(2)]
        m2 = [ar.alloc([512], F32) for _ in range(2)]
        m3 = [ar.alloc([512], F32) for _ in range(2)]
        pb = self.alloc_post_bufs()
        srcs = (self.oaT, self.obT, self.yrT)
        wc = 0
        postg = iter(())
        for b in range(3):
            parts = [(c0, n, 1 if part_is_ctx(b, p) else 0, not (skip_ctx and part_is_ctx(b, p)))
                     for p, (c0, n) in enumerate(PARTS[b])]
            bc0 = b * 768
            slot = b % 2
            for br in range(3):
                self.dma("act", obr[br], srcs[br][:, :, bc0:bc0 + 768].rearrange("h p t -> p h t"), [], [("obr", br)],
                         ("obr", br))
            for oc in range(16):
                next(postg, None)
                wb_ = wbr[wc % 3]
                wk = ("wbr", wc % 3)
                wc += 1
                self.dma("pool", wb_, W["wbr"][oc], [], [wk], wk)
                g_ = gb[oc % 2]
                gk = ("gb", oc % 2)
                for br in range(3):
                    self.dma("act", g_[:, br, :], self.gT[br * 16 + oc][:, bc0:bc0 + 768], [], [(gk, br)], (gk, br))
                for pi, (c0, n, j, on) in enumerate(parts):
                    if not on:
                        continue
                    u0 = 0 if pi == 0 else 512
                    st = (oc * 2 + pi) % 2
                    for br in range(3):
                        bk = st * 3 + br
                        for kc in range(8):
                            self.mm(self.bank(bk, n), wb_[:, br, kc, :], obr[br][:, kc, u0:u0 + n], kc == 0, kc == 7,
                                    [wk, ("obr", br)], [("ps", bk)])
                    a_, b_, c_ = m1[st], m2[st], m3[st]
                    self.tt(a_[:, :n], self.bank(st * 3, n), g_[:, 0, u0:u0 + n], ALU.mult, [("ps", st * 3), (gk, 0)],
                            [("m1", st)])
                    self.tt(b_[:, :n], self.bank(st * 3 + 1, n), g_[:, 1, u0:u0 + n], ALU.mult,
                            [("ps", st * 3 + 1), (gk, 1)], [("m2", st)])
                    self.tt(c_[:, :n], self.bank(st * 3 + 2, n), g_[:, 2, u0:u0 + n], ALU.mult,
                            [("ps", st * 3 + 2), (gk, 2)], [("m3", st)])
                    self.tt(a_[:, :n], a_[:, :n], b_[:, :n], ALU.add, [("m1", st), ("m2", st)], [("m1", st)], eng="pool")
                    self.tt(mT[:, oc, u0:u0 + n], a_[:, :n], c_[:, :n], ALU.add, [("m1", st), ("m3", st)], [("mT", oc)],
                            eng="pool")
            for oc2 in range(16):
                w_ = wo[oc2 % 2]
                wk = ("wo", oc2 % 2)
                self.dma("pool", w_, W["wo"][oc2], [], [wk], wk)
                st = oc2 % 2
                ybanks = (st * 2, st * 2 + 1)
                for pi, (c0, n, j, on) in enumerate(parts):
                    if not on:
                        continue
                    u0 = 0 if pi == 0 else 512
                    for oc in range(16):
                        self.mm(self.bank(ybanks[pi], n), w_[:, oc, :], mT[:, oc, u0:u0 + n], oc == 0, oc == 15,
                                [wk, ("mT", oc)], [("ps", ybanks[pi])])
                self.post_evac(pb, b, oc2, parts, ybanks, (6, 7), slot)
            for _ in postg:
                pass
            postg = self.post_gen(pb, l, 1, b, parts, (6, 7), slot, self.hT, self.hT, False)
        for _ in postg:
            pass
        ar.pop()


def _rope_tables():
    half = 16
    inv = (10000.0 ** (-np.arange(half, dtype=np.float32) / half)).astype(np.float32)
    pos = np.arange(2048)
    prow = (pos // 64).astype(np.float32)
    pcol = (pos % 64).astype(np.float32)
    cosT = np.zeros((128, 2048), np.float32)
    sinT = np.zeros((128, 2048), np.float32)
    perm = np.zeros((128, 128), np.float32)
    for p in range(128):
        dd = p % 64
        axis = dd // 32
        jj = dd % 32
        i = jj % 16
        first = jj < 16
        ang = (prow if axis == 0 else pcol) * inv[i]
        cosT[p] = np.cos(ang.astype(np.float32))
        s = np.sin(ang.astype(np.float32))
        sinT[p] = -s if first else s
        partner = p + 16 if first else p - 16
        perm[partner, p] = 1.0
    return cosT, sinT, perm


def _ret_tables():
    kl = np.arange(128, dtype=np.float64)[:, None]
    x = np.arange(512, dtype=np.float64)[None, :]
    tabs = np.zeros((128, 10, 512), np.float32)
    tabs[:, 0] = x - kl
    tabs[:, 1] = 512 - x + kl
    for a in range(4):
        z = x - 128 * a - kl
        tabs[:, 2 + a] = np.where(z >= 0, z, BIGZ)
        tabs[:, 6 + a] = np.where(z <= 0, -z, BIGZ)
    rows = np.tile((128.0 * np.arange(20, dtype=np.float32))[None, :], (128, 1)).astype(np.float32)
    return tabs, rows


def _rpb_expand(rpb):
    WIN_R, WIN_C, GW, ROWS = 8, 16, 64, 32
    out = np.zeros((8, 128, 25, 128), np.float32)
    seen = {}
    for i in range(16):
        cls = nb_cls(i)
        jl = nb_jlist(i)
        q = np.arange(128)
        qr = 2 * i + q // 64
        qc = q % 64
        rs = np.clip(qr - 4, 0, ROWS - WIN_R)
        cstart = np.clip(qc - WIN_C // 2, 0, GW - WIN_C)
        for slot, j in enumerate(jl):
            k = np.arange(128)
            kr = 2 * j + k // 64
            kc = k % 64
            row_ok = (kr[:, None] >= rs[None, :]) & (kr[:, None] < rs[None, :] + WIN_R)
            col_ok = (kc[:, None] >= cstart[None, :]) & (kc[:, None] < cstart[None, :] + WIN_C)
            dr = np.clip(kr[:, None] - qr[None, :] + WIN_R - 1, 0, 2 * WIN_R - 2)
            dc = np.clip(kc[:, None] - qc[None, :] + WIN_C - 1, 0, 2 * WIN_C - 2)
            ok = row_ok & col_ok
            key = (cls, slot)
            sig = (ok.tobytes(), dr.tobytes(), dc.tobytes(), j - i)
            if key in seen:
                assert seen[key] == sig, f"class structure mismatch {i} {slot}"
                continue
            seen[key] = sig
            vals = rpb[:, dr, dc]
            out[:, :, cls * 5 + slot, :] = np.where(ok[None], vals, np.float32(-30000.0))
    return out


def _fm(v):
    return np.ascontiguousarray(v.reshape(16, 128).T)


def host_weights(inp, L=2):
    m = {}
    cosT, sinT, perm = _rope_tables()
    retz, rows = _ret_tables()
    m["cosT"], m["sinT"], m["permM"], m["retz"], m["rows128"] = cosT, sinT, perm, retz, rows
    m["identM"] = np.eye(128, dtype=np.float32)
    for l in range(L):
        wm = inp["w_mod"][l].reshape(16, 128, 36, 4, 128)
        m[f"wmod{l}"] = np.ascontiguousarray(wm.transpose(2, 1, 3, 0, 4))
        m[f"bmod{l}"] = np.ascontiguousarray(inp["b_mod"][l].reshape(144, 128).T)
        m[f"pre{l}"] = np.ascontiguousarray(inp["pre_norm"][l].reshape(3, 16, 128).transpose(2, 0, 1))
        m[f"post{l}"] = np.ascontiguousarray(inp["post_norm"][l].reshape(3, 16, 128).transpose(2, 0, 1))
        for i in range(2):
            wi = inp["ffn_w_in"][l, i].reshape(16, 128, 2, 44, 128)
            m[f"fwin{l}_{i}"] = np.ascontiguousarray(wi.transpose(3, 1, 2, 0, 4))
            wo = inp["ffn_w_out"][l, i].reshape(44, 128, 16, 128)
            m[f"fwout{l}_{i}"] = np.ascontiguousarray(wo.transpose(2, 1, 0, 3))
        w = inp["w_in"][l].reshape(16, 128, 30, 4, 128)
        fm = w.transpose(2, 1, 3, 0, 4).reshape(30, 128, 8192)
        tm = w.transpose(2, 1, 0, 3, 4).reshape(30, 128, 8192)
        win = np.ascontiguousarray(fm)
        for g in (4, 5, 10, 11, 14, 15):
            win[g] = tm[g]
        m[f"win{l}"] = win
        wb = inp["w_branch"][l].reshape(3, 8, 128, 16, 128)
        m[f"wbr{l}"] = np.ascontiguousarray(wb.transpose(3, 2, 0, 1, 4))
        wo_ = inp["w_out"][l].reshape(16, 128, 16, 128)
        m[f"wo{l}"] = np.ascontiguousarray(wo_.transpose(2, 1, 0, 3))
        m[f"dlam{l}"] = np.ascontiguousarray(np.tile(inp["diff_lambda"][l].reshape(1, 256), (128, 1)))
        m[f"dnorm{l}"] = np.ascontiguousarray(inp["diff_norm"][l].reshape(128, 1))
        m[f"dnrow{l}"] = np.ascontiguousarray(np.tile(inp["diff_norm"][l].reshape(1, 128), (128, 1)))
        m[f"rnorm{l}"] = np.ascontiguousarray(inp["ret_norm"][l].reshape(128, 1))
        m[f"rdecay{l}"] = np.ascontiguousarray(np.tile(inp["ret_decay"][l].reshape(1, 16), (128, 1)))
        m[f"rpbx{l}"] = _rpb_expand(inp["na_rpb"][l])
    return m


def host_core_inputs(inp, b):
    x = inp["x"][b]
    ctx = inp["ctx"][b]
    tok = np.concatenate([x[:512], ctx, x[512:]], axis=0)
    xT = np.ascontiguousarray(tok.T.reshape(16, 128, T))
    cv = np.stack([_fm(inp["c"][b]), _fm(inp["c_ctx"])], axis=-1)
    return {"xT": xT, "cvec": np.ascontiguousarray(cv)}


_CACHE = {}


def kernel(**inputs):
    inp = {k: np.asarray(v, dtype=np.float32) for k, v in inputs.items()}
    if "prog" not in _CACHE:
        p = Prog(2)
        p.build()
        _CACHE["prog"] = p
    p = _CACHE["prog"]
    wm = host_weights(inp)
    in_maps = []
    for b in range(8):
        d = dict(wm)
        d.update(host_core_inputs(inp, b))
        in_maps.append(d)
    res = run_bass_kernel_spmd(p.nc, in_maps, core_ids=list(range(8)))
    out = np.empty((8, 2048, 2048), np.float32)
    for b in range(8):
        oT = res.results[b]["outT"]
        out[b] = oT.reshape(2048, 2048).T
    return out
```

```python
import math
import numpy as np
import concourse.bass as bass
import concourse.mybir as mybir
from concourse.bass_utils import run_bass_kernel_spmd

F32 = mybir.dt.float32
BF16 = mybir.dt.bfloat16
AF = mybir.ActivationFunctionType
ALU = mybir.AluOpType
AX = mybir.AxisListType

ENGS = ("pe", "act", "dve", "pool", "sp")
EPS = 1e-6
D = 2048
T = 2304
NT = 18
BIGZ = 1.0e6


class Op:
    __slots__ = ("eng", "fn", "deps", "dma", "signal", "sigval", "gidx", "dmaval")

    def __init__(self, eng, fn, dma):
        self.eng = eng
        self.fn = fn
        self.dma = dma
        self.deps = None
        self.signal = False
        self.sigval = 0
        self.dmaval = 0
        self.gidx = 0


class Sched:
    def __init__(self, nc, same_engine_sync=True):
        self.nc = nc
        self.ops = []
        self.last_w = {}
        self.readers = {}
        self.last_dma = {}
        self.dma_cnt = {}
        self.same_engine_sync = same_engine_sync
        self.last_op_eng = {e: None for e in ENGS}

    def add(self, eng, fn, reads=(), writes=(), dma=None):
        op = Op(eng, fn, dma)
        op.gidx = len(self.ops)
        deps = {}
        psr = [b for b in reads if isinstance(b, tuple) and b[0] == "ps"]
        if psr:
            writes = list(writes) + [b for b in psr if b not in writes]

        def adddep(d):
            if d is None or d is op:
                return
            if d.dma is not None:
                k = ("dma", d.dma)
                if k not in deps or deps[k].dmaval < d.dmaval:
                    deps[k] = d
            else:
                k = d.eng
                if k not in deps or deps[k].gidx < d.gidx:
                    deps[k] = d

        for b in reads:
            adddep(self.last_w.get(b))
        for b in writes:
            adddep(self.last_w.get(b))
            for r in self.readers.get(b, ()):
                adddep(r)
        for b in reads:
            self.readers.setdefault(b, []).append(op)
        for b in writes:
            self.last_w[b] = op
            self.readers[b] = []
        if dma is not None:
            adddep(self.last_dma.get(dma))
            self.last_dma[dma] = op
            self.dma_cnt[dma] = self.dma_cnt.get(dma, 0) + 16
            op.dmaval = self.dma_cnt[dma]
        op.deps = list(deps.values())
        self.ops.append(op)
        self.last_op_eng[eng] = op
        return op

    def barrier(self):
        lasts = [o for o in self.last_op_eng.values() if o is not None]
        dmas = list(self.last_dma.values())
        for e in ENGS:
            op = Op(e, None, None)
            op.gidx = len(self.ops)
            op.deps = list(lasts + dmas)
            self.ops.append(op)
        self.last_w = {}
        self.readers = {}

    def emit(self):
        nc = self.nc
        for op in self.ops:
            for d in op.deps:
                if d.dma is None:
                    if d.eng == op.eng and (d.eng == "pe" or not self.same_engine_sync):
                        continue
                    d.signal = True
        cnt = {e: 0 for e in ENGS}
        for op in self.ops:
            if op.signal:
                cnt[op.eng] += 1
                op.sigval = cnt[op.eng]
        per_eng = {e: [o for o in self.ops if o.eng == e] for e in ENGS}
        sems_eng = {e: nc.alloc_semaphore(name=f"se_{e}") for e in ENGS}
        dkeys = []
        for o in self.ops:
            if o.dma is not None and o.dma not in dkeys:
                dkeys.append(o.dma)
        sems_dma = {k: nc.alloc_semaphore(name=f"sd_{i}") for i, k in enumerate(dkeys)}
        same = self.same_engine_sync
        stats = {e: [0, 0] for e in ENGS}

        def run(eng_name, eng):
            waited = {}
            for op in per_eng[eng_name]:
                for d in op.deps:
                    if d.dma is not None:
                        sem = sems_dma[d.dma]
                        val = d.dmaval
                        key = ("dma", d.dma)
                    else:
                        if d.eng == eng_name and (eng_name == "pe" or not same):
                            continue
                        sem = sems_eng[d.eng]
                        val = d.sigval
                        key = d.eng
                    if waited.get(key, 0) >= val:
                        continue
                    waited[key] = val
                    eng.wait_ge(sem, val)
                    stats[eng_name][1] += 1
                if op.fn is None:
                    continue
                inst = op.fn(eng)
                stats[eng_name][0] += 1
                if op.dma is not None:
                    inst.then_inc(sems_dma[op.dma], 16)
                elif op.signal:
                    inst.then_inc(sems_eng[eng_name], 1)

        with nc.Block() as block:
            @block.tensor
            def _(e):
                run("pe", e)

            @block.scalar
            def _(e):
                run("act", e)

            @block.vector
            def _(e):
                run("dve", e)

            @block.gpsimd
            def _(e):
                run("pool", e)

            @block.sync
            def _(e):
                run("sp", e)
        return stats


class Arena:
    def __init__(self, tensor, nwords):
        self.t = tensor
        self.n = nwords
        self.off = 0
        self.marks = []

    def alloc(self, shape, dtype):
        nel = int(np.prod(shape))
        esz = 4 if dtype == F32 else 2
        words = (nel * esz + 3) // 4
        words = (words + 7) // 8 * 8
        assert self.off + words <= self.n, f"SBUF arena overflow {self.off}+{words}>{self.n}"
        ap = self.t[:, self.off:self.off + words]
        self.off += words
        if dtype != F32:
            ap = ap.bitcast(dtype)
        ap = ap[:, 0:nel]
        if len(shape) == 2:
            ap = ap.rearrange("p (a b) -> p a b", b=shape[1])
        elif len(shape) == 3:
            ap = ap.rearrange("p (a b c) -> p a b c", b=shape[1], c=shape[2])
        elif len(shape) == 4:
            ap = ap.rearrange("p (a b c d) -> p a b c d", b=shape[1], c=shape[2], d=shape[3])
        return ap

    def push(self):
        self.marks.append(self.off)

    def pop(self):
        self.off = self.marks.pop()


def lat_T(j):
    return j if j < 4 else j + 2


CTX_TILES = (4, 5)
QBLK = [(0, 512, 0), (768, 512, 4), (1280, 512, 8), (1792, 512, 12)]
CTX_COL0 = 512
PARTS = [[(b * 768, 512), (b * 768 + 512, 256)] for b in range(3)]


def part_is_ctx(b, p):
    return b == 0 and p == 1


def lat_col(c):
    return c if c < 512 else c - 256


def nb_jlist(i):
    s = set()
    for r in (2 * i, 2 * i + 1):
        rs = min(max(r - 4, 0), 24)
        for kr in range(rs, rs + 8):
            s.add(kr // 2)
    return sorted(s)


def nb_cls(i):
    return {0: 0, 1: 1, 14: 3, 15: 4}.get(i, 2)


class Prog:
    def __init__(self, nlayers=2, dbg=(), phases=None, same_engine_sync=True):
        self.nl = nlayers
        self.dbg = set(dbg)
        self.phases = phases
        nc = bass.Bass("TRN2", target_bir_lowering=False)
        self.nc = nc
        self.S = Sched(nc, same_engine_sync=same_engine_sync)
        self.in_names = []
        self.fast_rcp = False
        self.mod_overlap = True
        self.fuse_stats = True

    def din(self, name, shape, dt=F32):
        self.in_names.append(name)
        return self.nc.dram_tensor(name, list(shape), dt, kind="ExternalInput").ap()

    def dscr(self, name, shape, dt):
        kind = "ExternalOutput" if name in self.dbg else "Internal"
        return self.nc.dram_tensor(name, list(shape), dt, kind=kind).ap()

    def dma(self, eng, out, in_, reads, writes, sem):
        return self.S.add(eng, lambda e: e.dma_start(out=out, in_=in_), reads, writes, dma=sem)

    def mm(self, out, lhsT, rhs, start, stop, reads, writes):
        return self.S.add("pe", lambda e: e.matmul(out, lhsT, rhs, start=start, stop=stop), reads, writes)

    def act(self, out, in_, func, reads, writes, scale=None, bias=None):
        kw = {}
        if scale is not None:
            kw["scale"] = scale
        if bias is not None:
            kw["bias"] = bias
        return self.S.add("act", lambda e: e.activation(out=out, in_=in_, func=func, **kw), reads, writes)

    def tt(self, out, in0, in1, op, reads, writes, eng="dve"):
        return self.S.add(eng, lambda e: e.tensor_tensor(out=out, in0=in0, in1=in1, op=op), reads, writes)

    def stt(self, out, in0, scalar, in1, op0, op1, reads, writes):
        return self.S.add("dve", lambda e: e.scalar_tensor_tensor(out=out, in0=in0, scalar=scalar, in1=in1,
                                                                   op0=op0, op1=op1), reads, writes)

    def ts(self, out, in0, s1, op0, reads, writes, s2=None, op1=None, eng="dve"):
        if op1 is None:
            return self.S.add(eng, lambda e: e.tensor_scalar(out=out, in0=in0, scalar1=s1, scalar2=None, op0=op0),
                              reads, writes)
        return self.S.add(eng, lambda e: e.tensor_scalar(out=out, in0=in0, scalar1=s1, scalar2=s2, op0=op0, op1=op1),
                          reads, writes)

    def recip(self, out, in_, reads, writes):
        return self.S.add("dve", lambda e: e.reciprocal(out=out, in_=in_), reads, writes)

    def rcp(self, out, in_, scratch, reads, writes):
        if self.fast_rcp:
            return self.S.add("dve", lambda e: e.reciprocal_approx_accurate(out, in_, scratch), reads, writes)
        return self.S.add("dve", lambda e: e.reciprocal(out=out, in_=in_), reads, writes)

    def bank(self, b, n=512, c0=0):
        return self.ps[:, b * 512 + c0: b * 512 + c0 + n]

    def build(self):
        nc = self.nc
        NW = 52600
        self.declare_io()
        with nc.sbuf_tensor("arena", [128, NW], F32) as arena_t, \
             nc.psum_tensor("psum", [128, 8 * 512], F32) as ps:
            self.ps = ps
            self.ar = Arena(arena_t, NW)
            self.body()
            self.stats = self.S.emit()
        return nc

    def declare_io(self):
        L = self.nl
        self.xT = self.din("xT", [16, 128, T])
        self.cvec = self.din("cvec", [128, 16, 2])
        self.cosT = self.din("cosT", [128, 2048])
        self.sinT = self.din("sinT", [128, 2048])
        self.permM = self.din("permM", [128, 128])
        self.retz = self.din("retz", [128, 10, 512])
        self.rows128 = self.din("rows128", [128, 20])
        self.identM = self.din("identM", [128, 128])
        self.W = []
        for l in range(L):
            w = {}
            w["wmod"] = self.din(f"wmod{l}", [36, 128, 4, 16, 128])
            w["bmod"] = self.din(f"bmod{l}", [128, 144])
            w["pre"] = self.din(f"pre{l}", [128, 3, 16])
            w["post"] = self.din(f"post{l}", [128, 3, 16])
            w["fwin"] = [self.din(f"fwin{l}_{i}", [44, 128, 2, 16, 128]) for i in range(2)]
            w["fwout"] = [self.din(f"fwout{l}_{i}", [16, 128, 44, 128]) for i in range(2)]
            w["win"] = self.din(f"win{l}", [30, 128, 8192])
            w["wbr"] = self.din(f"wbr{l}", [16, 128, 3, 8, 128])
            w["wo"] = self.din(f"wo{l}", [16, 128, 16, 128])
            w["dlam"] = self.din(f"dlam{l}", [128, 256])
            w["dnorm"] = self.din(f"dnorm{l}", [128, 1])
            w["dnrow"] = self.din(f"dnrow{l}", [128, 128])
            w["rnorm"] = self.din(f"rnorm{l}", [128, 1])
            w["rdecay"] = self.din(f"rdecay{l}", [128, 16])
            w["rpbx"] = self.din(f"rpbx{l}", [8, 128, 25, 128])
            self.W.append(w)
        self.outT = self.nc.dram_tensor("outT", [16, 128, 2048], F32, kind="ExternalOutput").ap()
        self.hT = self.dscr("hT", [16, 128, T], F32)
        self.yscr = self.dscr("yscr", [2, 16, 128, 768], F32)
        for nm in ("qaT", "kaT", "qbT", "kbT", "cgT", "oaT", "obT", "yrT"):
            setattr(self, nm, self.dscr(nm, [8, 128, T], BF16))
        for nm in ("qcT", "kcT"):
            setattr(self, nm, self.dscr(nm, [4, 128, T], BF16))
        for nm in ("va", "vb", "vc"):
            setattr(self, nm, self.dscr(nm, [NT, 128, 1024], BF16))
        self.gT = self.dscr("gT", [48, 128, T], BF16)
        self.modout = self.dscr("modout", [128, 144, 2], F32) if "modout" in self.dbg else None

    def body(self):
        ar = self.ar
        S = self.S
        self.ones = ar.alloc([128], BF16)
        self.epsT = ar.alloc([1], F32)
        self.oneT = ar.alloc([1], F32)
        self.out_keys = []
        self.csil = ar.alloc([16, 2], F32)
        self.modT = ar.alloc([144, 2], F32)
        self.modN = ar.alloc([144, 2], F32)
        self.rstdN = ar.alloc([T], F32)
        self.rstd_valid = False
        self.mod_ready = set()
        self.bmodS = ar.alloc([144], F32)
        self.preS = ar.alloc([3, 16], F32)
        self.postS = ar.alloc([3, 16], F32)
        self.Avec = ar.alloc([3, 16, 2], F32)
        self.Gvec = ar.alloc([3, 16, 2], F32)
        self.lamS = ar.alloc([8], F32)
        self.dnS = ar.alloc([2], F32)
        self.lgS = ar.alloc([16], F32)
        self.cft = ar.alloc([16, 20], F32)
        S.add("dve", lambda e: e.memset(self.ones, 1.0), writes=["ones"])
        S.add("dve", lambda e: e.memset(self.epsT, EPS), writes=["eps"])
        S.add("dve", lambda e: e.memset(self.oneT, 1.0), writes=["one"])
        self.dma("sp", self.csil, self.cvec, [], ["csil"], "ld0")
        self.csilb = ar.alloc([16, 2], BF16)
        self.act(self.csilb, self.csil, AF.Silu, ["csil"], ["csilb"])
        ph = self.phases
        h_in = self.xT
        for l in range(self.nl):
            last = (l == self.nl - 1) and not getattr(self, 'force_not_last', False)
            if ph is None or "mod" in ph:
                self.phase_mod(l)
                S.barrier()
            if ph is None or "ffn1" in ph:
                self.ffn_sublayer(l, 0, 0, h_in, self.hT, skip_ctx=False, final=False)
                S.barrier()
            h_in = self.hT
            if ph is None or "mix_in" in ph:
                self.mixer_inproj(l)
                S.barrier()
            if ph is None or "attA" in ph:
                self.attn_a(l, with_ctx=not last)
                S.barrier()
            if ph is None or "attB" in ph:
                self.attn_b(l, with_ctx=not last)
                S.barrier()
            if ph is None or "ret" in ph:
                self.retention(l, with_ctx=not last)
                S.barrier()
            if ph is None or "merge" in ph:
                self.merge(l, skip_ctx=last)
                S.barrier()
            if ph is None or "ffn2" in ph:
                self.ffn_sublayer(l, 1, 2, self.hT, self.outT if last else self.hT, skip_ctx=last, final=last)
                S.barrier()
        S.add("sp", None, reads=list(self.out_keys))

    def phase_mod(self, l):
        ar, S, W = self.ar, self.S, self.W[l]
        ar.push()
        B = 7
        if l not in self.mod_ready:
            wb = [ar.alloc([4, 16, 128], BF16) for _ in range(3)]
            for g in range(36):
                buf = wb[g % 3]
                self.dma("pool", buf, W["wmod"][g], [], [("wm", g % 3)], ("wm", g % 3))
                for mi in range(4):
                    mc = g * 4 + mi
                    for kc in range(16):
                        self.mm(self.ps[:, B * 512 + mc * 2: B * 512 + mc * 2 + 2], buf[:, mi, kc, :],
                                self.csilb[:, kc, :], kc == 0, kc == 15, [("wm", g % 3), "csilb"], [("ps", B)])
        self.dma("sp", self.bmodS, W["bmod"], [], ["bmod"], "ld0")
        self.dma("sp", self.preS, W["pre"], [], ["pre"], "ld1")
        self.dma("sp", self.postS, W["post"], [], ["post"], "ld2")
        if l in self.mod_ready:
            for j in range(2):
                self.tt(self.modT[:, :, j], self.modN[:, :, j], self.bmodS, ALU.add, ["modN", "bmod"], ["modT"])
        else:
            psv = self.bank(B, 288).rearrange("p (a b) -> p a b", b=2)
            for j in range(2):
                self.tt(self.modT[:, :, j], psv[:, :, j], self.bmodS, ALU.add, [("ps", B), "bmod"], ["modT"])
        if self.modout is not None:
            self.dma("sp", self.modout, self.modT, ["modT"], ["modout"], "ld0")
        for s in range(3):
            for j in range(2):
                sc = self.modT[:, (3 * s + 1) * 16:(3 * s + 2) * 16, j]
                gt = self.modT[:, (3 * s + 2) * 16:(3 * s + 3) * 16, j]
                self.stt(self.Avec[:, s, :, j], sc, 1.0, self.preS[:, s, :], ALU.add, ALU.mult,
                         ["modT", "pre"], ["Avec"])
                self.stt(self.Gvec[:, s, :, j], gt, 0.5 if s != 1 else 1.0, self.postS[:, s, :], ALU.mult, ALU.mult,
                         ["modT", "post"], ["Gvec"])
        lam_init = 0.8 - 0.6 * math.exp(-0.3 * l)
        dl = ar.alloc([256], F32)
        pr = ar.alloc([128], F32)
        sm = ar.alloc([2], F32)
        self.dma("sp", dl, W["dlam"], [], ["dl"], "ld0")
        dlv = dl.rearrange("p (a b) -> p a b", b=64)
        prv = pr.rearrange("p (a b) -> p a b", b=64)
        self.tt(prv[:, 0, :], dlv[:, 0, :], dlv[:, 1, :], ALU.mult, ["dl"], ["pr"])
        self.tt(prv[:, 1, :], dlv[:, 2, :], dlv[:, 3, :], ALU.mult, ["dl"], ["pr"])
        S.add("dve", lambda e: e.reduce_sum(out=sm, in_=prv, axis=AX.X), ["pr"], ["sm"])
        self.act(sm, sm, AF.Exp, ["sm"], ["sm"])
        self.stt(self.lamS[:, 0:1], sm[:, 1:2], -lam_init, sm[:, 0:1], ALU.add, ALU.subtract, ["sm"], ["lamS"])
        dn = ar.alloc([2], F32)
        self.dma("sp", dn[:, 0:1], W["dnorm"], [], ["dn"], "ld1")
        self.dma("sp", dn[:, 1:2], W["rnorm"], [], ["dn"], "ld2")
        self.ts(self.dnS[:, 0:1], dn[:, 0:1], 1.0 - lam_init, ALU.mult, ["dn"], ["dnS"])
        self.ts(self.dnS[:, 1:2], dn[:, 1:2], 1.0, ALU.mult, ["dn"], ["dnS"])
        rd = ar.alloc([16], F32)
        self.dma("sp", rd, W["rdecay"], [], ["rd"], "ld0")
        self.act(rd, rd, AF.Exp, ["rd"], ["rd"], scale=-1.0)
        self.act(rd, rd, AF.Ln, ["rd"], ["rd", "one"][:1], bias=self.oneT[:, 0:1])
        self.ts(self.lgS, rd, -1.0, ALU.mult, ["rd"], ["lgS"])
        r128 = ar.alloc([20], F32)
        self.dma("sp", r128, self.rows128, [], ["r128"], "ld1")
        for k in range(16):
            self.act(self.cft[:, k, :], r128, AF.Exp, ["r128", "lgS"], ["cft"], scale=self.lgS[:, k:k + 1])
        ar.pop()

    def mod_overlap_gen(self, ln, wbufs):
        W = self.W[ln]
        modNf = self.modN.rearrange("p a b -> p (a b)")
        for g in range(36 + 2):
            if g < 36:
                self.dma("pool", wbufs[g % 3], W["wmod"][g], [], [("wmo", g % 3)], ("wm", g % 3))
            if g >= 2:
                g2 = g - 2
                buf = wbufs[g2 % 3]
                c0 = 7 * 512 + 384 + (g2 % 8) * 8
                for mi in range(4):
                    for kc in range(16):
                        self.mm(self.ps[:, c0 + mi * 2: c0 + mi * 2 + 2], buf[:, mi, kc, :], self.csilb[:, kc, :],
                                kc == 0, kc == 15, [("wmo", g2 % 3), "csilb"], [("ps", 7)])
                self.S.add("dve", (lambda e, o=modNf[:, g2 * 8:g2 * 8 + 8], i=self.ps[:, c0:c0 + 8]:
                                   e.tensor_copy(out=o, in_=i)), [("ps", 7)], ["modN"])
            yield
        self.mod_ready.add(ln)

    def prep_gen(self, l, sidx, h_src, parts, uT, ucols, statbank, bufs):
        hb, sqb, tmpf, rst = bufs
        if self.rstd_valid and self.fuse_stats:
            for (c0, n, j), uc0 in zip(parts, ucols):
                for ch in range(16):
                    k = ch % 4
                    self.dma("sp", hb[k][:, :n], h_src[ch][:, c0:c0 + n], [("h", ch, c0)], [("hb", k)], ("hb", k))
                    self.tt(tmpf[ch % 2][:, :n], hb[k][:, :n], self.rstdN[:, c0:c0 + n], ALU.mult,
                            [("hb", k), "rstdN"], [("tmpf", ch % 2)])
                    self.act(uT[:, ch, uc0:uc0 + n], tmpf[ch % 2][:, :n], AF.Identity,
                             [("tmpf", ch % 2), "Avec", "modT"], [("uT", ch)],
                             scale=self.Avec[:, sidx, ch, j:j + 1], bias=self.modT[:, 3 * sidx * 16 + ch, j:j + 1])
                    yield
            return
        for (c0, n, j), uc0 in zip(parts, ucols):
            for ch in range(18):
                if ch < 16:
                    k = ch % 4
                    self.dma("sp", hb[k][:, :n], h_src[ch][:, c0:c0 + n], [("h", ch, c0)], [("hb", k)], ("hb", k))
                    self.act(sqb[k][:, :n], hb[k][:, :n], AF.Square, [("hb", k)], [("sqb", k)])
                if ch >= 2:
                    c2 = ch - 2
                    self.mm(self.bank(statbank, n), self.ones, sqb[c2 % 4][:, :n], c2 == 0, c2 == 15,
                            [("sqb", c2 % 4), "ones"], [("ps", statbank)])
                yield
            self.act(rst[:, :n], self.bank(statbank, n), AF.Sqrt, [("ps", statbank), "eps"], ["rst"],
                     scale=1.0 / D, bias=self.epsT[:, 0:1])
            self.recip(rst[:, :n], rst[:, :n], ["rst"], ["rst"])
            for ch in range(16):
                k = ch % 4
                self.dma("sp", hb[k][:, :n], h_src[ch][:, c0:c0 + n], [("h", ch, c0)], [("hb", k)], ("hb", k))
                self.tt(tmpf[ch % 2][:, :n], hb[k][:, :n], rst[:, :n], ALU.mult, [("hb", k), "rst"], [("tmpf", ch % 2)])
                self.act(uT[:, ch, uc0:uc0 + n], tmpf[ch % 2][:, :n], AF.Identity,
                         [("tmpf", ch % 2), "Avec", "modT"], [("uT", ch)],
                         scale=self.Avec[:, sidx, ch, j:j + 1], bias=self.modT[:, 3 * sidx * 16 + ch, j:j + 1])
                yield

    def alloc_prep_bufs(self):
        ar = self.ar
        hb = [ar.alloc([512], F32) for _ in range(4)]
        sqb = [ar.alloc([512], BF16) for _ in range(4)]
        tmpf = [ar.alloc([512], F32) for _ in range(2)]
        rst = ar.alloc([512], F32)
        return hb, sqb, tmpf, rst

    def alloc_post_bufs(self):
        ar = self.ar
        d = {}
        d["ysb"] = [ar.alloc([768], F32) for _ in range(2)]
        d["ysq"] = [ar.alloc([768], BF16) for _ in range(2)]
        d["yl"] = [ar.alloc([768], F32) for _ in range(2)]
        d["hl"] = [ar.alloc([768], F32) for _ in range(2)]
        d["ho"] = [ar.alloc([768], F32) for _ in range(2)]
        d["rstp"] = ar.alloc([768], F32)
        return d

    def post_evac(self, pb, blk, oc, parts, ybanks, statbanks, slot):
        ysb, ysq = pb["ysb"][oc % 2], pb["ysq"][oc % 2]
        off = 0
        for pi, (c0, n, j, on) in enumerate(parts):
            if on:
                yb = self.bank(ybanks[pi], n)
                self.act(ysq[:, off:off + n], yb, AF.Square, [("ps", ybanks[pi])], [("ysq", oc % 2, pi)])
                self.mm(self.bank(statbanks[pi], n), self.ones, ysq[:, off:off + n], oc == 0, oc == 15,
                        [("ysq", oc % 2, pi), "ones"], [("ps", statbanks[pi])])
                self.S.add("dve", (lambda e, o=ysb[:, off:off + n], i=yb: e.tensor_copy(out=o, in_=i)),
                           [("ps", ybanks[pi])], [("ysb", oc % 2, pi)])
            off += n
        ntot = off
        act_parts = [pi for pi, p in enumerate(parts) if p[3]]
        self.dma("sp", self.yscr[slot][oc][:, :ntot], ysb[:, :ntot], [("ysb", oc % 2, pi) for pi in act_parts],
                 [("yscr", slot, oc)], ("ysb", oc % 2))

    def post_gen(self, pb, l, sidx, blk, parts, statbanks, slot, h_in, h_out, final):
        rstp = pb["rstp"]
        off = 0
        offs = []
        for pi, (c0, n, j, on) in enumerate(parts):
            offs.append(off)
            if on:
                self.act(rstp[:, off:off + n], self.bank(statbanks[pi], n), AF.Sqrt, [("ps", statbanks[pi]), "eps"],
                         [("rstp", pi)], scale=1.0 / D, bias=self.epsT[:, 0:1])
                self.recip(rstp[:, off:off + n], rstp[:, off:off + n], [("rstp", pi)], [("rstp", pi)])
            off += n
        ntot = off
        for oc in range(16):
            k = oc % 2
            yl, hl, ho = pb["yl"][k], pb["hl"][k], pb["ho"][k]
            self.dma("sp", yl[:, :ntot], self.yscr[slot][oc][:, :ntot], [("yscr", slot, oc)], [("yl", k)], ("yl", k))
            for pi, (c0, n, j, on) in enumerate(parts):
                if not on:
                    continue
                o = offs[pi]
                self.dma("sp", hl[:, o:o + n], h_in[oc][:, c0:c0 + n], [("h", oc, c0)], [("hl", k, pi)], ("hl", k, pi))
                self.tt(yl[:, o:o + n], yl[:, o:o + n], rstp[:, o:o + n], ALU.mult, [("yl", k), ("rstp", pi)],
                        [("yl", k)])
                self.stt(ho[:, o:o + n], yl[:, o:o + n], self.Gvec[:, sidx, oc, j:j + 1], hl[:, o:o + n],
                         ALU.mult, ALU.add, [("yl", k), ("hl", k, pi), "Gvec"], [("ho", k, pi)])
                if final:
                    lc = lat_col(c0)
                    self.out_keys.append(("OUT", oc, c0))
                    self.dma("sp", h_out[oc][:, lc:lc + n], ho[:, o:o + n], [("ho", k, pi)], [("OUT", oc, c0)],
                             ("ho", k, pi))
                else:
                    self.dma("sp", h_out[oc][:, c0:c0 + n], ho[:, o:o + n], [("ho", k, pi)], [("h", oc, c0)],
                             ("ho", k, pi))
                    if self.fuse_stats:
                        self.act(pb["ysq"][k][:, o:o + n], ho[:, o:o + n], AF.Square, [("ho", k, pi)], [("ysq", k, pi)])
            if self.fuse_stats and not final and oc >= 1:
                self._nstat_mm(pb, parts, offs, statbanks, oc - 1)
            yield
        if self.fuse_stats and not final:
            self._nstat_mm(pb, parts, offs, statbanks, 15)
            for pi, (c0, n, j, on) in enumerate(parts):
                if not on:
                    continue
                self.act(self.rstdN[:, c0:c0 + n], self.bank(statbanks[pi], n), AF.Sqrt, [("ps", statbanks[pi]), "eps"],
                         ["rstdN"], scale=1.0 / D, bias=self.epsT[:, 0:1])
                self.recip(self.rstdN[:, c0:c0 + n], self.rstdN[:, c0:c0 + n], ["rstdN"], ["rstdN"])
            yield

    def _nstat_mm(self, pb, parts, offs, statbanks, oc):
        k = oc % 2
        for pi, (c0, n, j, on) in enumerate(parts):
            if not on:
                continue
            o = offs[pi]
            self.mm(self.bank(statbanks[pi], n), self.ones, pb["ysq"][k][:, o:o + n], oc == 0, oc == 15,
                    [("ysq", k, pi), "ones"], [("ps", statbanks[pi])])

    def post_pass(self, *a):
        for _ in self.post_gen(*a):
            pass

    def ffn_sublayer(self, l, i, sidx, h_in, h_out, skip_ctx, final):
        ar, S, W = self.ar, self.S, self.W[l]
        ar.push()
        uT = ar.alloc([16, 768], BF16)
        gT = ar.alloc([44, 768], BF16)
        wbuf = [ar.alloc([2, 16, 128], BF16) for _ in range(3)]
        wobuf = [ar.alloc([44, 128], BF16) for _ in range(2)]
        sa = [ar.alloc([512], BF16) for _ in range(2)]
        pbufs = self.alloc_prep_bufs()
        pb = self.alloc_post_bufs()

        def parts_of(b):
            return [(c0, n, 1 if part_is_ctx(b, p) else 0, not (skip_ctx and part_is_ctx(b, p)))
                    for p, (c0, n) in enumerate(PARTS[b])]

        def prep_for(b, statbank):
            ps_ = [(c0, n, j) for (c0, n, j, on) in parts_of(b) if on]
            uc = [0 if idx == 0 else 512 for idx, (c0, n, j, on) in enumerate(parts_of(b)) if on]
            return self.prep_gen(l, sidx, h_in, ps_, uT, uc, statbank, pbufs)

        for _ in prep_for(0, 7):
            pass
        wcount = 0
        postg = iter(())
        for b in range(3):
            parts = parts_of(b)
            slot = b % 2
            for fc in range(44):
                next(postg, None)
                wb = wbuf[wcount % 3]
                wk = ("wbuf", wcount % 3)
                self.dma("pool", wb, W["fwin"][i][fc], [], [wk], wk)
                wcount += 1
                st = fc % 2
                banks = (st * 3, st * 3 + 1, st * 3 + 2)
                for half in range(2):
                    for kc in range(16):
                        self.mm(self.bank(banks[half]), wb[:, half, kc, :], uT[:, kc, 0:512], kc == 0, kc == 15,
                                [wk, ("uT", kc)], [("ps", banks[half])])
                if parts[1][3]:
                    for half in range(2):
                        for kc in range(16):
                            self.mm(self.bank(banks[2], 256, half * 256), wb[:, half, kc, :], uT[:, kc, 512:768],
                                    kc == 0, kc == 15, [wk, ("uT", kc)], [("ps", banks[2])])
                self.act(sa[0], self.bank(banks[0]), AF.Silu, [("ps", banks[0])], [("sa", 0)])
                self.tt(gT[:, fc, 0:512], sa[0], self.bank(banks[1]), ALU.mult, [("sa", 0), ("ps", banks[1])],
                        [("gT", fc)])
                if parts[1][3]:
                    self.act(sa[1][:, :256], self.bank(banks[2], 256, 0), AF.Silu, [("ps", banks[2])], [("sa", 1)])
                    self.tt(gT[:, fc, 512:768], sa[1][:, :256], self.bank(banks[2], 256, 256), ALU.mult,
                            [("sa", 1), ("ps", banks[2])], [("gT", fc)])
            pg = prep_for(b + 1, 5) if b < 2 else iter(())
            for oc in range(16):
                for _ in range(5):
                    next(pg, None)
                wo = wobuf[oc % 2]
                wok = ("wobuf", oc % 2)
                self.dma("pool", wo, W["fwout"][i][oc], [], [wok], wok)
                st = oc % 2
                ybanks = (st * 2, st * 2 + 1)
                for pi, (c0, n, j, on) in enumerate(parts):
                    if not on:
                        continue
                    u0 = 0 if pi == 0 else 512
                    for fc in range(44):
                        self.mm(self.bank(ybanks[pi], n), wo[:, fc, :], gT[:, fc, u0:u0 + n], fc == 0, fc == 43,
                                [wok, ("gT", fc)], [("ps", ybanks[pi])])
                self.post_evac(pb, b, oc, parts, ybanks, (6, 7), slot)
            for _ in pg:
                pass
            for _ in postg:
                pass
            postg = self.post_gen(pb, l, sidx, b, parts, (6, 7), slot, h_in, h_out, final)
        for _ in postg:
            pass
        self.rstd_valid = not final
        ar.pop()

    def mixer_inproj(self, l):
        ar, S, W = self.ar, self.S, self.W[l]
        ar.push()
        uT = ar.alloc([16, T], BF16)
        wbuf = [ar.alloc([8192], BF16) for _ in range(2)]
        cbuf = [ar.alloc([T], BF16) for _ in range(4)]
        vbuf = [ar.alloc([512], BF16) for _ in range(3)]
        cosS = ar.alloc([2048], F32)
        sinS = ar.alloc([2048], F32)
        pm = ar.alloc([128], F32)
        tsb = [ar.alloc([512], F32) for _ in range(2)]
        ra = [ar.alloc([512], F32) for _ in range(2)]
        rb = [ar.alloc([512], F32) for _ in range(2)]
        pbufs = self.alloc_prep_bufs()
        self.dma("sp", cosS, self.cosT, [], ["cos"], "ld0")
        self.dma("sp", sinS, self.sinT, [], ["sin"], "ld1")
        self.dma("sp", pm, self.permM, [], ["pm"], "ld2")
        allparts = [(0, 512, 0), (512, 256, 1), (768, 512, 0), (1280, 512, 0), (1792, 512, 0)]
        for _ in self.prep_gen(l, 1, self.hT, allparts, uT, [p[0] for p in allparts], 7, pbufs):
            pass
        fparts = [(0, 512, 0), (512, 256, None), (768, 512, 4), (1280, 512, 8), (1792, 512, 12)]
        ccount = 0
        ecount = 0
        vcount = 0
        rcount = 0
        for g in range(30):
            wb = wbuf[g % 2]
            wk = ("wbuf", g % 2)
            self.dma("pool", wb, W["win"][g], [], [wk], wk)
            if g in (4, 5, 10, 11, 14, 15):
                dst = {4: self.va, 5: self.va, 10: self.vb, 11: self.vb, 14: self.vc, 15: self.vc}[g]
                hoff = (g % 2) * 512
                wv = wb.rearrange("p (k c) -> p k c", c=512)
                for tt_ in range(NT):
                    bk = ecount % 4
                    ecount += 1
                    for kc in range(16):
                        self.mm(self.bank(bk), uT[:, kc, tt_ * 128:(tt_ + 1) * 128], wv[:, kc, :], kc == 0, kc == 15,
                                [wk, ("uT", kc)], [("ps", bk)])
                    vb_ = vbuf[vcount % 3]
                    vk = ("vbuf", vcount % 3)
                    vcount += 1
                    self.act(vb_, self.bank(bk), AF.Copy, [("ps", bk)], [vk])
                    self.dma("sp", dst[tt_][:, hoff:hoff + 512], vb_, [vk], [("vdst", g, tt_)], vk)
                continue
            wf = wb.rearrange("p (c k f) -> p c k f", k=16, f=128)
            for ci in range(4):
                cc = g * 4 + ci
                if g < 2:
                    dst, mode, sc = self.qaT[cc], "rope", 1.0
                elif g < 4:
                    dst, mode, sc = self.kaT[cc - 8], "rope", 1.0
                elif g < 8:
                    dst, mode, sc = self.qbT[cc - 24], "copy", 1.0
                elif g < 10:
                    dst, mode, sc = self.kbT[cc - 32], "copy", 1.0
                elif g == 12:
                    dst, mode, sc = self.qcT[cc - 48], "rope", 1.0
                elif g == 13:
                    dst, mode, sc = self.kcT[cc - 52], "rope", 0.125
                elif g < 18:
                    dst, mode, sc = self.cgT[cc - 64], "silu", 1.0
                else:
                    dst, mode, sc = self.gT[cc - 72], "sigmoid", 1.0
                cb = cbuf[ccount % 4]
                ck = ("cbuf", ccount % 4)
                ccount += 1
                for (c0, n, lt0) in fparts:
                    bk = ecount % 4
                    ecount += 1
                    for kc in range(16):
                        self.mm(self.bank(bk, n), wf[:, ci, kc, :], uT[:, kc, c0:c0 + n], kc == 0, kc == 15,
                                [wk, ("uT", kc)], [("ps", bk)])
                    if mode == "rope" and lt0 is not None:
                        r = rcount % 2
                        rcount += 1
                        rbk = 4 + r
                        t0 = lt0 * 128
                        self.act(tsb[r], self.bank(bk), AF.Copy, [("ps", bk)], [("tsb", r)], scale=sc)
                        self.mm(self.bank(rbk), pm, tsb[r], True, True, [("tsb", r), "pm"], [("ps", rbk)])
                        self.tt(ra[r], tsb[r], cosS[:, t0:t0 + 512], ALU.mult, [("tsb", r), "cos"], [("ra", r)])
                        self.tt(rb[r], self.bank(rbk), sinS[:, t0:t0 + 512], ALU.mult, [("ps", rbk), "sin"],
                                [("rb", r)])
                        self.tt(cb[:, c0:c0 + 512], ra[r], rb[r], ALU.add, [("ra", r), ("rb", r)], [ck])
                    else:
                        fn = {"rope": AF.Copy, "copy": AF.Copy, "silu": AF.Silu, "sigmoid": AF.Sigmoid}[mode]
                        self.act(cb[:, c0:c0 + n], self.bank(bk, n), fn, [("ps", bk)], [ck], scale=sc)
                self.dma("sp", dst, cb, [ck], [("fdst", cc)], ck)
        ar.pop()

    def attn_a(self, l, with_ctx):
        ar, S, W = self.ar, self.S, self.W[l]
        ar.push()
        q1 = [ar.alloc([T], BF16) for _ in range(2)]
        q2 = [ar.alloc([T], BF16) for _ in range(2)]
        kS = [ar.alloc([T], BF16) for _ in range(2)]
        vS = [ar.alloc([NT, 129], BF16) for _ in range(2)]
        E1 = [ar.alloc([512], BF16) for _ in range(3)]
        E2 = [ar.alloc([512], BF16) for _ in range(3)]
        cacc = [ar.alloc([3, 512], F32) for _ in range(2)]
        ident = ar.alloc([128], BF16)
        dnrow = ar.alloc([128], F32)
        sm = [ar.alloc([8], F32) for _ in range(2)]
        t1 = [ar.alloc([128], F32) for _ in range(2)]
        o_ = [ar.alloc([128], F32) for _ in range(2)]
        junkf = [ar.alloc([128], F32) for _ in range(2)]
        onb = [ar.alloc([128], BF16) for _ in range(2)]
        obuf = [ar.alloc([T], BF16) for _ in range(2)]
        self.dma("pool", ident, self.identM, [], ["ident"], "ld0")
        self.dma("sp", dnrow, W["dnrow"], [], ["dnrow"], "ld1")
        self.ts(dnrow, dnrow, 1.0 - (0.8 - 0.6 * math.exp(-0.3 * l)), ALU.mult, ["dnrow"], ["dnrow"])
        for i in range(2):
            S.add("dve", (lambda e, a=q1[i][64:128, :]: e.memset(a, 0.0)), writes=[("q1z", i)])
            S.add("dve", (lambda e, a=q2[i][0:64, :]: e.memset(a, 0.0)), writes=[("q2z", i)])
            S.add("dve", (lambda e, a=vS[i][:, :, 128:129]: e.memset(a, 1.0)), writes=[("vSo", i)])
        qblocks = [(c0, n, list(range(NT))) for (c0, n, _) in QBLK]
        if with_ctx:
            qblocks.append((CTX_COL0, 256, list(CTX_TILES)))
        slots1 = [(4, 0), (4, 129), (4, 258), (5, 0)]
        slots2 = [(5, 129), (5, 258), (6, 0), (6, 129)]
        scnt = 0
        ecnt = 0
        fcnt = 0
        tcnt = 0
        ps7b = self.bank(7).bitcast(BF16)
        tcnt_box = [0]

        def fin_gen(cb, c0, nq, ob, hb):
            ca = cacc[cb]
            ck = [("cacc", cb, 0), ("cacc", cb, 1), ("cacc", cb, 2)]
            for qt in range(nq):
                tb = tcnt_box[0] % 2
                tcnt_box[0] += 1
                b1, c1 = slots1[qt]
                b2, c2 = slots2[qt]
                A1 = ca[:, b1 - 4, c1:c1 + 129]
                A2 = ca[:, b2 - 4, c2:c2 + 129]
                smt = sm[tb]
                smk = ("sm", tb)
                self.recip(smt[:, 0:1], A1[:, 128:129], ck, [smk])
                self.recip(smt[:, 1:2], A2[:, 128:129], ck, [smk])
                self.tt(smt[:, 2:3], smt[:, 1:2], self.lamS[:, 0:1], ALU.mult, [smk, "lamS"], [smk])
                yield
                self.ts(t1[tb], A1[:, 0:128], smt[:, 0:1], ALU.mult, ck + [smk], [("t1", tb)])
                self.stt(o_[tb], A2[:, 0:128], smt[:, 2:3], t1[tb], ALU.mult, ALU.add, ck + [smk, ("t1", tb)],
                         [("o_", tb)])
                yield
                self.S.add("dve", (lambda e, o=junkf[tb], i=o_[tb], a=smt[:, 3:4]:
                                   e.scalar_tensor_tensor(out=o, in0=i, scalar=1.0, in1=i, op0=ALU.mult, op1=ALU.mult,
                                                          accum_out=a)),
                           [("o_", tb)], [("junk", tb), smk])
                yield
                self.act(smt[:, 4:5], smt[:, 3:4], AF.Ln, [smk, "eps"], [smk], scale=1.0 / 128,
                         bias=self.epsT[:, 0:1])
                self.act(smt[:, 5:6], smt[:, 4:5], AF.Exp, [smk], [smk], scale=-0.5)
                self.stt(onb[tb], o_[tb], smt[:, 5:6], dnrow, ALU.mult, ALU.mult, [("o_", tb), smk, "dnrow"],
                         [("onb", tb)])
                yield
                tv = ps7b[:, tb * 128:(tb + 1) * 128]
                self.S.add("pe", (lambda e, o=tv, i=onb[tb]: e.transpose(o, i, ident)), [("onb", tb), "ident"],
                           [("ps", 7)])
                yield
                qc = c0 + qt * 128
                self.S.add("dve", (lambda e, o=ob[:, qc:qc + 128], i=tv: e.tensor_copy(out=o, in_=i)),
                           [("ps", 7)], [("obuf", hb)])
                yield

        fin = iter(())
        modg = iter(())
        if l + 1 < self.nl and self.mod_overlap:
            mwb = [ar.alloc([4, 16, 128], BF16) for _ in range(3)]
            modg = self.mod_overlap_gen(l + 1, mwb)
        stepc = 0
        for h in range(8):
            hb = h % 2
            self.dma("sp", q1[hb][0:64, :], self.qaT[h][0:64, :], [], [("q1", hb)], ("q1", hb))
            self.dma("sp", q2[hb][64:128, :], self.qaT[h][64:128, :], [], [("q2", hb)], ("q2", hb))
            self.dma("sp", kS[hb], self.kaT[h], [], [("kS", hb)], ("kS", hb))
            self.dma("sp", vS[hb][:, :, 0:128], self.va[:, :, h * 128:(h + 1) * 128].rearrange("t p d -> p t d"),
                     [], [("vS", hb)], ("vS", hb))
            ob = obuf[hb]
            for (c0, n, ktl) in qblocks:
                nq = n // 128
                nk = len(ktl)
                pend = None
                first_in_bank = {}
                for idx in range(nk + 1):
                    cur = None
                    if idx < nk:
                        kt = ktl[idx]
                        sb = scnt % 2
                        scnt += 1
                        self.mm(self.bank(sb, n), kS[hb][:, kt * 128:(kt + 1) * 128], q1[hb][:, c0:c0 + n],
                                True, True, [("kS", hb), ("q1", hb), ("q1z", hb)], [("ps", sb)])
                        self.mm(self.bank(2 + sb, n), kS[hb][:, kt * 128:(kt + 1) * 128], q2[hb][:, c0:c0 + n],
                                True, True, [("kS", hb), ("q2", hb), ("q2z", hb)], [("ps", 2 + sb)])
                        eb = ecnt % 3
                        ecnt += 1
                        self.act(E1[eb][:, :n], self.bank(sb, n), AF.Exp, [("ps", sb)], [("E1", eb)], scale=0.125)
                        self.act(E2[eb][:, :n], self.bank(2 + sb, n), AF.Exp, [("ps", 2 + sb)], [("E2", eb)], scale=0.125)
                        cur = (kt, eb, idx)
                    if pend is not None:
                        kt_, eb_, i_ = pend
                        for (Eb, slots, ek) in ((E1[eb_], slots1, ("E1", eb_)), (E2[eb_], slots2, ("E2", eb_))):
                            for qt in range(nq):
                                bk, co = slots[qt]
                                st = (i_ == 0) and (bk not in first_in_bank)
                                if i_ == 0:
                                    first_in_bank[bk] = True
                                self.S.add("pe", (lambda e, o=self.bank(bk, 129, co), lt=Eb[:, qt * 128:(qt + 1) * 128],
                                                  r=vS[hb][:, kt_, :], st=st, sp=(i_ == nk - 1):
                                                  e.matmul(o, lt, r, start=st, stop=sp, skip_group_check=True)),
                                           [("vS", hb), ("vSo", hb), ek], [("ps", bk)])
                    pend = cur
                    next(fin, None)
                    next(fin, None)
                    stepc += 1
                    if stepc % 14 == 0:
                        next(modg, None)
                for _ in fin:
                    pass
                cb = fcnt % 2
                fcnt += 1
                ca = cacc[cb]
                for bi, bk in enumerate((4, 5, 6)):
                    if bi == 2 and nq < 3:
                        continue
                    self.S.add("dve", (lambda e, o=ca[:, bi, :], i=self.bank(bk): e.tensor_copy(out=o, in_=i)),
                               [("ps", bk)], [("cacc", cb, bi)])
                fin = fin_gen(cb, c0, nq, ob, hb)
            for _ in fin:
                pass
            fin = iter(())
            if with_ctx:
                self.dma("sp", self.oaT[h], ob, [("obuf", hb)], [("oaT", h)], ("obuf", hb))
            else:
                self.dma("sp", self.oaT[h][:, 0:512], ob[:, 0:512], [("obuf", hb)], [("oaT", h)], ("obuf", hb))
                self.dma("sp", self.oaT[h][:, 768:T], ob[:, 768:T], [("obuf", hb)], [("oaT", h)], ("obuf", hb))
        for _ in modg:
            pass
        ar.pop()

    def attn_b(self, l, with_ctx):
        ar, S, W = self.ar, self.S, self.W[l]
        ar.push()
        qS = [ar.alloc([T], BF16) for _ in range(2)]
        kS = [ar.alloc([T], BF16) for _ in range(2)]
        vS = [ar.alloc([NT, 128], BF16) for _ in range(2)]
        bias = [ar.alloc([25, 128], F32) for _ in range(2)]
        pre = [ar.alloc([5, 128], F32) for _ in range(2)]
        E = [ar.alloc([7, 128], BF16) for _ in range(2)]
        rr = [ar.alloc([256], F32) for _ in range(2)]
        rsc = [ar.alloc([256], F32) for _ in range(2)]
        obuf = [ar.alloc([T], BF16) for _ in range(2)]
        scale = 128 ** -0.5
        steps = []

        def loads(h):
            hb = h % 2
            self.dma("sp", qS[hb], self.qbT[h], [], [("qS", hb)], ("qS", hb))
            self.dma("sp", kS[hb], self.kbT[h], [], [("kS", hb)], ("kS", hb))
            self.dma("sp", vS[hb], self.vb[:, :, h * 128:(h + 1) * 128].rearrange("t p d -> p t d"),
                     [], [("vS", hb)], ("vS", hb))
            self.dma("sp", bias[hb], W["rpbx"][h], [], [("bias", hb)], ("bias", hb))

        def front(h, i, s2):
            hb = h % 2
            if i == 0:
                loads(h)
            qc0 = lat_T(i) * 128
            jl = nb_jlist(i)
            cls = nb_cls(i)
            nl_ = len(jl)
            tiles = [lat_T(j) for j in jl] + list(CTX_TILES)
            for sl, kt in enumerate(tiles):
                bk = s2 * 2 + (0 if sl < 4 else 1)
                cc = (sl if sl < 4 else sl - 4) * 128
                self.mm(self.bank(bk, 128, cc), kS[hb][:, kt * 128:(kt + 1) * 128], qS[hb][:, qc0:qc0 + 128],
                        True, True, [("kS", hb), ("qS", hb)], [("ps", bk)])
            b0 = self.bank(s2 * 2, 512).rearrange("p (a b) -> p a b", b=128)
            b1 = self.bank(s2 * 2 + 1, 512).rearrange("p (a b) -> p a b", b=128)
            n0 = min(nl_, 4)
            self.stt(pre[s2][:, 0:n0, :], b0[:, 0:n0, :], scale, bias[hb][:, cls * 5:cls * 5 + n0, :], ALU.mult, ALU.add,
                     [("ps", s2 * 2), ("bias", hb)], [("pre", s2)])
            if nl_ > 4:
                self.stt(pre[s2][:, 4:5, :], b1[:, 0:1, :], scale, bias[hb][:, cls * 5 + 4:cls * 5 + 5, :], ALU.mult,
                         ALU.add, [("ps", s2 * 2 + 1), ("bias", hb)], [("pre", s2)])
            self.act(E[s2][:, 0:nl_, :], pre[s2][:, 0:nl_, :], AF.Exp, [("pre", s2)], [("E", s2)])
            if nl_ <= 4:
                for ci in range(2):
                    sl = nl_ + ci
                    src = b0[:, sl:sl + 1, :] if sl < 4 else b1[:, sl - 4:sl - 3, :]
                    bkk = s2 * 2 + (0 if sl < 4 else 1)
                    self.act(E[s2][:, sl:sl + 1, :], src, AF.Exp, [("ps", bkk)], [("E", s2)], scale=scale)
            else:
                self.act(E[s2][:, 5:7, :], b1[:, 1:3, :], AF.Exp, [("ps", s2 * 2 + 1)], [("E", s2)], scale=scale)

        def back(h, i, s2):
            hb = h % 2
            ob = obuf[hb]
            qc0 = lat_T(i) * 128
            tiles = [lat_T(j) for j in nb_jlist(i)] + list(CTX_TILES)
            ns = len(tiles)
            ob_ = 4 + s2
            db_ = 6 + s2
            for sl, kt in enumerate(tiles):
                self.mm(self.bank(ob_, 128), vS[hb][:, kt, :], E[s2][:, sl, :], sl == 0, sl == ns - 1,
                        [("vS", hb), ("E", s2)], [("ps", ob_)])
            for sl, kt in enumerate(tiles):
                self.mm(self.bank(db_, 128), self.ones, E[s2][:, sl, :], sl == 0, sl == ns - 1,
                        ["ones", ("E", s2)], [("ps", db_)])
            self.rcp(rr[s2][:, :128], self.bank(db_, 128), rsc[s2][:, :128], [("ps", db_)], [("rr", s2)])
            self.tt(ob[:, qc0:qc0 + 128], self.bank(ob_, 128), rr[s2][:, :128], ALU.mult, [("ps", ob_), ("rr", s2)],
                    [("obuf", hb)])
            if i == 15 and not with_ctx:
                self.dma("sp", self.obT[h][:, 0:512], ob[:, 0:512], [("obuf", hb)], [("obT", h)], ("obuf", hb))
                self.dma("sp", self.obT[h][:, 768:T], ob[:, 768:T], [("obuf", hb)], [("obT", h)], ("obuf", hb))

        def front_c(h, s2):
            hb = h % 2
            n = 256
            for ci, kt in enumerate(CTX_TILES):
                self.mm(self.bank(s2 * 2 + ci, n), kS[hb][:, kt * 128:(kt + 1) * 128], qS[hb][:, CTX_COL0:CTX_COL0 + n],
                        True, True, [("kS", hb), ("qS", hb)], [("ps", s2 * 2 + ci)])
            Ev = E[s2].rearrange("p a b -> p (a b)")
            for ci in range(2):
                self.act(Ev[:, ci * 256:(ci + 1) * 256], self.bank(s2 * 2 + ci, n), AF.Exp, [("ps", s2 * 2 + ci)],
                         [("E", s2)], scale=scale)

        def back_c(h, s2):
            hb = h % 2
            ob = obuf[hb]
            n = 256
            Ev = E[s2].rearrange("p a b -> p (a b)")
            ob_ = 4 + s2
            db_ = 6 + s2
            for ci, kt in enumerate(CTX_TILES):
                self.mm(self.bank(ob_, n), vS[hb][:, kt, :], Ev[:, ci * 256:(ci + 1) * 256], ci == 0, ci == 1,
                        [("vS", hb), ("E", s2)], [("ps", ob_)])
            for ci, kt in enumerate(CTX_TILES):
                self.mm(self.bank(db_, n), self.ones, Ev[:, ci * 256:(ci + 1) * 256], ci == 0, ci == 1,
                        ["ones", ("E", s2)], [("ps", db_)])
            self.rcp(rr[s2][:, :n], self.bank(db_, n), rsc[s2][:, :n], [("ps", db_)], [("rr", s2)])
            self.tt(ob[:, CTX_COL0:CTX_COL0 + n], self.bank(ob_, n), rr[s2][:, :n], ALU.mult,
                    [("ps", ob_), ("rr", s2)], [("obuf", hb)])
            self.dma("sp", self.obT[h], ob, [("obuf", hb)], [("obT", h)], ("obuf", hb))

        cnt = 0
        for h in range(8):
            for i in range(16):
                s2 = cnt % 2
                cnt += 1
                steps.append((lambda h=h, i=i, s2=s2: front(h, i, s2), lambda h=h, i=i, s2=s2: back(h, i, s2)))
            if with_ctx:
                s2 = cnt % 2
                cnt += 1
                steps.append((lambda h=h, s2=s2: front_c(h, s2), lambda h=h, s2=s2: back_c(h, s2)))
        prev = None
        for fr, bk in steps:
            fr()
            if prev is not None:
                prev()
            prev = bk
        prev()
        ar.pop()

    def retention(self, l, with_ctx):
        ar, S = self.ar, self.S
        ar.push()
        qP = [[ar.alloc([T], BF16) for _ in range(2)] for _ in range(2)]
        kS = [ar.alloc([T], BF16) for _ in range(2)]
        vS = [ar.alloc([NT, 256], BF16) for _ in range(2)]
        for i in range(2):
            S.add("dve", (lambda e, a=qP[0][i][64:128, :]: e.memset(a, 0.0)), writes=[("qz", 0, i)])
            S.add("dve", (lambda e, a=qP[1][i][0:64, :]: e.memset(a, 0.0)), writes=[("qz", 1, i)])
        cg = [ar.alloc([T], BF16) for _ in range(2)]
        rz = ar.alloc([10, 512], F32)
        Dm = [ar.alloc([6, 512], F32) for _ in range(2)]
        dtmp = ar.alloc([512], F32)
        dctx = [ar.alloc([512], F32) for _ in range(2)]
        dctx2 = [ar.alloc([512], F32) for _ in range(2)]
        ssb = [ar.alloc([512], F32) for _ in range(2)]
        rs2 = ar.alloc([512], F32)
        rsc = ar.alloc([512], F32)
        att = [ar.alloc([512], BF16) for _ in range(4)]
        osb = ar.alloc([512], F32)
        osq = ar.alloc([512], BF16)
        rs = ar.alloc([512], F32)
        ybuf = [ar.alloc([T], BF16) for _ in range(2)]
        self.dma("sp", rz, self.retz, [], ["rz"], "ld0")
        qblocks = [(c0, n, j0, False) for (c0, n, j0) in QBLK]
        if with_ctx:
            qblocks.append((CTX_COL0, 256, 0, True))
        scnt = 0
        acnt = 0
        hcnt = 0
        osbs = [ar.alloc([512], F32) for _ in range(2)]
        osqs = [ar.alloc([512], BF16) for _ in range(2)]
        rfc = [0]

        def ret_fin(fb, n, hh, yb, c0, hb):
            yield
            yield
            self.mm(self.bank(6 + hh, n), self.ones, osqs[fb][:, :n], True, True, [("osq", fb), "ones"], [("ps", 6 + hh)])
            yield
            self.act(rs[:, :n], self.bank(6 + hh, n), AF.Ln, [("ps", 6 + hh), "eps"], ["rs"], scale=1.0 / 128,
                     bias=self.epsT[:, 0:1])
            self.act(rs2[:, :n], rs[:, :n], AF.Exp, ["rs"], ["rs2"], scale=-0.5)
            yield
            yield
            self.tt(osbs[fb][:, :n], osbs[fb][:, :n], rs2[:, :n], ALU.mult, [("osb", fb), "rs2"], [("osb", fb)])
            yield
            self.tt(yb[:, c0:c0 + n], osbs[fb][:, :n], cg[hb][:, c0:c0 + n], ALU.mult, [("osb", fb), ("cg", hb)],
                    [("ybuf", hb)])
            yield

        rfin = iter(())
        for m in range(4):
            mb = m % 2
            self.dma("sp", qP[0][mb][0:64, :], self.qcT[m][0:64, :], [], [("qS", 0, mb)], ("qS", 0, mb))
            self.dma("sp", qP[1][mb][64:128, :], self.qcT[m][64:128, :], [], [("qS", 1, mb)], ("qS", 1, mb))
            self.dma("sp", kS[mb], self.kcT[m], [], [("kS", mb)], ("kS", mb))
            self.dma("sp", vS[mb], self.vc[:, :, m * 256:(m + 1) * 256].rearrange("t p d -> p t d"),
                     [], [("vS", mb)], ("vS", mb))
            for hh in range(2):
                h = 2 * m + hh
                pb0 = hh * 64
                hb = hcnt % 2
                hcnt += 1
                self.dma("sp", cg[hb], self.cgT[h], [], [("cg", hb)], ("cg", hb))
                lgf = self.lgS[:, h:h + 1]
                lgb = self.lgS[:, 8 + h:9 + h]
                dm = Dm[hb]
                dk = ("Dm", hb)
                self.act(dm[:, 0, :], rz[:, 0, :], AF.Exp, ["rz", "lgS"], [dk], scale=lgf)
                self.act(dm[:, 1, :], rz[:, 1, :], AF.Exp, ["rz", "lgS"], [dk], scale=lgb)
                for a in range(4):
                    self.act(dm[:, 2 + a, :], rz[:, 2 + a, :], AF.Exp, ["rz", "lgS"], [dk], scale=lgf)
                    self.act(dtmp, rz[:, 6 + a, :], AF.Exp, ["rz", "lgS"], ["dtmp"], scale=lgb)
                    self.tt(dm[:, 2 + a, :], dm[:, 2 + a, :], dtmp, ALU.add, [dk, "dtmp"], [dk])
                yb = ybuf[hb]
                for (c0, n, j0, isctx) in qblocks:
                    if isctx:
                        ktl = [("cc", c) for c in range(2)]
                    else:
                        ktl = [("l", j) for j in range(16)] + [("c", c) for c in range(2)]
                    nk = len(ktl)
                    ob_ = 4 + (scnt // 100000) % 1
                    obank = 4 + hh
                    pendq = []
                    LAG = 3
                    for idx in range(nk + LAG):
                        cur = None
                        if idx < nk:
                            kind, jk = ktl[idx]
                            kt = lat_T(jk) if kind == "l" else CTX_TILES[jk]
                            sb = scnt % 4
                            scnt += 1
                            self.mm(self.bank(sb, n), kS[mb][:, kt * 128:(kt + 1) * 128],
                                    qP[hh][mb][:, c0:c0 + n], True, True,
                                    [("kS", mb), ("qS", hh, mb), ("qz", hh, mb)], [("ps", sb)])
                            ab = acnt % 4
                            acnt += 1
                            ak = ("att", ab)
                            sps = self.bank(sb, n)
                            alt = (kind == "l") and (acnt % 2 == 1)
                            if alt:
                                sf = ssb[(acnt // 2) % 2]
                                sfk = ("ssb", (acnt // 2) % 2)
                                if jk < j0:
                                    self.act(sf[:, :n], sps, AF.Copy, [("ps", sb), "cft"], [sfk],
                                             scale=self.cft[:, h, j0 - jk:j0 - jk + 1])
                                    dsel = dm[:, 0, :n]
                                elif jk > j0 + 3:
                                    self.act(sf[:, :n], sps, AF.Copy, [("ps", sb), "cft"], [sfk],
                                             scale=self.cft[:, 8 + h, jk - j0 - 4:jk - j0 - 3])
                                    dsel = dm[:, 1, :n]
                                else:
                                    self.act(sf[:, :n], sps, AF.Copy, [("ps", sb)], [sfk])
                                    dsel = dm[:, 2 + jk - j0, :n]
                                self.tt(att[ab][:, :n], sf[:, :n], dsel, ALU.mult, [sfk, dk], [ak], eng="pool")
                            elif kind == "l":
                                if jk < j0:
                                    self.stt(att[ab][:, :n], sps, self.cft[:, h, j0 - jk:j0 - jk + 1], dm[:, 0, :n],
                                             ALU.mult, ALU.mult, [("ps", sb), dk, "cft"], [ak])
                                elif jk > j0 + 3:
                                    self.stt(att[ab][:, :n], sps, self.cft[:, 8 + h, jk - j0 - 4:jk - j0 - 3], dm[:, 1, :n],
                                             ALU.mult, ALU.mult, [("ps", sb), dk, "cft"], [ak])
                                else:
                                    self.tt(att[ab][:, :n], sps, dm[:, 2 + jk - j0, :n], ALU.mult, [("ps", sb), dk], [ak])
                            elif kind == "c":
                                dc = dctx[acnt % 2]
                                dck = ("dctx", acnt % 2)
                                i1 = 2 + j0 - jk
                                i2 = 12 + jk - j0
                                dc2 = dctx2[acnt % 2]
                                self.act(dc, dm[:, 0, :], AF.Copy, [dk, "cft"], [dck], scale=self.cft[:, h, i1:i1 + 1])
                                self.act(dc2, dm[:, 1, :], AF.Copy, [dk, "cft"], [("dctx2", acnt % 2)],
                                         scale=self.cft[:, 8 + h, i2:i2 + 1])
                                self.tt(dc, dc, dc2, ALU.add, [dck, ("dctx2", acnt % 2)], [dck], eng="pool")
                                self.tt(att[ab][:, :n], sps, dc[:, :n], ALU.mult, [("ps", sb), dck], [ak])
                            else:
                                self.tt(att[ab][:, :n], sps, dm[:, 2 + jk, :n], ALU.mult, [("ps", sb), dk], [ak])
                            cur = (kt, ab, idx)
                        next(rfin, None)
                        if cur is not None:
                            pendq.append(cur)
                        if idx >= LAG and pendq:
                            kt_, ab_, i_ = pendq.pop(0)
                            self.mm(self.bank(obank, n), vS[mb][:, kt_, hh * 128:(hh + 1) * 128], att[ab_][:, :n],
                                    i_ == 0, i_ == nk - 1, [("vS", mb), ("att", ab_)], [("ps", obank)])
                    for _ in rfin:
                        pass
                    fb = rfc[0] % 2
                    rfc[0] += 1
                    self.act(osbs[fb][:, :n], self.bank(obank, n), AF.Copy, [("ps", obank), "dnS"], [("osb", fb)],
                             scale=self.dnS[:, 1:2])
                    self.act(osqs[fb][:, :n], self.bank(obank, n), AF.Square, [("ps", obank)], [("osq", fb)])
                    rfin = ret_fin(fb, n, hh, yb, c0, hb)
                for _ in rfin:
                    pass
                rfin = iter(())
                if with_ctx:
                    self.dma("sp", self.yrT[h], yb, [("ybuf", hb)], [("yrT", h)], ("ybuf", hb))
                else:
                    self.dma("sp", self.yrT[h][:, 0:512], yb[:, 0:512], [("ybuf", hb)], [("yrT", h)], ("ybuf", hb))
                    self.dma("sp", self.yrT[h][:, 768:T], yb[:, 768:T], [("ybuf", hb)], [("yrT", h)], ("ybuf", hb))
        ar.pop()

    def merge(self, l, skip_ctx):
        ar, S, W = self.ar, self.S, self.W[l]
        ar.push()
        obr = [ar.alloc([8, 768], BF16) for _ in range(3)]
        mT = ar.alloc([16, 768], BF16)
        wbr = [ar.alloc([3, 8, 128], BF16) for _ in range(3)]
        wo = [ar.alloc([16, 128], BF16) for _ in range(2)]
        gb = [ar.alloc([3, 768], BF16) for _ in range(2)]
        m1 = [ar.alloc([512], F32) for _ in range(2)]
        m2 = [ar.alloc([512], F32) for _ in range(2)]
        m3 = [ar.alloc([512], F32) for _ in range(2)]
        pb = self.alloc_post_bufs()
        srcs = (self.oaT, self.obT, self.yrT)
        wc = 0
        postg = iter(())
        for b in range(3):
            parts = [(c0, n, 1 if part_is_ctx(b, p) else 0, not (skip_ctx and part_is_ctx(b, p)))
                     for p, (c0, n) in enumerate(PARTS[b])]
            bc0 = b * 768
            slot = b % 2
            for br in range(3):
                self.dma("sp", obr[br], srcs[br][:, :, bc0:bc0 + 768].rearrange("h p t -> p h t"), [], [("obr", br)],
                         ("obr", br))
            for oc in range(16):
                next(postg, None)
                next(postg, None)
                wb_ = wbr[wc % 3]
                wk = ("wbr", wc % 3)
                wc += 1
                self.dma("pool", wb_, W["wbr"][oc], [], [wk], wk)
                g_ = gb[oc % 2]
                gk = ("gb", oc % 2)
                for br in range(3):
                    self.dma("sp", g_[:, br, :], self.gT[br * 16 + oc][:, bc0:bc0 + 768], [], [(gk, br)], (gk, br))
                for pi, (c0, n, j, on) in enumerate(parts):
                    if not on:
                        continue
                    u0 = 0 if pi == 0 else 512
                    st = (oc * 2 + pi) % 2
                    for br in range(3):
                        bk = st * 3 + br
                        for kc in range(8):
                            self.mm(self.bank(bk, n), wb_[:, br, kc, :], obr[br][:, kc, u0:u0 + n], kc == 0, kc == 7,
                                    [wk, ("obr", br)], [("ps", bk)])
                    a_, b_, c_ = m1[st], m2[st], m3[st]
                    self.tt(a_[:, :n], self.bank(st * 3, n), g_[:, 0, u0:u0 + n], ALU.mult, [("ps", st * 3), (gk, 0)],
                            [("m1", st)])
                    self.tt(b_[:, :n], self.bank(st * 3 + 1, n), g_[:, 1, u0:u0 + n], ALU.mult,
                            [("ps", st * 3 + 1), (gk, 1)], [("m2", st)])
                    self.tt(c_[:, :n], self.bank(st * 3 + 2, n), g_[:, 2, u0:u0 + n], ALU.mult,
                            [("ps", st * 3 + 2), (gk, 2)], [("m3", st)])
                    self.tt(a_[:, :n], a_[:, :n], b_[:, :n], ALU.add, [("m1", st), ("m2", st)], [("m1", st)])
                    self.tt(mT[:, oc, u0:u0 + n], a_[:, :n], c_[:, :n], ALU.add, [("m1", st), ("m3", st)], [("mT", oc)])
            for oc2 in range(16):
                w_ = wo[oc2 % 2]
                wk = ("wo", oc2 % 2)
                self.dma("pool", w_, W["wo"][oc2], [], [wk], wk)
                st = oc2 % 2
                ybanks = (st * 2, st * 2 + 1)
                for pi, (c0, n, j, on) in enumerate(parts):
                    if not on:
                        continue
                    u0 = 0 if pi == 0 else 512
                    for oc in range(16):
                        self.mm(self.bank(ybanks[pi], n), w_[:, oc, :], mT[:, oc, u0:u0 + n], oc == 0, oc == 15,
                                [wk, ("mT", oc)], [("ps", ybanks[pi])])
                self.post_evac(pb, b, oc2, parts, ybanks, (6, 7), slot)
            for _ in postg:
                pass
            postg = self.post_gen(pb, l, 1, b, parts, (6, 7), slot, self.hT, self.hT, False)
        for _ in postg:
            pass
        self.rstd_valid = True
        ar.pop()


def _rope_tables():
    half = 16
    inv = (10000.0 ** (-np.arange(half, dtype=np.float32) / half)).astype(np.float32)
    pos = np.arange(2048)
    prow = (pos // 64).astype(np.float32)
    pcol = (pos % 64).astype(np.float32)
    cosT = np.zeros((128, 2048), np.float32)
    sinT = np.zeros((128, 2048), np.float32)
    perm = np.zeros((128, 128), np.float32)
    for p in range(128):
        dd = p % 64
        axis = dd // 32
        jj = dd % 32
        i = jj % 16
        first = jj < 16
        ang = (prow if axis == 0 else pcol) * inv[i]
        cosT[p] = np.cos(ang.astype(np.float32))
        s = np.sin(ang.astype(np.float32))
        sinT[p] = -s if first else s
        partner = p + 16 if first else p - 16
        perm[partner, p] = 1.0
    return cosT, sinT, perm


def _ret_tables():
    kl = np.arange(128, dtype=np.float64)[:, None]
    x = np.arange(512, dtype=np.float64)[None, :]
    tabs = np.zeros((128, 10, 512), np.float32)
    tabs[:, 0] = x - kl
    tabs[:, 1] = 512 - x + kl
    for a in range(4):
        z = x - 128 * a - kl
        tabs[:, 2 + a] = np.where(z >= 0, z, BIGZ)
        tabs[:, 6 + a] = np.where(z <= 0, -z, BIGZ)
    rows = np.tile((128.0 * np.arange(20, dtype=np.float32))[None, :], (128, 1)).astype(np.float32)
    return tabs, rows


def _rpb_expand(rpb):
    WIN_R, WIN_C, GW, ROWS = 8, 16, 64, 32
    out = np.zeros((8, 128, 25, 128), np.float32)
    seen = {}
    for i in range(16):
        cls = nb_cls(i)
        jl = nb_jlist(i)
        q = np.arange(128)
        qr = 2 * i + q // 64
        qc = q % 64
        rs = np.clip(qr - 4, 0, ROWS - WIN_R)
        cstart = np.clip(qc - WIN_C // 2, 0, GW - WIN_C)
        for slot, j in enumerate(jl):
            k = np.arange(128)
            kr = 2 * j + k // 64
            kc = k % 64
            row_ok = (kr[:, None] >= rs[None, :]) & (kr[:, None] < rs[None, :] + WIN_R)
            col_ok = (kc[:, None] >= cstart[None, :]) & (kc[:, None] < cstart[None, :] + WIN_C)
            dr = np.clip(kr[:, None] - qr[None, :] + WIN_R - 1, 0, 2 * WIN_R - 2)
            dc = np.clip(kc[:, None] - qc[None, :] + WIN_C - 1, 0, 2 * WIN_C - 2)
            ok = row_ok & col_ok
            key = (cls, slot)
            sig = (ok.tobytes(), dr.tobytes(), dc.tobytes(), j - i)
            if key in seen:
                assert seen[key] == sig, f"class structure mismatch {i} {slot}"
                continue
            seen[key] = sig
            vals = rpb[:, dr, dc]
            out[:, :, cls * 5 + slot, :] = np.where(ok[None], vals, np.float32(-30000.0))
    return out


def _fm(v):
    return np.ascontiguousarray(v.reshape(16, 128).T)


def host_weights(inp, L=2):
    m = {}
    cosT, sinT, perm = _rope_tables()
    retz, rows = _ret_tables()
    m["cosT"], m["sinT"], m["permM"], m["retz"], m["rows128"] = cosT, sinT, perm, retz, rows
    m["identM"] = np.eye(128, dtype=np.float32)
    for l in range(L):
        wm = inp["w_mod"][l].reshape(16, 128, 36, 4, 128)
        m[f"wmod{l}"] = np.ascontiguousarray(wm.transpose(2, 1, 3, 0, 4))
        m[f"bmod{l}"] = np.ascontiguousarray(inp["b_mod"][l].reshape(144, 128).T)
        m[f"pre{l}"] = np.ascontiguousarray(inp["pre_norm"][l].reshape(3, 16, 128).transpose(2, 0, 1))
        m[f"post{l}"] = np.ascontiguousarray(inp["post_norm"][l].reshape(3, 16, 128).transpose(2, 0, 1))
        for i in range(2):
            wi = inp["ffn_w_in"][l, i].reshape(16, 128, 2, 44, 128)
            m[f"fwin{l}_{i}"] = np.ascontiguousarray(wi.transpose(3, 1, 2, 0, 4))
            wo = inp["ffn_w_out"][l, i].reshape(44, 128, 16, 128)
            m[f"fwout{l}_{i}"] = np.ascontiguousarray(wo.transpose(2, 1, 0, 3))
        w = inp["w_in"][l].reshape(16, 128, 30, 4, 128)
        fm = w.transpose(2, 1, 3, 0, 4).reshape(30, 128, 8192)
        tm = w.transpose(2, 1, 0, 3, 4).reshape(30, 128, 8192)
        win = np.ascontiguousarray(fm)
        for g in (4, 5, 10, 11, 14, 15):
            win[g] = tm[g]
        m[f"win{l}"] = win
        wb = inp["w_branch"][l].reshape(3, 8, 128, 16, 128)
        m[f"wbr{l}"] = np.ascontiguousarray(wb.transpose(3, 2, 0, 1, 4))
        wo_ = inp["w_out"][l].reshape(16, 128, 16, 128)
        m[f"wo{l}"] = np.ascontiguousarray(wo_.transpose(2, 1, 0, 3))
        m[f"dlam{l}"] = np.ascontiguousarray(np.tile(inp["diff_lambda"][l].reshape(1, 256), (128, 1)))
        m[f"dnorm{l}"] = np.ascontiguousarray(inp["diff_norm"][l].reshape(128, 1))
        m[f"dnrow{l}"] = np.ascontiguousarray(np.tile(inp["diff_norm"][l].reshape(1, 128), (128, 1)))
        m[f"rnorm{l}"] = np.ascontiguousarray(inp["ret_norm"][l].reshape(128, 1))
        m[f"rdecay{l}"] = np.ascontiguousarray(np.tile(inp["ret_decay"][l].reshape(1, 16), (128, 1)))
        m[f"rpbx{l}"] = _rpb_expand(inp["na_rpb"][l])
    return m


def host_core_inputs(inp, b):
    x = inp["x"][b]
    ctx = inp["ctx"][b]
    tok = np.concatenate([x[:512], ctx, x[512:]], axis=0)
    xT = np.ascontiguousarray(tok.T.reshape(16, 128, T))
    cv = np.stack([_fm(inp["c"][b]), _fm(inp["c_ctx"])], axis=-1)
    return {"xT": xT, "cvec": np.ascontiguousarray(cv)}


_CACHE = {}


def kernel(**inputs):
    inp = {k: np.asarray(v, dtype=np.float32) for k, v in inputs.items()}
    if "prog" not in _CACHE:
        p = Prog(2)
        p.build()
        _CACHE["prog"] = p
    p = _CACHE["prog"]
    wm = host_weights(inp)
    in_maps = []
    for b in range(8):
        d = dict(wm)
        d.update(host_core_inputs(inp, b))
        in_maps.append(d)
    res = run_bass_kernel_spmd(p.nc, in_maps, core_ids=list(range(8)))
    out = np.empty((8, 2048, 2048), np.float32)
    for b in range(8):
        oT = res.results[b]["outT"]
        out[b] = oT.reshape(2048, 2048).T
    return out
```

```python
import math
import numpy as np
import concourse.bass as bass
import concourse.mybir as mybir
from concourse.bass_utils import run_bass_kernel_spmd

F32 = mybir.dt.float32
BF16 = mybir.dt.bfloat16
AF = mybir.ActivationFunctionType
ALU = mybir.AluOpType
AX = mybir.AxisListType

ENGS = ("pe", "act", "dve", "pool", "sp")
EPS = 1e-6
D = 2048
T = 2304
NT = 18
BIGZ = 1.0e6


class Op:
    __slots__ = ("eng", "fn", "deps", "dma", "signal", "sigval", "gidx", "dmaval")

    def __init__(self, eng, fn, dma):
        self.eng = eng
        self.fn = fn
        self.dma = dma
        self.deps = None
        self.signal = False
        self.sigval = 0
        self.dmaval = 0
        self.gidx = 0


class Sched:
    def __init__(self, nc, same_engine_sync=True):
        self.nc = nc
        self.ops = []
        self.last_w = {}
        self.readers = {}
        self.last_dma = {}
        self.dma_cnt = {}
        self.same_engine_sync = same_engine_sync
        self.last_op_eng = {e: None for e in ENGS}

    def add(self, eng, fn, reads=(), writes=(), dma=None):
        op = Op(eng, fn, dma)
        op.gidx = len(self.ops)
        deps = {}
        psr = [b for b in reads if isinstance(b, tuple) and b[0] == "ps"]
        if psr:
            writes = list(writes) + [b for b in psr if b not in writes]

        def adddep(d):
            if d is None or d is op:
                return
            if d.dma is not None:
                k = ("dma", d.dma)
                if k not in deps or deps[k].dmaval < d.dmaval:
                    deps[k] = d
            else:
                k = d.eng
                if k not in deps or deps[k].gidx < d.gidx:
                    deps[k] = d

        for b in reads:
            adddep(self.last_w.get(b))
        for b in writes:
            adddep(self.last_w.get(b))
            for r in self.readers.get(b, ()):
                adddep(r)
        for b in reads:
            self.readers.setdefault(b, []).append(op)
        for b in writes:
            self.last_w[b] = op
            self.readers[b] = []
        if dma is not None:
            adddep(self.last_dma.get(dma))
            self.last_dma[dma] = op
            self.dma_cnt[dma] = self.dma_cnt.get(dma, 0) + 16
            op.dmaval = self.dma_cnt[dma]
        op.deps = list(deps.values())
        self.ops.append(op)
        self.last_op_eng[eng] = op
        return op

    def barrier(self):
        lasts = [o for o in self.last_op_eng.values() if o is not None]
        dmas = list(self.last_dma.values())
        for e in ENGS:
            op = Op(e, None, None)
            op.gidx = len(self.ops)
            op.deps = list(lasts + dmas)
            self.ops.append(op)
        self.last_w = {}
        self.readers = {}

    def emit(self):
        nc = self.nc
        for op in self.ops:
            for d in op.deps:
                if d.dma is None:
                    if d.eng == op.eng and (d.eng == "pe" or not self.same_engine_sync):
                        continue
                    d.signal = True
        cnt = {e: 0 for e in ENGS}
        for op in self.ops:
            if op.signal:
                cnt[op.eng] += 1
                op.sigval = cnt[op.eng]
        per_eng = {e: [o for o in self.ops if o.eng == e] for e in ENGS}
        sems_eng = {e: nc.alloc_semaphore(name=f"se_{e}") for e in ENGS}
        dkeys = []
        for o in self.ops:
            if o.dma is not None and o.dma not in dkeys:
                dkeys.append(o.dma)
        sems_dma = {k: nc.alloc_semaphore(name=f"sd_{i}") for i, k in enumerate(dkeys)}
        same = self.same_engine_sync
        stats = {e: [0, 0] for e in ENGS}

        def run(eng_name, eng):
            waited = {}
            for op in per_eng[eng_name]:
                for d in op.deps:
                    if d.dma is not None:
                        sem = sems_dma[d.dma]
                        val = d.dmaval
                        key = ("dma", d.dma)
                    else:
                        if d.eng == eng_name and (eng_name == "pe" or not same):
                            continue
                        sem = sems_eng[d.eng]
                        val = d.sigval
                        key = d.eng
                    if waited.get(key, 0) >= val:
                        continue
                    waited[key] = val
                    eng.wait_ge(sem, val)
                    stats[eng_name][1] += 1
                if op.fn is None:
                    continue
                inst = op.fn(eng)
                stats[eng_name][0] += 1
                if op.dma is not None:
                    inst.then_inc(sems_dma[op.dma], 16)
                elif op.signal:
                    inst.then_inc(sems_eng[eng_name], 1)

        with nc.Block() as block:
            @block.tensor
            def _(e):
                run("pe", e)

            @block.scalar
            def _(e):
                run("act", e)

            @block.vector
            def _(e):
                run("dve", e)

            @block.gpsimd
            def _(e):
                run("pool", e)

            @block.sync
            def _(e):
                run("sp", e)
        return stats


class Arena:
    def __init__(self, tensor, nwords):
        self.t = tensor
        self.n = nwords
        self.off = 0
        self.marks = []

    def alloc(self, shape, dtype):
        nel = int(np.prod(shape))
        esz = 4 if dtype == F32 else 2
        words = (nel * esz + 3) // 4
        words = (words + 7) // 8 * 8
        assert self.off + words <= self.n, f"SBUF arena overflow {self.off}+{words}>{self.n}"
        ap = self.t[:, self.off:self.off + words]
        self.off += words
        if dtype != F32:
            ap = ap.bitcast(dtype)
        ap = ap[:, 0:nel]
        if len(shape) == 2:
            ap = ap.rearrange("p (a b) -> p a b", b=shape[1])
        elif len(shape) == 3:
            ap = ap.rearrange("p (a b c) -> p a b c", b=shape[1], c=shape[2])
        elif len(shape) == 4:
            ap = ap.rearrange("p (a b c d) -> p a b c d", b=shape[1], c=shape[2], d=shape[3])
        return ap

    def push(self):
        self.marks.append(self.off)

    def pop(self):
        self.off = self.marks.pop()


def lat_T(j):
    return j if j < 4 else j + 2


CTX_TILES = (4, 5)
QBLK = [(0, 512, 0), (768, 512, 4), (1280, 512, 8), (1792, 512, 12)]
CTX_COL0 = 512
PARTS = [[(b * 768, 512), (b * 768 + 512, 256)] for b in range(3)]


def part_is_ctx(b, p):
    return b == 0 and p == 1


def lat_col(c):
    return c if c < 512 else c - 256


def nb_jlist(i):
    s = set()
    for r in (2 * i, 2 * i + 1):
        rs = min(max(r - 4, 0), 24)
        for kr in range(rs, rs + 8):
            s.add(kr // 2)
    return sorted(s)


def nb_cls(i):
    return {0: 0, 1: 1, 14: 3, 15: 4}.get(i, 2)


class Prog:
    def __init__(self, nlayers=2, dbg=(), phases=None, same_engine_sync=True):
        self.nl = nlayers
        self.dbg = set(dbg)
        self.phases = phases
        nc = bass.Bass("TRN2", target_bir_lowering=False)
        self.nc = nc
        self.S = Sched(nc, same_engine_sync=same_engine_sync)
        self.in_names = []
        self.fast_rcp = False
        self.mod_overlap = True
        self.fuse_stats = True

    def din(self, name, shape, dt=F32):
        self.in_names.append(name)
        return self.nc.dram_tensor(name, list(shape), dt, kind="ExternalInput").ap()

    def dscr(self, name, shape, dt):
        kind = "ExternalOutput" if name in self.dbg else "Internal"
        return self.nc.dram_tensor(name, list(shape), dt, kind=kind).ap()

    def dma(self, eng, out, in_, reads, writes, sem):
        return self.S.add(eng, lambda e: e.dma_start(out=out, in_=in_), reads, writes, dma=sem)

    def mm(self, out, lhsT, rhs, start, stop, reads, writes):
        return self.S.add("pe", lambda e: e.matmul(out, lhsT, rhs, start=start, stop=stop), reads, writes)

    def act(self, out, in_, func, reads, writes, scale=None, bias=None):
        kw = {}
        if scale is not None:
            kw["scale"] = scale
        if bias is not None:
            kw["bias"] = bias
        return self.S.add("act", lambda e: e.activation(out=out, in_=in_, func=func, **kw), reads, writes)

    def tt(self, out, in0, in1, op, reads, writes, eng="dve"):
        return self.S.add(eng, lambda e: e.tensor_tensor(out=out, in0=in0, in1=in1, op=op), reads, writes)

    def stt(self, out, in0, scalar, in1, op0, op1, reads, writes):
        return self.S.add("dve", lambda e: e.scalar_tensor_tensor(out=out, in0=in0, scalar=scalar, in1=in1,
                                                                   op0=op0, op1=op1), reads, writes)

    def ts(self, out, in0, s1, op0, reads, writes, s2=None, op1=None, eng="dve"):
        if op1 is None:
            return self.S.add(eng, lambda e: e.tensor_scalar(out=out, in0=in0, scalar1=s1, scalar2=None, op0=op0),
                              reads, writes)
        return self.S.add(eng, lambda e: e.tensor_scalar(out=out, in0=in0, scalar1=s1, scalar2=s2, op0=op0, op1=op1),
                          reads, writes)

    def recip(self, out, in_, reads, writes):
        return self.S.add("dve", lambda e: e.reciprocal(out=out, in_=in_), reads, writes)

    def rcp(self, out, in_, scratch, reads, writes):
        if self.fast_rcp:
            return self.S.add("dve", lambda e: e.reciprocal_approx_accurate(out, in_, scratch), reads, writes)
        return self.S.add("dve", lambda e: e.reciprocal(out=out, in_=in_), reads, writes)

    def bank(self, b, n=512, c0=0):
        return self.ps[:, b * 512 + c0: b * 512 + c0 + n]

    def build(self):
        nc = self.nc
        NW = 52600
        self.declare_io()
        with nc.sbuf_tensor("arena", [128, NW], F32) as arena_t, \
             nc.psum_tensor("psum", [128, 8 * 512], F32) as ps:
            self.ps = ps
            self.ar = Arena(arena_t, NW)
            self.body()
            self.stats = self.S.emit()
        return nc

    def declare_io(self):
        L = self.nl
        self.xT = self.din("xT", [16, 128, T])
        self.cvec = self.din("cvec", [128, 16, 2])
        self.cosT = self.din("cosT", [128, 2048])
        self.sinT = self.din("sinT", [128, 2048])
        self.permM = self.din("permM", [128, 128])
        self.retz = self.din("retz", [128, 10, 512])
        self.rows128 = self.din("rows128", [128, 20])
        self.identM = self.din("identM", [128, 128])
        self.W = []
        for l in range(L):
            w = {}
            w["wmod"] = self.din(f"wmod{l}", [36, 128, 4, 16, 128])
            w["bmod"] = self.din(f"bmod{l}", [128, 144])
            w["pre"] = self.din(f"pre{l}", [128, 3, 16])
            w["post"] = self.din(f"post{l}", [128, 3, 16])
            w["fwin"] = [self.din(f"fwin{l}_{i}", [44, 128, 2, 16, 128]) for i in range(2)]
            w["fwout"] = [self.din(f"fwout{l}_{i}", [16, 128, 44, 128]) for i in range(2)]
            w["win"] = self.din(f"win{l}", [30, 128, 8192])
            w["wbr"] = self.din(f"wbr{l}", [16, 128, 3, 8, 128])
            w["wo"] = self.din(f"wo{l}", [16, 128, 16, 128])
            w["dlam"] = self.din(f"dlam{l}", [128, 256])
            w["dnorm"] = self.din(f"dnorm{l}", [128, 1])
            w["dnrow"] = self.din(f"dnrow{l}", [128, 128])
            w["rnorm"] = self.din(f"rnorm{l}", [128, 1])
            w["rdecay"] = self.din(f"rdecay{l}", [128, 16])
            w["rpbx"] = self.din(f"rpbx{l}", [8, 128, 25, 128])
            self.W.append(w)
        self.outT = self.nc.dram_tensor("outT", [16, 128, 2048], F32, kind="ExternalOutput").ap()
        self.hT = self.dscr("hT", [16, 128, T], F32)
        self.yscr = self.dscr("yscr", [2, 16, 128, 768], F32)
        for nm in ("qaT", "kaT", "qbT", "kbT", "cgT", "oaT", "obT", "yrT"):
            setattr(self, nm, self.dscr(nm, [8, 128, T], BF16))
        for nm in ("qcT", "kcT"):
            setattr(self, nm, self.dscr(nm, [4, 128, T], BF16))
        for nm in ("va", "vb", "vc"):
            setattr(self, nm, self.dscr(nm, [NT, 128, 1024], BF16))
        self.gT = self.dscr("gT", [48, 128, T], BF16)
        self.modout = self.dscr("modout", [128, 144, 2], F32) if "modout" in self.dbg else None

    def body(self):
        ar = self.ar
        S = self.S
        self.ones = ar.alloc([128], BF16)
        self.epsT = ar.alloc([1], F32)
        self.oneT = ar.alloc([1], F32)
        self.out_keys = []
        self.csil = ar.alloc([16, 2], F32)
        self.modT = ar.alloc([144, 2], F32)
        self.modN = ar.alloc([144, 2], F32)
        self.rstdN = ar.alloc([T], F32)
        self.rstd_valid = False
        self.mod_ready = set()
        self.bmodS = ar.alloc([144], F32)
        self.preS = ar.alloc([3, 16], F32)
        self.postS = ar.alloc([3, 16], F32)
        self.Avec = ar.alloc([3, 16, 2], F32)
        self.Gvec = ar.alloc([3, 16, 2], F32)
        self.lamS = ar.alloc([8], F32)
        self.dnS = ar.alloc([2], F32)
        self.lgS = ar.alloc([16], F32)
        self.cft = ar.alloc([16, 20], F32)
        S.add("dve", lambda e: e.memset(self.ones, 1.0), writes=["ones"])
        S.add("dve", lambda e: e.memset(self.epsT, EPS), writes=["eps"])
        S.add("dve", lambda e: e.memset(self.oneT, 1.0), writes=["one"])
        self.dma("sp", self.csil, self.cvec, [], ["csil"], "ld0")
        self.csilb = ar.alloc([16, 2], BF16)
        self.act(self.csilb, self.csil, AF.Silu, ["csil"], ["csilb"])
        ph = self.phases
        h_in = self.xT
        for l in range(self.nl):
            last = (l == self.nl - 1) and not getattr(self, 'force_not_last', False)
            if ph is None or "mod" in ph:
                self.phase_mod(l)
                S.barrier()
            if ph is None or "ffn1" in ph:
                self.ffn_sublayer(l, 0, 0, h_in, self.hT, skip_ctx=False, final=False)
                S.barrier()
            h_in = self.hT
            if ph is None or "mix_in" in ph:
                self.mixer_inproj(l)
                S.barrier()
            if ph is None or "attA" in ph:
                self.attn_a(l, with_ctx=not last)
                S.barrier()
            if ph is None or "attB" in ph:
                self.attn_b(l, with_ctx=not last)
                S.barrier()
            if ph is None or "ret" in ph:
                self.retention(l, with_ctx=not last)
                S.barrier()
            if ph is None or "merge" in ph:
                self.merge(l, skip_ctx=last)
                S.barrier()
            if ph is None or "ffn2" in ph:
                self.ffn_sublayer(l, 1, 2, self.hT, self.outT if last else self.hT, skip_ctx=last, final=last)
                S.barrier()
        S.add("sp", None, reads=list(self.out_keys))

    def phase_mod(self, l):
        ar, S, W = self.ar, self.S, self.W[l]
        ar.push()
        B = 7
        if l not in self.mod_ready:
            wb = [ar.alloc([4, 16, 128], BF16) for _ in range(3)]
            for g in range(36):
                buf = wb[g % 3]
                self.dma("pool", buf, W["wmod"][g], [], [("wm", g % 3)], ("wm", g % 3))
                for mi in range(4):
                    mc = g * 4 + mi
                    for kc in range(16):
                        self.mm(self.ps[:, B * 512 + mc * 2: B * 512 + mc * 2 + 2], buf[:, mi, kc, :],
                                self.csilb[:, kc, :], kc == 0, kc == 15, [("wm", g % 3), "csilb"], [("ps", B)])
        self.dma("sp", self.bmodS, W["bmod"], [], ["bmod"], "ld0")
        self.dma("sp", self.preS, W["pre"], [], ["pre"], "ld1")
        self.dma("sp", self.postS, W["post"], [], ["post"], "ld2")
        if l in self.mod_ready:
            for j in range(2):
                self.tt(self.modT[:, :, j], self.modN[:, :, j], self.bmodS, ALU.add, ["modN", "bmod"], ["modT"])
        else:
            psv = self.bank(B, 288).rearrange("p (a b) -> p a b", b=2)
            for j in range(2):
                self.tt(self.modT[:, :, j], psv[:, :, j], self.bmodS, ALU.add, [("ps", B), "bmod"], ["modT"])
        if self.modout is not None:
            self.dma("sp", self.modout, self.modT, ["modT"], ["modout"], "ld0")
        for s in range(3):
            for j in range(2):
                sc = self.modT[:, (3 * s + 1) * 16:(3 * s + 2) * 16, j]
                gt = self.modT[:, (3 * s + 2) * 16:(3 * s + 3) * 16, j]
                self.stt(self.Avec[:, s, :, j], sc, 1.0, self.preS[:, s, :], ALU.add, ALU.mult,
                         ["modT", "pre"], ["Avec"])
                self.stt(self.Gvec[:, s, :, j], gt, 0.5 if s != 1 else 1.0, self.postS[:, s, :], ALU.mult, ALU.mult,
                         ["modT", "post"], ["Gvec"])
        lam_init = 0.8 - 0.6 * math.exp(-0.3 * l)
        dl = ar.alloc([256], F32)
        pr = ar.alloc([128], F32)
        sm = ar.alloc([2], F32)
        self.dma("sp", dl, W["dlam"], [], ["dl"], "ld0")
        dlv = dl.rearrange("p (a b) -> p a b", b=64)
        prv = pr.rearrange("p (a b) -> p a b", b=64)
        self.tt(prv[:, 0, :], dlv[:, 0, :], dlv[:, 1, :], ALU.mult, ["dl"], ["pr"])
        self.tt(prv[:, 1, :], dlv[:, 2, :], dlv[:, 3, :], ALU.mult, ["dl"], ["pr"])
        S.add("dve", lambda e: e.reduce_sum(out=sm, in_=prv, axis=AX.X), ["pr"], ["sm"])
        self.act(sm, sm, AF.Exp, ["sm"], ["sm"])
        self.stt(self.lamS[:, 0:1], sm[:, 1:2], -lam_init, sm[:, 0:1], ALU.add, ALU.subtract, ["sm"], ["lamS"])
        dn = ar.alloc([2], F32)
        self.dma("sp", dn[:, 0:1], W["dnorm"], [], ["dn"], "ld1")
        self.dma("sp", dn[:, 1:2], W["rnorm"], [], ["dn"], "ld2")
        self.ts(self.dnS[:, 0:1], dn[:, 0:1], 1.0 - lam_init, ALU.mult, ["dn"], ["dnS"])
        self.ts(self.dnS[:, 1:2], dn[:, 1:2], 1.0, ALU.mult, ["dn"], ["dnS"])
        rd = ar.alloc([16], F32)
        self.dma("sp", rd, W["rdecay"], [], ["rd"], "ld0")
        self.act(rd, rd, AF.Exp, ["rd"], ["rd"], scale=-1.0)
        self.act(rd, rd, AF.Ln, ["rd"], ["rd", "one"][:1], bias=self.oneT[:, 0:1])
        self.ts(self.lgS, rd, -1.0, ALU.mult, ["rd"], ["lgS"])
        r128 = ar.alloc([20], F32)
        self.dma("sp", r128, self.rows128, [], ["r128"], "ld1")
        for k in range(16):
            self.act(self.cft[:, k, :], r128, AF.Exp, ["r128", "lgS"], ["cft"], scale=self.lgS[:, k:k + 1])
        ar.pop()

    def mod_overlap_gen(self, ln, wbufs):
        W = self.W[ln]
        modNf = self.modN.rearrange("p a b -> p (a b)")
        for g in range(36 + 2):
            if g < 36:
                self.dma("pool", wbufs[g % 3], W["wmod"][g], [], [("wmo", g % 3)], ("wm", g % 3))
            if g >= 2:
                g2 = g - 2
                buf = wbufs[g2 % 3]
                c0 = 7 * 512 + 384 + (g2 % 8) * 8
                for mi in range(4):
                    for kc in range(16):
                        self.mm(self.ps[:, c0 + mi * 2: c0 + mi * 2 + 2], buf[:, mi, kc, :], self.csilb[:, kc, :],
                                kc == 0, kc == 15, [("wmo", g2 % 3), "csilb"], [("ps", 7)])
                self.S.add("dve", (lambda e, o=modNf[:, g2 * 8:g2 * 8 + 8], i=self.ps[:, c0:c0 + 8]:
                                   e.tensor_copy(out=o, in_=i)), [("ps", 7)], ["modN"])
            yield
        self.mod_ready.add(ln)

    def prep_gen(self, l, sidx, h_src, parts, uT, ucols, statbank, bufs):
        hb, sqb, tmpf, rst = bufs
        if self.rstd_valid and self.fuse_stats:
            for (c0, n, j), uc0 in zip(parts, ucols):
                for ch in range(16):
                    k = ch % 4
                    self.dma("sp", hb[k][:, :n], h_src[ch][:, c0:c0 + n], [("h", ch, c0)], [("hb", k)], ("hb", k))
                    self.tt(tmpf[ch % 2][:, :n], hb[k][:, :n], self.rstdN[:, c0:c0 + n], ALU.mult,
                            [("hb", k), "rstdN"], [("tmpf", ch % 2)])
                    self.act(uT[:, ch, uc0:uc0 + n], tmpf[ch % 2][:, :n], AF.Identity,
                             [("tmpf", ch % 2), "Avec", "modT"], [("uT", ch)],
                             scale=self.Avec[:, sidx, ch, j:j + 1], bias=self.modT[:, 3 * sidx * 16 + ch, j:j + 1])
                    yield
            return
        for (c0, n, j), uc0 in zip(parts, ucols):
            for ch in range(18):
                if ch < 16:
                    k = ch % 4
                    self.dma("sp", hb[k][:, :n], h_src[ch][:, c0:c0 + n], [("h", ch, c0)], [("hb", k)], ("hb", k))
                    self.act(sqb[k][:, :n], hb[k][:, :n], AF.Square, [("hb", k)], [("sqb", k)])
                if ch >= 2:
                    c2 = ch - 2
                    self.mm(self.bank(statbank, n), self.ones, sqb[c2 % 4][:, :n], c2 == 0, c2 == 15,
                            [("sqb", c2 % 4), "ones"], [("ps", statbank)])
                yield
            self.act(rst[:, :n], self.bank(statbank, n), AF.Sqrt, [("ps", statbank), "eps"], ["rst"],
                     scale=1.0 / D, bias=self.epsT[:, 0:1])
            self.recip(rst[:, :n], rst[:, :n], ["rst"], ["rst"])
            for ch in range(16):
                k = ch % 4
                self.dma("sp", hb[k][:, :n], h_src[ch][:, c0:c0 + n], [("h", ch, c0)], [("hb", k)], ("hb", k))
                self.tt(tmpf[ch % 2][:, :n], hb[k][:, :n], rst[:, :n], ALU.mult, [("hb", k), "rst"], [("tmpf", ch % 2)])
                self.act(uT[:, ch, uc0:uc0 + n], tmpf[ch % 2][:, :n], AF.Identity,
                         [("tmpf", ch % 2), "Avec", "modT"], [("uT", ch)],
                         scale=self.Avec[:, sidx, ch, j:j + 1], bias=self.modT[:, 3 * sidx * 16 + ch, j:j + 1])
                yield

    def alloc_prep_bufs(self):
        ar = self.ar
        hb = [ar.alloc([512], F32) for _ in range(4)]
        sqb = [ar.alloc([512], BF16) for _ in range(4)]
        tmpf = [ar.alloc([512], F32) for _ in range(2)]
        rst = ar.alloc([512], F32)
        return hb, sqb, tmpf, rst

    def alloc_post_bufs(self):
        ar = self.ar
        d = {}
        d["ysb"] = [ar.alloc([768], F32) for _ in range(2)]
        d["ysq"] = [ar.alloc([768], BF16) for _ in range(2)]
        d["yl"] = [ar.alloc([768], F32) for _ in range(2)]
        d["hl"] = [ar.alloc([768], F32) for _ in range(2)]
        d["ho"] = [ar.alloc([768], F32) for _ in range(2)]
        d["rstp"] = ar.alloc([768], F32)
        return d

    def post_evac(self, pb, blk, oc, parts, ybanks, statbanks, slot):
        ysb, ysq = pb["ysb"][oc % 2], pb["ysq"][oc % 2]
        off = 0
        for pi, (c0, n, j, on) in enumerate(parts):
            if on:
                yb = self.bank(ybanks[pi], n)
                self.act(ysq[:, off:off + n], yb, AF.Square, [("ps", ybanks[pi])], [("ysq", oc % 2, pi)])
                self.mm(self.bank(statbanks[pi], n), self.ones, ysq[:, off:off + n], oc == 0, oc == 15,
                        [("ysq", oc % 2, pi), "ones"], [("ps", statbanks[pi])])
                self.S.add("dve", (lambda e, o=ysb[:, off:off + n], i=yb: e.tensor_copy(out=o, in_=i)),
                           [("ps", ybanks[pi])], [("ysb", oc % 2, pi)])
            off += n
        ntot = off
        act_parts = [pi for pi, p in enumerate(parts) if p[3]]
        self.dma("sp", self.yscr[slot][oc][:, :ntot], ysb[:, :ntot], [("ysb", oc % 2, pi) for pi in act_parts],
                 [("yscr", slot, oc)], ("ysb", oc % 2))

    def post_gen(self, pb, l, sidx, blk, parts, statbanks, slot, h_in, h_out, final):
        rstp = pb["rstp"]
        off = 0
        offs = []
        for pi, (c0, n, j, on) in enumerate(parts):
            offs.append(off)
            if on:
                self.act(rstp[:, off:off + n], self.bank(statbanks[pi], n), AF.Sqrt, [("ps", statbanks[pi]), "eps"],
                         [("rstp", pi)], scale=1.0 / D, bias=self.epsT[:, 0:1])
                self.recip(rstp[:, off:off + n], rstp[:, off:off + n], [("rstp", pi)], [("rstp", pi)])
            off += n
        ntot = off
        for oc in range(16):
            k = oc % 2
            yl, hl, ho = pb["yl"][k], pb["hl"][k], pb["ho"][k]
            self.dma("sp", yl[:, :ntot], self.yscr[slot][oc][:, :ntot], [("yscr", slot, oc)], [("yl", k)], ("yl", k))
            for pi, (c0, n, j, on) in enumerate(parts):
                if not on:
                    continue
                o = offs[pi]
                self.dma("sp", hl[:, o:o + n], h_in[oc][:, c0:c0 + n], [("h", oc, c0)], [("hl", k, pi)], ("hl", k, pi))
                self.tt(yl[:, o:o + n], yl[:, o:o + n], rstp[:, o:o + n], ALU.mult, [("yl", k), ("rstp", pi)],
                        [("yl", k)])
                self.stt(ho[:, o:o + n], yl[:, o:o + n], self.Gvec[:, sidx, oc, j:j + 1], hl[:, o:o + n],
                         ALU.mult, ALU.add, [("yl", k), ("hl", k, pi), "Gvec"], [("ho", k, pi)])
                if final:
                    lc = lat_col(c0)
                    self.out_keys.append(("OUT", oc, c0))
                    self.dma("sp", h_out[oc][:, lc:lc + n], ho[:, o:o + n], [("ho", k, pi)], [("OUT", oc, c0)],
                             ("ho", k, pi))
                else:
                    self.dma("sp", h_out[oc][:, c0:c0 + n], ho[:, o:o + n], [("ho", k, pi)], [("h", oc, c0)],
                             ("ho", k, pi))
                    if self.fuse_stats:
                        self.act(pb["ysq"][k][:, o:o + n], ho[:, o:o + n], AF.Square, [("ho", k, pi)], [("ysq", k, pi)])
            if self.fuse_stats and not final and oc >= 1:
                self._nstat_mm(pb, parts, offs, statbanks, oc - 1)
            yield
        if self.fuse_stats and not final:
            self._nstat_mm(pb, parts, offs, statbanks, 15)
            for pi, (c0, n, j, on) in enumerate(parts):
                if not on:
                    continue
                self.act(self.rstdN[:, c0:c0 + n], self.bank(statbanks[pi], n), AF.Sqrt, [("ps", statbanks[pi]), "eps"],
                         ["rstdN"], scale=1.0 / D, bias=self.epsT[:, 0:1])
                self.recip(self.rstdN[:, c0:c0 + n], self.rstdN[:, c0:c0 + n], ["rstdN"], ["rstdN"])
            yield

    def _nstat_mm(self, pb, parts, offs, statbanks, oc):
        k = oc % 2
        for pi, (c0, n, j, on) in enumerate(parts):
            if not on:
                continue
            o = offs[pi]
            self.mm(self.bank(statbanks[pi], n), self.ones, pb["ysq"][k][:, o:o + n], oc == 0, oc == 15,
                    [("ysq", k, pi), "ones"], [("ps", statbanks[pi])])

    def post_pass(self, *a):
        for _ in self.post_gen(*a):
            pass

    def ffn_sublayer(self, l, i, sidx, h_in, h_out, skip_ctx, final):
        ar, S, W = self.ar, self.S, self.W[l]
        ar.push()
        uT = ar.alloc([16, 768], BF16)
        gT = ar.alloc([44, 768], BF16)
        wbuf = [ar.alloc([2, 16, 128], BF16) for _ in range(3)]
        wobuf = [ar.alloc([44, 128], BF16) for _ in range(2)]
        sa = [ar.alloc([512], BF16) for _ in range(2)]
        pbufs = self.alloc_prep_bufs()
        pb = self.alloc_post_bufs()

        def parts_of(b):
            return [(c0, n, 1 if part_is_ctx(b, p) else 0, not (skip_ctx and part_is_ctx(b, p)))
                    for p, (c0, n) in enumerate(PARTS[b])]

        def prep_for(b, statbank):
            ps_ = [(c0, n, j) for (c0, n, j, on) in parts_of(b) if on]
            uc = [0 if idx == 0 else 512 for idx, (c0, n, j, on) in enumerate(parts_of(b)) if on]
            return self.prep_gen(l, sidx, h_in, ps_, uT, uc, statbank, pbufs)

        for _ in prep_for(0, 7):
            pass
        wcount = 0
        postg = iter(())
        for b in range(3):
            parts = parts_of(b)
            slot = b % 2
            for fc in range(44):
                next(postg, None)
                wb = wbuf[wcount % 3]
                wk = ("wbuf", wcount % 3)
                self.dma("pool", wb, W["fwin"][i][fc], [], [wk], wk)
                wcount += 1
                st = fc % 2
                banks = (st * 3, st * 3 + 1, st * 3 + 2)
                for half in range(2):
                    for kc in range(16):
                        self.mm(self.bank(banks[half]), wb[:, half, kc, :], uT[:, kc, 0:512], kc == 0, kc == 15,
                                [wk, ("uT", kc)], [("ps", banks[half])])
                if parts[1][3]:
                    for half in range(2):
                        for kc in range(16):
                            self.mm(self.bank(banks[2], 256, half * 256), wb[:, half, kc, :], uT[:, kc, 512:768],
                                    kc == 0, kc == 15, [wk, ("uT", kc)], [("ps", banks[2])])
                self.act(sa[0], self.bank(banks[0]), AF.Silu, [("ps", banks[0])], [("sa", 0)])
                self.tt(gT[:, fc, 0:512], sa[0], self.bank(banks[1]), ALU.mult, [("sa", 0), ("ps", banks[1])],
                        [("gT", fc)])
                if parts[1][3]:
                    self.act(sa[1][:, :256], self.bank(banks[2], 256, 0), AF.Silu, [("ps", banks[2])], [("sa", 1)])
                    self.tt(gT[:, fc, 512:768], sa[1][:, :256], self.bank(banks[2], 256, 256), ALU.mult,
                            [("sa", 1), ("ps", banks[2])], [("gT", fc)])
            pg = prep_for(b + 1, 5) if b < 2 else iter(())
            for oc in range(16):
                for _ in range(5):
                    next(pg, None)
                wo = wobuf[oc % 2]
                wok = ("wobuf", oc % 2)
                self.dma("pool", wo, W["fwout"][i][oc], [], [wok], wok)
                st = oc % 2
                ybanks = (st * 2, st * 2 + 1)
                for pi, (c0, n, j, on) in enumerate(parts):
                    if not on:
                        continue
                    u0 = 0 if pi == 0 else 512
                    for fc in range(44):
                        self.mm(self.bank(ybanks[pi], n), wo[:, fc, :], gT[:, fc, u0:u0 + n], fc == 0, fc == 43,
                                [wok, ("gT", fc)], [("ps", ybanks[pi])])
                self.post_evac(pb, b, oc, parts, ybanks, (6, 7), slot)
            for _ in pg:
                pass
            for _ in postg:
                pass
            postg = self.post_gen(pb, l, sidx, b, parts, (6, 7), slot, h_in, h_out, final)
        for _ in postg:
            pass
        self.rstd_valid = not final
        ar.pop()

    def mixer_inproj(self, l):
        ar, S, W = self.ar, self.S, self.W[l]
        ar.push()
        uT = ar.alloc([16, T], BF16)
        wbuf = [ar.alloc([8192], BF16) for _ in range(2)]
        cbuf = [ar.alloc([T], BF16) for _ in range(4)]
        vbuf = [ar.alloc([512], BF16) for _ in range(3)]
        cosS = ar.alloc([2048], F32)
        sinS = ar.alloc([2048], F32)
        pm = ar.alloc([128], F32)
        tsb = [ar.alloc([512], F32) for _ in range(2)]
        ra = [ar.alloc([512], F32) for _ in range(2)]
        rb = [ar.alloc([512], F32) for _ in range(2)]
        pbufs = self.alloc_prep_bufs()
        self.dma("sp", cosS, self.cosT, [], ["cos"], "ld0")
        self.dma("sp", sinS, self.sinT, [], ["sin"], "ld1")
        self.dma("sp", pm, self.permM, [], ["pm"], "ld2")
        allparts = [(0, 512, 0), (512, 256, 1), (768, 512, 0), (1280, 512, 0), (1792, 512, 0)]
        for _ in self.prep_gen(l, 1, self.hT, allparts, uT, [p[0] for p in allparts], 7, pbufs):
            pass
        fparts = [(0, 512, 0), (512, 256, None), (768, 512, 4), (1280, 512, 8), (1792, 512, 12)]
        ccount = 0
        ecount = 0
        vcount = 0
        rcount = 0
        deferred = []
        for g in range(30):
            wb = wbuf[g % 2]
            wk = ("wbuf", g % 2)
            self.dma("pool", wb, W["win"][g], [], [wk], wk)
            if g in (4, 5, 10, 11, 14, 15):
                dst = {4: self.va, 5: self.va, 10: self.vb, 11: self.vb, 14: self.vc, 15: self.vc}[g]
                hoff = (g % 2) * 512
                wv = wb.rearrange("p (k c) -> p k c", c=512)
                for tt_ in range(NT):
                    bk = ecount % 4
                    ecount += 1
                    for kc in range(16):
                        self.mm(self.bank(bk), uT[:, kc, tt_ * 128:(tt_ + 1) * 128], wv[:, kc, :], kc == 0, kc == 15,
                                [wk, ("uT", kc)], [("ps", bk)])
                    vb_ = vbuf[vcount % 3]
                    vk = ("vbuf", vcount % 3)
                    vcount += 1
                    self.act(vb_, self.bank(bk), AF.Copy, [("ps", bk)], [vk])
                    self.dma("sp", dst[tt_][:, hoff:hoff + 512], vb_, [vk], [("vdst", g, tt_)], vk)
                continue
            wf = wb.rearrange("p (c k f) -> p c k f", k=16, f=128)
            for ci in range(4):
                cc = g * 4 + ci
                if g < 2:
                    dst, mode, sc = self.qaT[cc], "rope", 1.0
                elif g < 4:
                    dst, mode, sc = self.kaT[cc - 8], "rope", 1.0
                elif g < 8:
                    dst, mode, sc = self.qbT[cc - 24], "copy", 1.0
                elif g < 10:
                    dst, mode, sc = self.kbT[cc - 32], "copy", 1.0
                elif g == 12:
                    dst, mode, sc = self.qcT[cc - 48], "rope", 1.0
                elif g == 13:
                    dst, mode, sc = self.kcT[cc - 52], "rope", 0.125
                elif g < 18:
                    dst, mode, sc = self.cgT[cc - 64], "silu", 1.0
                else:
                    dst, mode, sc = self.gT[cc - 72], "sigmoid", 1.0
                cb = cbuf[ccount % 4]
                ck = ("cbuf", ccount % 4)
                ccount += 1
                for (c0, n, lt0) in fparts:
                    bk = ecount % 4
                    ecount += 1
                    for kc in range(16):
                        self.mm(self.bank(bk, n), wf[:, ci, kc, :], uT[:, kc, c0:c0 + n], kc == 0, kc == 15,
                                [wk, ("uT", kc)], [("ps", bk)])
                    for fnc in deferred:
                        fnc()
                    deferred = []
                    if mode == "rope" and lt0 is not None:
                        r = rcount % 2
                        rcount += 1
                        self.act(tsb[r], self.bank(bk), AF.Copy, [("ps", bk)], [("tsb", r)], scale=sc)

                        def tail(r=r, t0=lt0 * 128, cb=cb, c0=c0, ck=ck):
                            rbk = 4 + r
                            self.mm(self.bank(rbk), pm, tsb[r], True, True, [("tsb", r), "pm"], [("ps", rbk)])
                            self.tt(ra[r], tsb[r], cosS[:, t0:t0 + 512], ALU.mult, [("tsb", r), "cos"], [("ra", r)])
                            self.tt(rb[r], self.bank(rbk), sinS[:, t0:t0 + 512], ALU.mult, [("ps", rbk), "sin"],
                                    [("rb", r)])
                            self.tt(cb[:, c0:c0 + 512], ra[r], rb[r], ALU.add, [("ra", r), ("rb", r)], [ck])
                        deferred.append(tail)
                    else:
                        fn = {"rope": AF.Copy, "copy": AF.Copy, "silu": AF.Silu, "sigmoid": AF.Sigmoid}[mode]
                        self.act(cb[:, c0:c0 + n], self.bank(bk, n), fn, [("ps", bk)], [ck], scale=sc)
                deferred.append(lambda dst=dst, cb=cb, ck=ck, cc=cc: self.dma("sp", dst, cb, [ck], [("fdst", cc)], ck))
            for fnc in deferred:
                fnc()
            deferred = []
        ar.pop()

    def attn_a(self, l, with_ctx):
        ar, S, W = self.ar, self.S, self.W[l]
        ar.push()
        q1 = [ar.alloc([T], BF16) for _ in range(2)]
        q2 = [ar.alloc([T], BF16) for _ in range(2)]
        kS = [ar.alloc([T], BF16) for _ in range(2)]
        vS = [ar.alloc([NT, 129], BF16) for _ in range(2)]
        E1 = [ar.alloc([512], BF16) for _ in range(3)]
        E2 = [ar.alloc([512], BF16) for _ in range(3)]
        cacc = [ar.alloc([3, 512], F32) for _ in range(2)]
        ident = ar.alloc([128], BF16)
        dnrow = ar.alloc([128], F32)
        sm = [ar.alloc([8], F32) for _ in range(2)]
        t1 = [ar.alloc([128], F32) for _ in range(2)]
        o_ = [ar.alloc([128], F32) for _ in range(2)]
        junkf = [ar.alloc([128], F32) for _ in range(2)]
        onb = [ar.alloc([128], BF16) for _ in range(2)]
        obuf = [ar.alloc([T], BF16) for _ in range(2)]
        self.dma("pool", ident, self.identM, [], ["ident"], "ld0")
        self.dma("sp", dnrow, W["dnrow"], [], ["dnrow"], "ld1")
        self.ts(dnrow, dnrow, 1.0 - (0.8 - 0.6 * math.exp(-0.3 * l)), ALU.mult, ["dnrow"], ["dnrow"])
        for i in range(2):
            S.add("dve", (lambda e, a=q1[i][64:128, :]: e.memset(a, 0.0)), writes=[("q1z", i)])
            S.add("dve", (lambda e, a=q2[i][0:64, :]: e.memset(a, 0.0)), writes=[("q2z", i)])
            S.add("dve", (lambda e, a=vS[i][:, :, 128:129]: e.memset(a, 1.0)), writes=[("vSo", i)])
        qblocks = [(c0, n, list(range(NT))) for (c0, n, _) in QBLK]
        if with_ctx:
            qblocks.append((CTX_COL0, 256, list(CTX_TILES)))
        slots1 = [(4, 0), (4, 129), (4, 258), (5, 0)]
        slots2 = [(5, 129), (5, 258), (6, 0), (6, 129)]
        scnt = 0
        ecnt = 0
        fcnt = 0
        tcnt = 0
        ps7b = self.bank(7).bitcast(BF16)
        tcnt_box = [0]

        def fin_gen(cb, c0, nq, ob, hb):
            ca = cacc[cb]
            ck = [("cacc", cb, 0), ("cacc", cb, 1), ("cacc", cb, 2)]
            for qt in range(nq):
                tb = tcnt_box[0] % 2
                tcnt_box[0] += 1
                b1, c1 = slots1[qt]
                b2, c2 = slots2[qt]
                A1 = ca[:, b1 - 4, c1:c1 + 129]
                A2 = ca[:, b2 - 4, c2:c2 + 129]
                smt = sm[tb]
                smk = ("sm", tb)
                self.recip(smt[:, 0:1], A1[:, 128:129], ck, [smk])
                self.recip(smt[:, 1:2], A2[:, 128:129], ck, [smk])
                self.tt(smt[:, 2:3], smt[:, 1:2], self.lamS[:, 0:1], ALU.mult, [smk, "lamS"], [smk])
                yield
                self.ts(t1[tb], A1[:, 0:128], smt[:, 0:1], ALU.mult, ck + [smk], [("t1", tb)])
                self.stt(o_[tb], A2[:, 0:128], smt[:, 2:3], t1[tb], ALU.mult, ALU.add, ck + [smk, ("t1", tb)],
                         [("o_", tb)])
                yield
                self.S.add("dve", (lambda e, o=junkf[tb], i=o_[tb], a=smt[:, 3:4]:
                                   e.scalar_tensor_tensor(out=o, in0=i, scalar=1.0, in1=i, op0=ALU.mult, op1=ALU.mult,
                                                          accum_out=a)),
                           [("o_", tb)], [("junk", tb), smk])
                yield
                self.act(smt[:, 4:5], smt[:, 3:4], AF.Ln, [smk, "eps"], [smk], scale=1.0 / 128,
                         bias=self.epsT[:, 0:1])
                self.act(smt[:, 5:6], smt[:, 4:5], AF.Exp, [smk], [smk], scale=-0.5)
                self.stt(onb[tb], o_[tb], smt[:, 5:6], dnrow, ALU.mult, ALU.mult, [("o_", tb), smk, "dnrow"],
                         [("onb", tb)])
                yield
                tv = ps7b[:, tb * 128:(tb + 1) * 128]
                self.S.add("pe", (lambda e, o=tv, i=onb[tb]: e.transpose(o, i, ident)), [("onb", tb), "ident"],
                           [("ps", 7)])
                yield
                qc = c0 + qt * 128
                self.S.add("dve", (lambda e, o=ob[:, qc:qc + 128], i=tv: e.tensor_copy(out=o, in_=i)),
                           [("ps", 7)], [("obuf", hb)])
                yield

        fin = iter(())
        modg = iter(())
        if l + 1 < self.nl and self.mod_overlap:
            mwb = [ar.alloc([4, 16, 128], BF16) for _ in range(3)]
            modg = self.mod_overlap_gen(l + 1, mwb)
        stepc = 0
        for h in range(8):
            hb = h % 2
            self.dma("sp", q1[hb][0:64, :], self.qaT[h][0:64, :], [], [("q1", hb)], ("q1", hb))
            self.dma("sp", q2[hb][64:128, :], self.qaT[h][64:128, :], [], [("q2", hb)], ("q2", hb))
            self.dma("sp", kS[hb], self.kaT[h], [], [("kS", hb)], ("kS", hb))
            self.dma("sp", vS[hb][:, :, 0:128], self.va[:, :, h * 128:(h + 1) * 128].rearrange("t p d -> p t d"),
                     [], [("vS", hb)], ("vS", hb))
            ob = obuf[hb]
            for (c0, n, ktl) in qblocks:
                nq = n // 128
                nk = len(ktl)
                pend = None
                first_in_bank = {}
                for idx in range(nk + 1):
                    cur = None
                    if idx < nk:
                        kt = ktl[idx]
                        sb = scnt % 2
                        scnt += 1
                        self.mm(self.bank(sb, n), kS[hb][:, kt * 128:(kt + 1) * 128], q1[hb][:, c0:c0 + n],
                                True, True, [("kS", hb), ("q1", hb), ("q1z", hb)], [("ps", sb)])
                        self.mm(self.bank(2 + sb, n), kS[hb][:, kt * 128:(kt + 1) * 128], q2[hb][:, c0:c0 + n],
                                True, True, [("kS", hb), ("q2", hb), ("q2z", hb)], [("ps", 2 + sb)])
                        eb = ecnt % 3
                        ecnt += 1
                        self.act(E1[eb][:, :n], self.bank(sb, n), AF.Exp, [("ps", sb)], [("E1", eb)], scale=0.125)
                        self.act(E2[eb][:, :n], self.bank(2 + sb, n), AF.Exp, [("ps", 2 + sb)], [("E2", eb)], scale=0.125)
                        cur = (kt, eb, idx)
                    if pend is not None:
                        kt_, eb_, i_ = pend
                        for (Eb, slots, ek) in ((E1[eb_], slots1, ("E1", eb_)), (E2[eb_], slots2, ("E2", eb_))):
                            for qt in range(nq):
                                bk, co = slots[qt]
                                st = (i_ == 0) and (bk not in first_in_bank)
                                if i_ == 0:
                                    first_in_bank[bk] = True
                                self.S.add("pe", (lambda e, o=self.bank(bk, 129, co), lt=Eb[:, qt * 128:(qt + 1) * 128],
                                                  r=vS[hb][:, kt_, :], st=st, sp=(i_ == nk - 1):
                                                  e.matmul(o, lt, r, start=st, stop=sp, skip_group_check=True)),
                                           [("vS", hb), ("vSo", hb), ek], [("ps", bk)])
                    pend = cur
                    next(fin, None)
                    next(fin, None)
                    stepc += 1
                    if stepc % 14 == 0:
                        next(modg, None)
                for _ in fin:
                    pass
                cb = fcnt % 2
                fcnt += 1
                ca = cacc[cb]
                for bi, bk in enumerate((4, 5, 6)):
                    if bi == 2 and nq < 3:
                        continue
                    self.S.add("dve", (lambda e, o=ca[:, bi, :], i=self.bank(bk): e.tensor_copy(out=o, in_=i)),
                               [("ps", bk)], [("cacc", cb, bi)])
                fin = fin_gen(cb, c0, nq, ob, hb)
            for _ in fin:
                pass
            fin = iter(())
            if with_ctx:
                self.dma("sp", self.oaT[h], ob, [("obuf", hb)], [("oaT", h)], ("obuf", hb))
            else:
                self.dma("sp", self.oaT[h][:, 0:512], ob[:, 0:512], [("obuf", hb)], [("oaT", h)], ("obuf", hb))
                self.dma("sp", self.oaT[h][:, 768:T], ob[:, 768:T], [("obuf", hb)], [("oaT", h)], ("obuf", hb))
        for _ in modg:
            pass
        ar.pop()

    def attn_b(self, l, with_ctx):
        ar, S, W = self.ar, self.S, self.W[l]
        ar.push()
        qS = [ar.alloc([T], BF16) for _ in range(2)]
        kS = [ar.alloc([T], BF16) for _ in range(2)]
        vS = [ar.alloc([NT, 128], BF16) for _ in range(2)]
        bias = [ar.alloc([25, 128], F32) for _ in range(2)]
        pre = [ar.alloc([5, 128], F32) for _ in range(2)]
        E = [ar.alloc([7, 128], BF16) for _ in range(2)]
        rr = [ar.alloc([256], F32) for _ in range(2)]
        rsc = [ar.alloc([256], F32) for _ in range(2)]
        obuf = [ar.alloc([T], BF16) for _ in range(2)]
        scale = 128 ** -0.5
        steps = []

        def loads(h):
            hb = h % 2
            self.dma("sp", qS[hb], self.qbT[h], [], [("qS", hb)], ("qS", hb))
            self.dma("sp", kS[hb], self.kbT[h], [], [("kS", hb)], ("kS", hb))
            self.dma("sp", vS[hb], self.vb[:, :, h * 128:(h + 1) * 128].rearrange("t p d -> p t d"),
                     [], [("vS", hb)], ("vS", hb))
            self.dma("sp", bias[hb], W["rpbx"][h], [], [("bias", hb)], ("bias", hb))

        def front(h, i, s2):
            hb = h % 2
            if i == 0:
                loads(h)
            qc0 = lat_T(i) * 128
            jl = nb_jlist(i)
            cls = nb_cls(i)
            nl_ = len(jl)
            tiles = [lat_T(j) for j in jl] + list(CTX_TILES)
            for sl, kt in enumerate(tiles):
                bk = s2 * 2 + (0 if sl < 4 else 1)
                cc = (sl if sl < 4 else sl - 4) * 128
                self.mm(self.bank(bk, 128, cc), kS[hb][:, kt * 128:(kt + 1) * 128], qS[hb][:, qc0:qc0 + 128],
                        True, True, [("kS", hb), ("qS", hb)], [("ps", bk)])
            b0 = self.bank(s2 * 2, 512).rearrange("p (a b) -> p a b", b=128)
            b1 = self.bank(s2 * 2 + 1, 512).rearrange("p (a b) -> p a b", b=128)
            n0 = min(nl_, 4)
            self.stt(pre[s2][:, 0:n0, :], b0[:, 0:n0, :], scale, bias[hb][:, cls * 5:cls * 5 + n0, :], ALU.mult, ALU.add,
                     [("ps", s2 * 2), ("bias", hb)], [("pre", s2)])
            if nl_ > 4:
                self.stt(pre[s2][:, 4:5, :], b1[:, 0:1, :], scale, bias[hb][:, cls * 5 + 4:cls * 5 + 5, :], ALU.mult,
                         ALU.add, [("ps", s2 * 2 + 1), ("bias", hb)], [("pre", s2)])
            self.act(E[s2][:, 0:nl_, :], pre[s2][:, 0:nl_, :], AF.Exp, [("pre", s2)], [("E", s2)])
            if nl_ <= 4:
                for ci in range(2):
                    sl = nl_ + ci
                    src = b0[:, sl:sl + 1, :] if sl < 4 else b1[:, sl - 4:sl - 3, :]
                    bkk = s2 * 2 + (0 if sl < 4 else 1)
                    self.act(E[s2][:, sl:sl + 1, :], src, AF.Exp, [("ps", bkk)], [("E", s2)], scale=scale)
            else:
                self.act(E[s2][:, 5:7, :], b1[:, 1:3, :], AF.Exp, [("ps", s2 * 2 + 1)], [("E", s2)], scale=scale)

        def back(h, i, s2):
            hb = h % 2
            ob = obuf[hb]
            qc0 = lat_T(i) * 128
            tiles = [lat_T(j) for j in nb_jlist(i)] + list(CTX_TILES)
            ns = len(tiles)
            ob_ = 4 + s2
            db_ = 6 + s2
            for sl, kt in enumerate(tiles):
                self.mm(self.bank(ob_, 128), vS[hb][:, kt, :], E[s2][:, sl, :], sl == 0, sl == ns - 1,
                        [("vS", hb), ("E", s2)], [("ps", ob_)])
            for sl, kt in enumerate(tiles):
                self.mm(self.bank(db_, 128), self.ones, E[s2][:, sl, :], sl == 0, sl == ns - 1,
                        ["ones", ("E", s2)], [("ps", db_)])
            self.rcp(rr[s2][:, :128], self.bank(db_, 128), rsc[s2][:, :128], [("ps", db_)], [("rr", s2)])
            self.tt(ob[:, qc0:qc0 + 128], self.bank(ob_, 128), rr[s2][:, :128], ALU.mult, [("ps", ob_), ("rr", s2)],
                    [("obuf", hb)])
            if i == 15 and not with_ctx:
                self.dma("sp", self.obT[h][:, 0:512], ob[:, 0:512], [("obuf", hb)], [("obT", h)], ("obuf", hb))
                self.dma("sp", self.obT[h][:, 768:T], ob[:, 768:T], [("obuf", hb)], [("obT", h)], ("obuf", hb))

        def front_c(h, s2):
            hb = h % 2
            n = 256
            for ci, kt in enumerate(CTX_TILES):
                self.mm(self.bank(s2 * 2 + ci, n), kS[hb][:, kt * 128:(kt + 1) * 128], qS[hb][:, CTX_COL0:CTX_COL0 + n],
                        True, True, [("kS", hb), ("qS", hb)], [("ps", s2 * 2 + ci)])
            Ev = E[s2].rearrange("p a b -> p (a b)")
            for ci in range(2):
                self.act(Ev[:, ci * 256:(ci + 1) * 256], self.bank(s2 * 2 + ci, n), AF.Exp, [("ps", s2 * 2 + ci)],
                         [("E", s2)], scale=scale)

        def back_c(h, s2):
            hb = h % 2
            ob = obuf[hb]
            n = 256
            Ev = E[s2].rearrange("p a b -> p (a b)")
            ob_ = 4 + s2
            db_ = 6 + s2
            for ci, kt in enumerate(CTX_TILES):
                self.mm(self.bank(ob_, n), vS[hb][:, kt, :], Ev[:, ci * 256:(ci + 1) * 256], ci == 0, ci == 1,
                        [("vS", hb), ("E", s2)], [("ps", ob_)])
            for ci, kt in enumerate(CTX_TILES):
                self.mm(self.bank(db_, n), self.ones, Ev[:, ci * 256:(ci + 1) * 256], ci == 0, ci == 1,
                        ["ones", ("E", s2)], [("ps", db_)])
            self.rcp(rr[s2][:, :n], self.bank(db_, n), rsc[s2][:, :n], [("ps", db_)], [("rr", s2)])
            self.tt(ob[:, CTX_COL0:CTX_COL0 + n], self.bank(ob_, n), rr[s2][:, :n], ALU.mult,
                    [("ps", ob_), ("rr", s2)], [("obuf", hb)])
            self.dma("sp", self.obT[h], ob, [("obuf", hb)], [("obT", h)], ("obuf", hb))

        cnt = 0
        for h in range(8):
            for i in range(16):
                s2 = cnt % 2
                cnt += 1
                steps.append((lambda h=h, i=i, s2=s2: front(h, i, s2), lambda h=h, i=i, s2=s2: back(h, i, s2)))
            if with_ctx:
                s2 = cnt % 2
                cnt += 1
                steps.append((lambda h=h, s2=s2: front_c(h, s2), lambda h=h, s2=s2: back_c(h, s2)))
        prev = None
        for fr, bk in steps:
            fr()
            if prev is not None:
                prev()
            prev = bk
        prev()
        ar.pop()

    def retention(self, l, with_ctx):
        ar, S = self.ar, self.S
        ar.push()
        qP = [[ar.alloc([T], BF16) for _ in range(2)] for _ in range(2)]
        kS = [ar.alloc([T], BF16) for _ in range(2)]
        vS = [ar.alloc([NT, 256], BF16) for _ in range(2)]
        for i in range(2):
            S.add("dve", (lambda e, a=qP[0][i][64:128, :]: e.memset(a, 0.0)), writes=[("qz", 0, i)])
            S.add("dve", (lambda e, a=qP[1][i][0:64, :]: e.memset(a, 0.0)), writes=[("qz", 1, i)])
        cg = [ar.alloc([T], BF16) for _ in range(2)]
        rz = ar.alloc([10, 512], F32)
        Dm = [ar.alloc([6, 512], F32) for _ in range(2)]
        dtmp = ar.alloc([512], F32)
        dctx = [ar.alloc([512], F32) for _ in range(2)]
        dctx2 = [ar.alloc([512], F32) for _ in range(2)]
        ssb = [ar.alloc([512], F32) for _ in range(2)]
        rs2 = ar.alloc([512], F32)
        rsc = ar.alloc([512], F32)
        att = [ar.alloc([512], BF16) for _ in range(4)]
        osb = ar.alloc([512], F32)
        osq = ar.alloc([512], BF16)
        rs = ar.alloc([512], F32)
        ybuf = [ar.alloc([T], BF16) for _ in range(2)]
        self.dma("sp", rz, self.retz, [], ["rz"], "ld0")
        qblocks = [(c0, n, j0, False) for (c0, n, j0) in QBLK]
        if with_ctx:
            qblocks.append((CTX_COL0, 256, 0, True))
        scnt = 0
        acnt = 0
        hcnt = 0
        osbs = [ar.alloc([512], F32) for _ in range(2)]
        osqs = [ar.alloc([512], BF16) for _ in range(2)]
        rfc = [0]

        def ret_fin(fb, n, hh, yb, c0, hb):
            yield
            yield
            self.mm(self.bank(6 + hh, n), self.ones, osqs[fb][:, :n], True, True, [("osq", fb), "ones"], [("ps", 6 + hh)])
            yield
            self.act(rs[:, :n], self.bank(6 + hh, n), AF.Ln, [("ps", 6 + hh), "eps"], ["rs"], scale=1.0 / 128,
                     bias=self.epsT[:, 0:1])
            self.act(rs2[:, :n], rs[:, :n], AF.Exp, ["rs"], ["rs2"], scale=-0.5)
            yield
            yield
            self.tt(osbs[fb][:, :n], osbs[fb][:, :n], rs2[:, :n], ALU.mult, [("osb", fb), "rs2"], [("osb", fb)])
            yield
            self.tt(yb[:, c0:c0 + n], osbs[fb][:, :n], cg[hb][:, c0:c0 + n], ALU.mult, [("osb", fb), ("cg", hb)],
                    [("ybuf", hb)])
            yield

        rfin = iter(())
        for m in range(4):
            mb = m % 2
            self.dma("sp", qP[0][mb][0:64, :], self.qcT[m][0:64, :], [], [("qS", 0, mb)], ("qS", 0, mb))
            self.dma("sp", qP[1][mb][64:128, :], self.qcT[m][64:128, :], [], [("qS", 1, mb)], ("qS", 1, mb))
            self.dma("sp", kS[mb], self.kcT[m], [], [("kS", mb)], ("kS", mb))
            self.dma("sp", vS[mb], self.vc[:, :, m * 256:(m + 1) * 256].rearrange("t p d -> p t d"),
                     [], [("vS", mb)], ("vS", mb))
            for hh in range(2):
                h = 2 * m + hh
                pb0 = hh * 64
                hb = hcnt % 2
                hcnt += 1
                self.dma("sp", cg[hb], self.cgT[h], [], [("cg", hb)], ("cg", hb))
                lgf = self.lgS[:, h:h + 1]
                lgb = self.lgS[:, 8 + h:9 + h]
                dm = Dm[hb]
                dk = ("Dm", hb)
                self.act(dm[:, 0, :], rz[:, 0, :], AF.Exp, ["rz", "lgS"], [dk], scale=lgf)
                self.act(dm[:, 1, :], rz[:, 1, :], AF.Exp, ["rz", "lgS"], [dk], scale=lgb)
                for a in range(4):
                    self.act(dm[:, 2 + a, :], rz[:, 2 + a, :], AF.Exp, ["rz", "lgS"], [dk], scale=lgf)
                    self.act(dtmp, rz[:, 6 + a, :], AF.Exp, ["rz", "lgS"], ["dtmp"], scale=lgb)
                    self.tt(dm[:, 2 + a, :], dm[:, 2 + a, :], dtmp, ALU.add, [dk, "dtmp"], [dk])
                yb = ybuf[hb]
                for (c0, n, j0, isctx) in qblocks:
                    if isctx:
                        ktl = [("cc", c) for c in range(2)]
                    else:
                        ktl = [("l", j) for j in range(16)] + [("c", c) for c in range(2)]
                    nk = len(ktl)
                    ob_ = 4 + (scnt // 100000) % 1
                    obank = 4 + hh
                    pendq = []
                    LAG = 3
                    for idx in range(nk + LAG):
                        cur = None
                        if idx < nk:
                            kind, jk = ktl[idx]
                            kt = lat_T(jk) if kind == "l" else CTX_TILES[jk]
                            sb = scnt % 4
                            scnt += 1
                            self.mm(self.bank(sb, n), kS[mb][:, kt * 128:(kt + 1) * 128],
                                    qP[hh][mb][:, c0:c0 + n], True, True,
                                    [("kS", mb), ("qS", hh, mb), ("qz", hh, mb)], [("ps", sb)])
                            ab = acnt % 4
                            acnt += 1
                            ak = ("att", ab)
                            sps = self.bank(sb, n)
                            alt = (kind == "l") and (acnt % 2 == 1)
                            if alt:
                                sf = ssb[(acnt // 2) % 2]
                                sfk = ("ssb", (acnt // 2) % 2)
                                if jk < j0:
                                    self.act(sf[:, :n], sps, AF.Copy, [("ps", sb), "cft"], [sfk],
                                             scale=self.cft[:, h, j0 - jk:j0 - jk + 1])
                                    dsel = dm[:, 0, :n]
                                elif jk > j0 + 3:
                                    self.act(sf[:, :n], sps, AF.Copy, [("ps", sb), "cft"], [sfk],
                                             scale=self.cft[:, 8 + h, jk - j0 - 4:jk - j0 - 3])
                                    dsel = dm[:, 1, :n]
                                else:
                                    self.act(sf[:, :n], sps, AF.Copy, [("ps", sb)], [sfk])
                                    dsel = dm[:, 2 + jk - j0, :n]
                                self.tt(att[ab][:, :n], sf[:, :n], dsel, ALU.mult, [sfk, dk], [ak], eng="pool")
                            elif kind == "l":
                                if jk < j0:
                                    self.stt(att[ab][:, :n], sps, self.cft[:, h, j0 - jk:j0 - jk + 1], dm[:, 0, :n],
                                             ALU.mult, ALU.mult, [("ps", sb), dk, "cft"], [ak])
                                elif jk > j0 + 3:
                                    self.stt(att[ab][:, :n], sps, self.cft[:, 8 + h, jk - j0 - 4:jk - j0 - 3], dm[:, 1, :n],
                                             ALU.mult, ALU.mult, [("ps", sb), dk, "cft"], [ak])
                                else:
                                    self.tt(att[ab][:, :n], sps, dm[:, 2 + jk - j0, :n], ALU.mult, [("ps", sb), dk], [ak])
                            elif kind == "c":
                                dc = dctx[acnt % 2]
                                dck = ("dctx", acnt % 2)
                                i1 = 2 + j0 - jk
                                i2 = 12 + jk - j0
                                dc2 = dctx2[acnt % 2]
                                self.act(dc, dm[:, 0, :], AF.Copy, [dk, "cft"], [dck], scale=self.cft[:, h, i1:i1 + 1])
                                self.act(dc2, dm[:, 1, :], AF.Copy, [dk, "cft"], [("dctx2", acnt % 2)],
                                         scale=self.cft[:, 8 + h, i2:i2 + 1])
                                self.tt(dc, dc, dc2, ALU.add, [dck, ("dctx2", acnt % 2)], [dck], eng="pool")
                                self.tt(att[ab][:, :n], sps, dc[:, :n], ALU.mult, [("ps", sb), dck], [ak])
                            else:
                                self.tt(att[ab][:, :n], sps, dm[:, 2 + jk, :n], ALU.mult, [("ps", sb), dk], [ak])
                            cur = (kt, ab, idx)
                        next(rfin, None)
                        if cur is not None:
                            pendq.append(cur)
                        if idx >= LAG and pendq:
                            kt_, ab_, i_ = pendq.pop(0)
                            self.mm(self.bank(obank, n), vS[mb][:, kt_, hh * 128:(hh + 1) * 128], att[ab_][:, :n],
                                    i_ == 0, i_ == nk - 1, [("vS", mb), ("att", ab_)], [("ps", obank)])
                    for _ in rfin:
                        pass
                    fb = rfc[0] % 2
                    rfc[0] += 1
                    self.act(osbs[fb][:, :n], self.bank(obank, n), AF.Copy, [("ps", obank), "dnS"], [("osb", fb)],
                             scale=self.dnS[:, 1:2])
                    self.act(osqs[fb][:, :n], self.bank(obank, n), AF.Square, [("ps", obank)], [("osq", fb)])
                    rfin = ret_fin(fb, n, hh, yb, c0, hb)
                for _ in rfin:
                    pass
                rfin = iter(())
                if with_ctx:
                    self.dma("sp", self.yrT[h], yb, [("ybuf", hb)], [("yrT", h)], ("ybuf", hb))
                else:
                    self.dma("sp", self.yrT[h][:, 0:512], yb[:, 0:512], [("ybuf", hb)], [("yrT", h)], ("ybuf", hb))
                    self.dma("sp", self.yrT[h][:, 768:T], yb[:, 768:T], [("ybuf", hb)], [("yrT", h)], ("ybuf", hb))
        ar.pop()

    def merge(self, l, skip_ctx):
        ar, S, W = self.ar, self.S, self.W[l]
        ar.push()
        obr = [ar.alloc([8, 768], BF16) for _ in range(3)]
        mT = ar.alloc([16, 768], BF16)
        wbr = [ar.alloc([3, 8, 128], BF16) for _ in range(3)]
        wo = [ar.alloc([16, 128], BF16) for _ in range(2)]
        gb = [ar.alloc([3, 768], BF16) for _ in range(2)]
        m1 = [ar.alloc([512], F32) for _ in range(2)]
        m2 = [ar.alloc([512], F32) for _ in range(2)]
        m3 = [ar.alloc([512], F32) for _ in range(2)]
        pb = self.alloc_post_bufs()
        srcs = (self.oaT, self.obT, self.yrT)
        wc = 0
        postg = iter(())
        for b in range(3):
            parts = [(c0, n, 1 if part_is_ctx(b, p) else 0, not (skip_ctx and part_is_ctx(b, p)))
                     for p, (c0, n) in enumerate(PARTS[b])]
            bc0 = b * 768
            slot = b % 2
            for br in range(3):
                self.dma("sp", obr[br], srcs[br][:, :, bc0:bc0 + 768].rearrange("h p t -> p h t"), [], [("obr", br)],
                         ("obr", br))
            for oc in range(16):
                next(postg, None)
                next(postg, None)
                wb_ = wbr[wc % 3]
                wk = ("wbr", wc % 3)
                wc += 1
                self.dma("pool", wb_, W["wbr"][oc], [], [wk], wk)
                g_ = gb[oc % 2]
                gk = ("gb", oc % 2)
                for br in range(3):
                    self.dma("sp", g_[:, br, :], self.gT[br * 16 + oc][:, bc0:bc0 + 768], [], [(gk, br)], (gk, br))
                for pi, (c0, n, j, on) in enumerate(parts):
                    if not on:
                        continue
                    u0 = 0 if pi == 0 else 512
                    st = (oc * 2 + pi) % 2
                    for br in range(3):
                        bk = st * 3 + br
                        for kc in range(8):
                            self.mm(self.bank(bk, n), wb_[:, br, kc, :], obr[br][:, kc, u0:u0 + n], kc == 0, kc == 7,
                                    [wk, ("obr", br)], [("ps", bk)])
                    a_, b_, c_ = m1[st], m2[st], m3[st]
                    self.tt(a_[:, :n], self.bank(st * 3, n), g_[:, 0, u0:u0 + n], ALU.mult, [("ps", st * 3), (gk, 0)],
                            [("m1", st)])
                    self.tt(b_[:, :n], self.bank(st * 3 + 1, n), g_[:, 1, u0:u0 + n], ALU.mult,
                            [("ps", st * 3 + 1), (gk, 1)], [("m2", st)])
                    self.tt(c_[:, :n], self.bank(st * 3 + 2, n), g_[:, 2, u0:u0 + n], ALU.mult,
                            [("ps", st * 3 + 2), (gk, 2)], [("m3", st)])
                    self.tt(a_[:, :n], a_[:, :n], b_[:, :n], ALU.add, [("m1", st), ("m2", st)], [("m1", st)])
                    self.tt(mT[:, oc, u0:u0 + n], a_[:, :n], c_[:, :n], ALU.add, [("m1", st), ("m3", st)], [("mT", oc)])
            for oc2 in range(16):
                w_ = wo[oc2 % 2]
                wk = ("wo", oc2 % 2)
                self.dma("pool", w_, W["wo"][oc2], [], [wk], wk)
                st = oc2 % 2
                ybanks = (st * 2, st * 2 + 1)
                for pi, (c0, n, j, on) in enumerate(parts):
                    if not on:
                        continue
                    u0 = 0 if pi == 0 else 512
                    for oc in range(16):
                        self.mm(self.bank(ybanks[pi], n), w_[:, oc, :], mT[:, oc, u0:u0 + n], oc == 0, oc == 15,
                                [wk, ("mT", oc)], [("ps", ybanks[pi])])
                self.post_evac(pb, b, oc2, parts, ybanks, (6, 7), slot)
            for _ in postg:
                pass
            postg = self.post_gen(pb, l, 1, b, parts, (6, 7), slot, self.hT, self.hT, False)
        for _ in postg:
            pass
        self.rstd_valid = True
        ar.pop()


def _rope_tables():
    half = 16
    inv = (10000.0 ** (-np.arange(half, dtype=np.float32) / half)).astype(np.float32)
    pos = np.arange(2048)
    prow = (pos // 64).astype(np.float32)
    pcol = (pos % 64).astype(np.float32)
    cosT = np.zeros((128, 2048), np.float32)
    sinT = np.zeros((128, 2048), np.float32)
    perm = np.zeros((128, 128), np.float32)
    for p in range(128):
        dd = p % 64
        axis = dd // 32
        jj = dd % 32
        i = jj % 16
        first = jj < 16
        ang = (prow if axis == 0 else pcol) * inv[i]
        cosT[p] = np.cos(ang.astype(np.float32))
        s = np.sin(ang.astype(np.float32))
        sinT[p] = -s if first else s
        partner = p + 16 if first else p - 16
        perm[partner, p] = 1.0
    return cosT, sinT, perm


def _ret_tables():
    kl = np.arange(128, dtype=np.float64)[:, None]
    x = np.arange(512, dtype=np.float64)[None, :]
    tabs = np.zeros((128, 10, 512), np.float32)
    tabs[:, 0] = x - kl
    tabs[:, 1] = 512 - x + kl
    for a in range(4):
        z = x - 128 * a - kl
        tabs[:, 2 + a] = np.where(z >= 0, z, BIGZ)
        tabs[:, 6 + a] = np.where(z <= 0, -z, BIGZ)
    rows = np.tile((128.0 * np.arange(20, dtype=np.float32))[None, :], (128, 1)).astype(np.float32)
    return tabs, rows


def _rpb_expand(rpb):
    WIN_R, WIN_C, GW, ROWS = 8, 16, 64, 32
    out = np.zeros((8, 128, 25, 128), np.float32)
    seen = {}
    for i in range(16):
        cls = nb_cls(i)
        jl = nb_jlist(i)
        q = np.arange(128)
        qr = 2 * i + q // 64
        qc = q % 64
        rs = np.clip(qr - 4, 0, ROWS - WIN_R)
        cstart = np.clip(qc - WIN_C // 2, 0, GW - WIN_C)
        for slot, j in enumerate(jl):
            k = np.arange(128)
            kr = 2 * j + k // 64
            kc = k % 64
            row_ok = (kr[:, None] >= rs[None, :]) & (kr[:, None] < rs[None, :] + WIN_R)
            col_ok = (kc[:, None] >= cstart[None, :]) & (kc[:, None] < cstart[None, :] + WIN_C)
            dr = np.clip(kr[:, None] - qr[None, :] + WIN_R - 1, 0, 2 * WIN_R - 2)
            dc = np.clip(kc[:, None] - qc[None, :] + WIN_C - 1, 0, 2 * WIN_C - 2)
            ok = row_ok & col_ok
            key = (cls, slot)
            sig = (ok.tobytes(), dr.tobytes(), dc.tobytes(), j - i)
            if key in seen:
                assert seen[key] == sig, f"class structure mismatch {i} {slot}"
                continue
            seen[key] = sig
            vals = rpb[:, dr, dc]
            out[:, :, cls * 5 + slot, :] = np.where(ok[None], vals, np.float32(-30000.0))
    return out


def _fm(v):
    return np.ascontiguousarray(v.reshape(16, 128).T)


def host_weights(inp, L=2):
    m = {}
    cosT, sinT, perm = _rope_tables()
    retz, rows = _ret_tables()
    m["cosT"], m["sinT"], m["permM"], m["retz"], m["rows128"] = cosT, sinT, perm, retz, rows
    m["identM"] = np.eye(128, dtype=np.float32)
    for l in range(L):
        wm = inp["w_mod"][l].reshape(16, 128, 36, 4, 128)
        m[f"wmod{l}"] = np.ascontiguousarray(wm.transpose(2, 1, 3, 0, 4))
        m[f"bmod{l}"] = np.ascontiguousarray(inp["b_mod"][l].reshape(144, 128).T)
        m[f"pre{l}"] = np.ascontiguousarray(inp["pre_norm"][l].reshape(3, 16, 128).transpose(2, 0, 1))
        m[f"post{l}"] = np.ascontiguousarray(inp["post_norm"][l].reshape(3, 16, 128).transpose(2, 0, 1))
        for i in range(2):
            wi = inp["ffn_w_in"][l, i].reshape(16, 128, 2, 44, 128)
            m[f"fwin{l}_{i}"] = np.ascontiguousarray(wi.transpose(3, 1, 2, 0, 4))
            wo = inp["ffn_w_out"][l, i].reshape(44, 128, 16, 128)
            m[f"fwout{l}_{i}"] = np.ascontiguousarray(wo.transpose(2, 1, 0, 3))
        w = inp["w_in"][l].reshape(16, 128, 30, 4, 128)
        fm = w.transpose(2, 1, 3, 0, 4).reshape(30, 128, 8192)
        tm = w.transpose(2, 1, 0, 3, 4).reshape(30, 128, 8192)
        win = np.ascontiguousarray(fm)
        for g in (4, 5, 10, 11, 14, 15):
            win[g] = tm[g]
        m[f"win{l}"] = win
        wb = inp["w_branch"][l].reshape(3, 8, 128, 16, 128)
        m[f"wbr{l}"] = np.ascontiguousarray(wb.transpose(3, 2, 0, 1, 4))
        wo_ = inp["w_out"][l].reshape(16, 128, 16, 128)
        m[f"wo{l}"] = np.ascontiguousarray(wo_.transpose(2, 1, 0, 3))
        m[f"dlam{l}"] = np.ascontiguousarray(np.tile(inp["diff_lambda"][l].reshape(1, 256), (128, 1)))
        m[f"dnorm{l}"] = np.ascontiguousarray(inp["diff_norm"][l].reshape(128, 1))
        m[f"dnrow{l}"] = np.ascontiguousarray(np.tile(inp["diff_norm"][l].reshape(1, 128), (128, 1)))
        m[f"rnorm{l}"] = np.ascontiguousarray(inp["ret_norm"][l].reshape(128, 1))
        m[f"rdecay{l}"] = np.ascontiguousarray(np.tile(inp["ret_decay"][l].reshape(1, 16), (128, 1)))
        m[f"rpbx{l}"] = _rpb_expand(inp["na_rpb"][l])
    return m


def host_core_inputs(inp, b):
    x = inp["x"][b]
    ctx = inp["ctx"][b]
    tok = np.concatenate([x[:512], ctx, x[512:]], axis=0)
    xT = np.ascontiguousarray(tok.T.reshape(16, 128, T))
    cv = np.stack([_fm(inp["c"][b]), _fm(inp["c_ctx"])], axis=-1)
    return {"xT": xT, "cvec": np.ascontiguousarray(cv)}


_CACHE = {}


def kernel(**inputs):
    inp = {k: np.asarray(v, dtype=np.float32) for k, v in inputs.items()}
    if "prog" not in _CACHE:
        p = Prog(2)
        p.build()
        _CACHE["prog"] = p
    p = _CACHE["prog"]
    wm = host_weights(inp)
    in_maps = []
    for b in range(8):
        d = dict(wm)
        d.update(host_core_inputs(inp, b))
        in_maps.append(d)
    res = run_bass_kernel_spmd(p.nc, in_maps, core_ids=list(range(8)))
    out = np.empty((8, 2048, 2048), np.float32)
    for b in range(8):
        oT = res.results[b]["outT"]
        out[b] = oT.reshape(2048, 2048).T
    return out
```

```python
import math
import numpy as np
import concourse.bass as bass
import concourse.mybir as mybir
from concourse.bass_utils import run_bass_kernel_spmd

F32 = mybir.dt.float32
BF16 = mybir.dt.bfloat16
AF = mybir.ActivationFunctionType
ALU = mybir.AluOpType
AX = mybir.AxisListType

ENGS = ("pe", "act", "dve", "pool", "sp")
EPS = 1e-6
D = 2048
T = 2304
NT = 18
BIGZ = 1.0e6


class Op:
    __slots__ = ("eng", "fn", "deps", "dma", "signal", "sigval", "gidx", "dmaval")

    def __init__(self, eng, fn, dma):
        self.eng = eng
        self.fn = fn
        self.dma = dma
        self.deps = None
        self.signal = False
        self.sigval = 0
        self.dmaval = 0
        self.gidx = 0


class Sched:
    def __init__(self, nc, same_engine_sync=True):
        self.nc = nc
        self.ops = []
        self.last_w = {}
        self.readers = {}
        self.last_dma = {}
        self.dma_cnt = {}
        self.same_engine_sync = same_engine_sync
        self.last_op_eng = {e: None for e in ENGS}

    def add(self, eng, fn, reads=(), writes=(), dma=None):
        op = Op(eng, fn, dma)
        op.gidx = len(self.ops)
        deps = {}
        psr = [b for b in reads if isinstance(b, tuple) and b[0] == "ps"]
        if psr:
            writes = list(writes) + [b for b in psr if b not in writes]

        def adddep(d):
            if d is None or d is op:
                return
            if d.dma is not None:
                k = ("dma", d.dma)
                if k not in deps or deps[k].dmaval < d.dmaval:
                    deps[k] = d
            else:
                k = d.eng
                if k not in deps or deps[k].gidx < d.gidx:
                    deps[k] = d

        for b in reads:
            adddep(self.last_w.get(b))
        for b in writes:
            adddep(self.last_w.get(b))
            for r in self.readers.get(b, ()):
                adddep(r)
        for b in reads:
            self.readers.setdefault(b, []).append(op)
        for b in writes:
            self.last_w[b] = op
            self.readers[b] = []
        if dma is not None:
            adddep(self.last_dma.get(dma))
            self.last_dma[dma] = op
            self.dma_cnt[dma] = self.dma_cnt.get(dma, 0) + 16
            op.dmaval = self.dma_cnt[dma]
        op.deps = list(deps.values())
        self.ops.append(op)
        self.last_op_eng[eng] = op
        return op

    def barrier(self):
        lasts = [o for o in self.last_op_eng.values() if o is not None]
        dmas = list(self.last_dma.values())
        for e in ENGS:
            op = Op(e, None, None)
            op.gidx = len(self.ops)
            op.deps = list(lasts + dmas)
            self.ops.append(op)
        self.last_w = {}
        self.readers = {}

    def emit(self):
        nc = self.nc
        for op in self.ops:
            for d in op.deps:
                if d.dma is None:
                    if d.eng == op.eng and (d.eng == "pe" or not self.same_engine_sync):
                        continue
                    d.signal = True
        cnt = {e: 0 for e in ENGS}
        for op in self.ops:
            if op.signal:
                cnt[op.eng] += 1
                op.sigval = cnt[op.eng]
        per_eng = {e: [o for o in self.ops if o.eng == e] for e in ENGS}
        sems_eng = {e: nc.alloc_semaphore(name=f"se_{e}") for e in ENGS}
        dkeys = []
        for o in self.ops:
            if o.dma is not None and o.dma not in dkeys:
                dkeys.append(o.dma)
        sems_dma = {k: nc.alloc_semaphore(name=f"sd_{i}") for i, k in enumerate(dkeys)}
        same = self.same_engine_sync
        stats = {e: [0, 0] for e in ENGS}

        def run(eng_name, eng):
            waited = {}
            for op in per_eng[eng_name]:
                for d in op.deps:
                    if d.dma is not None:
                        sem = sems_dma[d.dma]
                        val = d.dmaval
                        key = ("dma", d.dma)
                    else:
                        if d.eng == eng_name and (eng_name == "pe" or not same):
                            continue
                        sem = sems_eng[d.eng]
                        val = d.sigval
                        key = d.eng
                    if waited.get(key, 0) >= val:
                        continue
                    waited[key] = val
                    eng.wait_ge(sem, val)
                    stats[eng_name][1] += 1
                if op.fn is None:
                    continue
                inst = op.fn(eng)
                stats[eng_name][0] += 1
                if op.dma is not None:
                    inst.then_inc(sems_dma[op.dma], 16)
                elif op.signal:
                    inst.then_inc(sems_eng[eng_name], 1)

        with nc.Block() as block:
            @block.tensor
            def _(e):
                run("pe", e)

            @block.scalar
            def _(e):
                run("act", e)

            @block.vector
            def _(e):
                run("dve", e)

            @block.gpsimd
            def _(e):
                run("pool", e)

            @block.sync
            def _(e):
                run("sp", e)
        return stats


class Arena:
    def __init__(self, tensor, nwords):
        self.t = tensor
        self.n = nwords
        self.off = 0
        self.marks = []

    def alloc(self, shape, dtype):
        nel = int(np.prod(shape))
        esz = 4 if dtype == F32 else 2
        words = (nel * esz + 3) // 4
        words = (words + 7) // 8 * 8
        assert self.off + words <= self.n, f"SBUF arena overflow {self.off}+{words}>{self.n}"
        ap = self.t[:, self.off:self.off + words]
        self.off += words
        if dtype != F32:
            ap = ap.bitcast(dtype)
        ap = ap[:, 0:nel]
        if len(shape) == 2:
            ap = ap.rearrange("p (a b) -> p a b", b=shape[1])
        elif len(shape) == 3:
            ap = ap.rearrange("p (a b c) -> p a b c", b=shape[1], c=shape[2])
        elif len(shape) == 4:
            ap = ap.rearrange("p (a b c d) -> p a b c d", b=shape[1], c=shape[2], d=shape[3])
        return ap

    def push(self):
        self.marks.append(self.off)

    def pop(self):
        self.off = self.marks.pop()


def lat_T(j):
    return j if j < 4 else j + 2


CTX_TILES = (4, 5)
QBLK = [(0, 512, 0), (768, 512, 4), (1280, 512, 8), (1792, 512, 12)]
CTX_COL0 = 512
PARTS = [[(b * 768, 512), (b * 768 + 512, 256)] for b in range(3)]


def part_is_ctx(b, p):
    return b == 0 and p == 1


def lat_col(c):
    return c if c < 512 else c - 256


def nb_jlist(i):
    s = set()
    for r in (2 * i, 2 * i + 1):
        rs = min(max(r - 4, 0), 24)
        for kr in range(rs, rs + 8):
            s.add(kr // 2)
    return sorted(s)


def nb_cls(i):
    return {0: 0, 1: 1, 14: 3, 15: 4}.get(i, 2)


class Prog:
    def __init__(self, nlayers=2, dbg=(), phases=None, same_engine_sync=True):
        self.nl = nlayers
        self.dbg = set(dbg)
        self.phases = phases
        nc = bass.Bass("TRN2", target_bir_lowering=False)
        self.nc = nc
        self.S = Sched(nc, same_engine_sync=same_engine_sync)
        self.in_names = []
        self.fast_rcp = False
        self.mod_overlap = True
        self.fuse_stats = True

    def din(self, name, shape, dt=F32):
        self.in_names.append(name)
        return self.nc.dram_tensor(name, list(shape), dt, kind="ExternalInput").ap()

    def dscr(self, name, shape, dt):
        kind = "ExternalOutput" if name in self.dbg else "Internal"
        return self.nc.dram_tensor(name, list(shape), dt, kind=kind).ap()

    def dma(self, eng, out, in_, reads, writes, sem):
        return self.S.add(eng, lambda e: e.dma_start(out=out, in_=in_), reads, writes, dma=sem)

    def mm(self, out, lhsT, rhs, start, stop, reads, writes):
        return self.S.add("pe", lambda e: e.matmul(out, lhsT, rhs, start=start, stop=stop), reads, writes)

    def act(self, out, in_, func, reads, writes, scale=None, bias=None):
        kw = {}
        if scale is not None:
            kw["scale"] = scale
        if bias is not None:
            kw["bias"] = bias
        return self.S.add("act", lambda e: e.activation(out=out, in_=in_, func=func, **kw), reads, writes)

    def tt(self, out, in0, in1, op, reads, writes, eng="dve"):
        return self.S.add(eng, lambda e: e.tensor_tensor(out=out, in0=in0, in1=in1, op=op), reads, writes)

    def stt(self, out, in0, scalar, in1, op0, op1, reads, writes):
        return self.S.add("dve", lambda e: e.scalar_tensor_tensor(out=out, in0=in0, scalar=scalar, in1=in1,
                                                                   op0=op0, op1=op1), reads, writes)

    def ts(self, out, in0, s1, op0, reads, writes, s2=None, op1=None, eng="dve"):
        if op1 is None:
            return self.S.add(eng, lambda e: e.tensor_scalar(out=out, in0=in0, scalar1=s1, scalar2=None, op0=op0),
                              reads, writes)
        return self.S.add(eng, lambda e: e.tensor_scalar(out=out, in0=in0, scalar1=s1, scalar2=s2, op0=op0, op1=op1),
                          reads, writes)

    def recip(self, out, in_, reads, writes):
        return self.S.add("dve", lambda e: e.reciprocal(out=out, in_=in_), reads, writes)

    def rcp(self, out, in_, scratch, reads, writes):
        if self.fast_rcp:
            return self.S.add("dve", lambda e: e.reciprocal_approx_accurate(out, in_, scratch), reads, writes)
        return self.S.add("dve", lambda e: e.reciprocal(out=out, in_=in_), reads, writes)

    def bank(self, b, n=512, c0=0):
        return self.ps[:, b * 512 + c0: b * 512 + c0 + n]

    def build(self):
        nc = self.nc
        NW = 52600
        self.declare_io()
        with nc.sbuf_tensor("arena", [128, NW], F32) as arena_t, \
             nc.psum_tensor("psum", [128, 8 * 512], F32) as ps:
            self.ps = ps
            self.ar = Arena(arena_t, NW)
            self.body()
            self.stats = self.S.emit()
        return nc

    def declare_io(self):
        L = self.nl
        self.xT = self.din("xT", [16, 128, T])
        self.cvec = self.din("cvec", [128, 16, 2])
        self.cosT = self.din("cosT", [128, 2048])
        self.sinT = self.din("sinT", [128, 2048])
        self.permM = self.din("permM", [128, 128])
        self.retz = self.din("retz", [128, 10, 512])
        self.rows128 = self.din("rows128", [128, 20])
        self.identM = self.din("identM", [128, 128])
        self.W = []
        for l in range(L):
            w = {}
            w["wmod"] = self.din(f"wmod{l}", [36, 128, 4, 16, 128])
            w["bmod"] = self.din(f"bmod{l}", [128, 144])
            w["pre"] = self.din(f"pre{l}", [128, 3, 16])
            w["post"] = self.din(f"post{l}", [128, 3, 16])
            w["fwin"] = [self.din(f"fwin{l}_{i}", [44, 128, 2, 16, 128]) for i in range(2)]
            w["fwout"] = [self.din(f"fwout{l}_{i}", [16, 128, 44, 128]) for i in range(2)]
            w["win"] = self.din(f"win{l}", [30, 128, 8192])
            w["wbr"] = self.din(f"wbr{l}", [16, 128, 3, 8, 128])
            w["wo"] = self.din(f"wo{l}", [16, 128, 16, 128])
            w["dlam"] = self.din(f"dlam{l}", [128, 256])
            w["dnorm"] = self.din(f"dnorm{l}", [128, 1])
            w["dnrow"] = self.din(f"dnrow{l}", [128, 128])
            w["rnorm"] = self.din(f"rnorm{l}", [128, 1])
            w["rdecay"] = self.din(f"rdecay{l}", [128, 16])
            w["rpbx"] = self.din(f"rpbx{l}", [8, 128, 25, 128])
            self.W.append(w)
        self.outT = self.nc.dram_tensor("outT", [16, 128, 2048], F32, kind="ExternalOutput").ap()
        self.hT = self.dscr("hT", [16, 128, T], F32)
        self.yscr = self.dscr("yscr", [2, 16, 128, 768], F32)
        for nm in ("qaT", "kaT", "qbT", "kbT", "cgT", "oaT", "obT", "yrT"):
            setattr(self, nm, self.dscr(nm, [8, 128, T], BF16))
        for nm in ("qcT", "kcT"):
            setattr(self, nm, self.dscr(nm, [4, 128, T], BF16))
        for nm in ("va", "vb", "vc"):
            setattr(self, nm, self.dscr(nm, [NT, 128, 1024], BF16))
        self.gT = self.dscr("gT", [48, 128, T], BF16)
        self.modout = self.dscr("modout", [128, 144, 2], F32) if "modout" in self.dbg else None

    def body(self):
        ar = self.ar
        S = self.S
        self.ones = ar.alloc([128], BF16)
        self.epsT = ar.alloc([1], F32)
        self.oneT = ar.alloc([1], F32)
        self.out_keys = []
        self.csil = ar.alloc([16, 2], F32)
        self.modT = ar.alloc([144, 2], F32)
        self.modN = ar.alloc([144, 2], F32)
        self.rstdN = ar.alloc([T], F32)
        self.rstd_valid = False
        self.mod_ready = set()
        self.bmodS = ar.alloc([144], F32)
        self.preS = ar.alloc([3, 16], F32)
        self.postS = ar.alloc([3, 16], F32)
        self.Avec = ar.alloc([3, 16, 2], F32)
        self.Gvec = ar.alloc([3, 16, 2], F32)
        self.lamS = ar.alloc([8], F32)
        self.dnS = ar.alloc([2], F32)
        self.lgS = ar.alloc([16], F32)
        self.cft = ar.alloc([16, 20], F32)
        S.add("dve", lambda e: e.memset(self.ones, 1.0), writes=["ones"])
        S.add("dve", lambda e: e.memset(self.epsT, EPS), writes=["eps"])
        S.add("dve", lambda e: e.memset(self.oneT, 1.0), writes=["one"])
        self.dma("sp", self.csil, self.cvec, [], ["csil"], "ld0")
        self.csilb = ar.alloc([16, 2], BF16)
        self.act(self.csilb, self.csil, AF.Silu, ["csil"], ["csilb"])
        ph = self.phases
        h_in = self.xT
        for l in range(self.nl):
            last = (l == self.nl - 1) and not getattr(self, 'force_not_last', False)
            if ph is None or "mod" in ph:
                self.phase_mod(l)
                S.barrier()
            if ph is None or "ffn1" in ph:
                self.ffn_sublayer(l, 0, 0, h_in, self.hT, skip_ctx=False, final=False)
                S.barrier()
            h_in = self.hT
            if ph is None or "mix_in" in ph:
                self.mixer_inproj(l)
                S.barrier()
            if ph is None or "attA" in ph:
                self.attn_a(l, with_ctx=not last)
                S.barrier()
            if ph is None or "attB" in ph:
                self.attn_b(l, with_ctx=not last)
                S.barrier()
            if ph is None or "ret" in ph:
                self.retention(l, with_ctx=not last)
                S.barrier()
            if ph is None or "merge" in ph:
                self.merge(l, skip_ctx=last)
                S.barrier()
            if ph is None or "ffn2" in ph:
                self.ffn_sublayer(l, 1, 2, self.hT, self.outT if last else self.hT, skip_ctx=last, final=last)
                S.barrier()
        S.add("sp", None, reads=list(self.out_keys))

    def phase_mod(self, l):
        ar, S, W = self.ar, self.S, self.W[l]
        ar.push()
        B = 7
        if l not in self.mod_ready:
            wb = [ar.alloc([4, 16, 128], BF16) for _ in range(3)]
            for g in range(36):
                buf = wb[g % 3]
                self.dma("pool", buf, W["wmod"][g], [], [("wm", g % 3)], ("wm", g % 3))
                for mi in range(4):
                    mc = g * 4 + mi
                    for kc in range(16):
                        self.mm(self.ps[:, B * 512 + mc * 2: B * 512 + mc * 2 + 2], buf[:, mi, kc, :],
                                self.csilb[:, kc, :], kc == 0, kc == 15, [("wm", g % 3), "csilb"], [("ps", B)])
        self.dma("sp", self.bmodS, W["bmod"], [], ["bmod"], "ld0")
        self.dma("sp", self.preS, W["pre"], [], ["pre"], "ld1")
        self.dma("sp", self.postS, W["post"], [], ["post"], "ld2")
        if l in self.mod_ready:
            for j in range(2):
                self.tt(self.modT[:, :, j], self.modN[:, :, j], self.bmodS, ALU.add, ["modN", "bmod"], ["modT"])
        else:
            psv = self.bank(B, 288).rearrange("p (a b) -> p a b", b=2)
            for j in range(2):
                self.tt(self.modT[:, :, j], psv[:, :, j], self.bmodS, ALU.add, [("ps", B), "bmod"], ["modT"])
        if self.modout is not None:
            self.dma("sp", self.modout, self.modT, ["modT"], ["modout"], "ld0")
        for s in range(3):
            for j in range(2):
                sc = self.modT[:, (3 * s + 1) * 16:(3 * s + 2) * 16, j]
                gt = self.modT[:, (3 * s + 2) * 16:(3 * s + 3) * 16, j]
                self.stt(self.Avec[:, s, :, j], sc, 1.0, self.preS[:, s, :], ALU.add, ALU.mult,
                         ["modT", "pre"], ["Avec"])
                self.stt(self.Gvec[:, s, :, j], gt, 0.5 if s != 1 else 1.0, self.postS[:, s, :], ALU.mult, ALU.mult,
                         ["modT", "post"], ["Gvec"])
        lam_init = 0.8 - 0.6 * math.exp(-0.3 * l)
        dl = ar.alloc([256], F32)
        pr = ar.alloc([128], F32)
        sm = ar.alloc([2], F32)
        self.dma("sp", dl, W["dlam"], [], ["dl"], "ld0")
        dlv = dl.rearrange("p (a b) -> p a b", b=64)
        prv = pr.rearrange("p (a b) -> p a b", b=64)
        self.tt(prv[:, 0, :], dlv[:, 0, :], dlv[:, 1, :], ALU.mult, ["dl"], ["pr"])
        self.tt(prv[:, 1, :], dlv[:, 2, :], dlv[:, 3, :], ALU.mult, ["dl"], ["pr"])
        S.add("dve", lambda e: e.reduce_sum(out=sm, in_=prv, axis=AX.X), ["pr"], ["sm"])
        self.act(sm, sm, AF.Exp, ["sm"], ["sm"])
        self.stt(self.lamS[:, 0:1], sm[:, 1:2], -lam_init, sm[:, 0:1], ALU.add, ALU.subtract, ["sm"], ["lamS"])
        dn = ar.alloc([2], F32)
        self.dma("sp", dn[:, 0:1], W["dnorm"], [], ["dn"], "ld1")
        self.dma("sp", dn[:, 1:2], W["rnorm"], [], ["dn"], "ld2")
        self.ts(self.dnS[:, 0:1], dn[:, 0:1], 1.0 - lam_init, ALU.mult, ["dn"], ["dnS"])
        self.ts(self.dnS[:, 1:2], dn[:, 1:2], 1.0, ALU.mult, ["dn"], ["dnS"])
        rd = ar.alloc([16], F32)
        self.dma("sp", rd, W["rdecay"], [], ["rd"], "ld0")
        self.act(rd, rd, AF.Exp, ["rd"], ["rd"], scale=-1.0)
        self.act(rd, rd, AF.Ln, ["rd"], ["rd", "one"][:1], bias=self.oneT[:, 0:1])
        self.ts(self.lgS, rd, -1.0, ALU.mult, ["rd"], ["lgS"])
        r128 = ar.alloc([20], F32)
        self.dma("sp", r128, self.rows128, [], ["r128"], "ld1")
        for k in range(16):
            self.act(self.cft[:, k, :], r128, AF.Exp, ["r128", "lgS"], ["cft"], scale=self.lgS[:, k:k + 1])
        ar.pop()

    def mod_overlap_gen(self, ln, wbufs):
        W = self.W[ln]
        modNf = self.modN.rearrange("p a b -> p (a b)")
        for g in range(36 + 2):
            if g < 36:
                self.dma("pool", wbufs[g % 3], W["wmod"][g], [], [("wmo", g % 3)], ("wm", g % 3))
            if g >= 2:
                g2 = g - 2
                buf = wbufs[g2 % 3]
                c0 = 7 * 512 + 384 + (g2 % 8) * 8
                for mi in range(4):
                    for kc in range(16):
                        self.mm(self.ps[:, c0 + mi * 2: c0 + mi * 2 + 2], buf[:, mi, kc, :], self.csilb[:, kc, :],
                                kc == 0, kc == 15, [("wmo", g2 % 3), "csilb"], [("ps", 7)])
                self.S.add("dve", (lambda e, o=modNf[:, g2 * 8:g2 * 8 + 8], i=self.ps[:, c0:c0 + 8]:
                                   e.tensor_copy(out=o, in_=i)), [("ps", 7)], ["modN"])
            yield
        self.mod_ready.add(ln)

    def prep_gen(self, l, sidx, h_src, parts, uT, ucols, statbank, bufs):
        hb, sqb, tmpf, rst = bufs
        if self.rstd_valid and self.fuse_stats:
            for (c0, n, j), uc0 in zip(parts, ucols):
                for ch in range(16):
                    k = ch % 4
                    self.dma("sp", hb[k][:, :n], h_src[ch][:, c0:c0 + n], [("h", ch, c0)], [("hb", k)], ("hb", k))
                    self.tt(tmpf[ch % 2][:, :n], hb[k][:, :n], self.rstdN[:, c0:c0 + n], ALU.mult,
                            [("hb", k), "rstdN"], [("tmpf", ch % 2)])
                    self.act(uT[:, ch, uc0:uc0 + n], tmpf[ch % 2][:, :n], AF.Identity,
                             [("tmpf", ch % 2), "Avec", "modT"], [("uT", ch)],
                             scale=self.Avec[:, sidx, ch, j:j + 1], bias=self.modT[:, 3 * sidx * 16 + ch, j:j + 1])
                    yield
            return
        for (c0, n, j), uc0 in zip(parts, ucols):
            for ch in range(18):
                if ch < 16:
                    k = ch % 4
                    self.dma("sp", hb[k][:, :n], h_src[ch][:, c0:c0 + n], [("h", ch, c0)], [("hb", k)], ("hb", k))
                    self.act(sqb[k][:, :n], hb[k][:, :n], AF.Square, [("hb", k)], [("sqb", k)])
                if ch >= 2:
                    c2 = ch - 2
                    self.mm(self.bank(statbank, n), self.ones, sqb[c2 % 4][:, :n], c2 == 0, c2 == 15,
                            [("sqb", c2 % 4), "ones"], [("ps", statbank)])
                yield
            self.act(rst[:, :n], self.bank(statbank, n), AF.Sqrt, [("ps", statbank), "eps"], ["rst"],
                     scale=1.0 / D, bias=self.epsT[:, 0:1])
            self.recip(rst[:, :n], rst[:, :n], ["rst"], ["rst"])
            for ch in range(16):
                k = ch % 4
                self.dma("sp", hb[k][:, :n], h_src[ch][:, c0:c0 + n], [("h", ch, c0)], [("hb", k)], ("hb", k))
                self.tt(tmpf[ch % 2][:, :n], hb[k][:, :n], rst[:, :n], ALU.mult, [("hb", k), "rst"], [("tmpf", ch % 2)])
                self.act(uT[:, ch, uc0:uc0 + n], tmpf[ch % 2][:, :n], AF.Identity,
                         [("tmpf", ch % 2), "Avec", "modT"], [("uT", ch)],
                         scale=self.Avec[:, sidx, ch, j:j + 1], bias=self.modT[:, 3 * sidx * 16 + ch, j:j + 1])
                yield

    def alloc_prep_bufs(self):
        ar = self.ar
        hb = [ar.alloc([512], F32) for _ in range(4)]
        sqb = [ar.alloc([512], BF16) for _ in range(4)]
        tmpf = [ar.alloc([512], F32) for _ in range(2)]
        rst = ar.alloc([512], F32)
        return hb, sqb, tmpf, rst

    def alloc_post_bufs(self):
        ar = self.ar
        d = {}
        d["ysb"] = [ar.alloc([768], F32) for _ in range(2)]
        d["ysq"] = [ar.alloc([768], BF16) for _ in range(2)]
        d["yl"] = [ar.alloc([768], F32) for _ in range(2)]
        d["hl"] = [ar.alloc([768], F32) for _ in range(2)]
        d["ho"] = [ar.alloc([768], F32) for _ in range(2)]
        d["rstp"] = ar.alloc([768], F32)
        return d

    def post_evac(self, pb, blk, oc, parts, ybanks, statbanks, slot):
        ysb, ysq = pb["ysb"][oc % 2], pb["ysq"][oc % 2]
        off = 0
        for pi, (c0, n, j, on) in enumerate(parts):
            if on:
                yb = self.bank(ybanks[pi], n)
                self.act(ysq[:, off:off + n], yb, AF.Square, [("ps", ybanks[pi])], [("ysq", oc % 2, pi)])
                self.mm(self.bank(statbanks[pi], n), self.ones, ysq[:, off:off + n], oc == 0, oc == 15,
                        [("ysq", oc % 2, pi), "ones"], [("ps", statbanks[pi])])
                self.S.add("dve", (lambda e, o=ysb[:, off:off + n], i=yb: e.tensor_copy(out=o, in_=i)),
                           [("ps", ybanks[pi])], [("ysb", oc % 2, pi)])
            off += n
        ntot = off
        act_parts = [pi for pi, p in enumerate(parts) if p[3]]
        self.dma("sp", self.yscr[slot][oc][:, :ntot], ysb[:, :ntot], [("ysb", oc % 2, pi) for pi in act_parts],
                 [("yscr", slot, oc)], ("ysb", oc % 2))

    def post_gen(self, pb, l, sidx, blk, parts, statbanks, slot, h_in, h_out, final):
        rstp = pb["rstp"]
        off = 0
        offs = []
        for pi, (c0, n, j, on) in enumerate(parts):
            offs.append(off)
            if on:
                self.act(rstp[:, off:off + n], self.bank(statbanks[pi], n), AF.Sqrt, [("ps", statbanks[pi]), "eps"],
                         [("rstp", pi)], scale=1.0 / D, bias=self.epsT[:, 0:1])
                self.recip(rstp[:, off:off + n], rstp[:, off:off + n], [("rstp", pi)], [("rstp", pi)])
            off += n
        ntot = off
        for oc in range(16):
            k = oc % 2
            yl, hl, ho = pb["yl"][k], pb["hl"][k], pb["ho"][k]
            self.dma("sp", yl[:, :ntot], self.yscr[slot][oc][:, :ntot], [("yscr", slot, oc)], [("yl", k)], ("yl", k))
            for pi, (c0, n, j, on) in enumerate(parts):
                if not on:
                    continue
                o = offs[pi]
                self.dma("sp", hl[:, o:o + n], h_in[oc][:, c0:c0 + n], [("h", oc, c0)], [("hl", k, pi)], ("hl", k, pi))
                self.tt(yl[:, o:o + n], yl[:, o:o + n], rstp[:, o:o + n], ALU.mult, [("yl", k), ("rstp", pi)],
                        [("yl", k)])
                self.stt(ho[:, o:o + n], yl[:, o:o + n], self.Gvec[:, sidx, oc, j:j + 1], hl[:, o:o + n],
                         ALU.mult, ALU.add, [("yl", k), ("hl", k, pi), "Gvec"], [("ho", k, pi)])
                if final:
                    lc = lat_col(c0)
                    self.out_keys.append(("OUT", oc, c0))
                    self.dma("sp", h_out[oc][:, lc:lc + n], ho[:, o:o + n], [("ho", k, pi)], [("OUT", oc, c0)],
                             ("ho", k, pi))
                else:
                    self.dma("sp", h_out[oc][:, c0:c0 + n], ho[:, o:o + n], [("ho", k, pi)], [("h", oc, c0)],
                             ("ho", k, pi))
                    if self.fuse_stats:
                        self.act(pb["ysq"][k][:, o:o + n], ho[:, o:o + n], AF.Square, [("ho", k, pi)], [("ysq", k, pi)])
            if self.fuse_stats and not final and oc >= 1:
                self._nstat_mm(pb, parts, offs, statbanks, oc - 1)
            yield
        if self.fuse_stats and not final:
            self._nstat_mm(pb, parts, offs, statbanks, 15)
            for pi, (c0, n, j, on) in enumerate(parts):
                if not on:
                    continue
                self.act(self.rstdN[:, c0:c0 + n], self.bank(statbanks[pi], n), AF.Sqrt, [("ps", statbanks[pi]), "eps"],
                         ["rstdN"], scale=1.0 / D, bias=self.epsT[:, 0:1])
                self.recip(self.rstdN[:, c0:c0 + n], self.rstdN[:, c0:c0 + n], ["rstdN"], ["rstdN"])
            yield

    def _nstat_mm(self, pb, parts, offs, statbanks, oc):
        k = oc % 2
        for pi, (c0, n, j, on) in enumerate(parts):
            if not on:
                continue
            o = offs[pi]
            self.mm(self.bank(statbanks[pi], n), self.ones, pb["ysq"][k][:, o:o + n], oc == 0, oc == 15,
                    [("ysq", k, pi), "ones"], [("ps", statbanks[pi])])

    def post_pass(self, *a):
        for _ in self.post_gen(*a):
            pass

    def ffn_sublayer(self, l, i, sidx, h_in, h_out, skip_ctx, final):
        ar, S, W = self.ar, self.S, self.W[l]
        ar.push()
        uT = ar.alloc([16, 768], BF16)
        gT = ar.alloc([44, 768], BF16)
        wbuf = [ar.alloc([2, 16, 128], BF16) for _ in range(3)]
        wobuf = [ar.alloc([44, 128], BF16) for _ in range(2)]
        sa = [ar.alloc([512], BF16) for _ in range(2)]
        pbufs = self.alloc_prep_bufs()
        pb = self.alloc_post_bufs()

        def parts_of(b):
            return [(c0, n, 1 if part_is_ctx(b, p) else 0, not (skip_ctx and part_is_ctx(b, p)))
                    for p, (c0, n) in enumerate(PARTS[b])]

        def prep_for(b, statbank):
            ps_ = [(c0, n, j) for (c0, n, j, on) in parts_of(b) if on]
            uc = [0 if idx == 0 else 512 for idx, (c0, n, j, on) in enumerate(parts_of(b)) if on]
            return self.prep_gen(l, sidx, h_in, ps_, uT, uc, statbank, pbufs)

        for _ in prep_for(0, 7):
            pass
        wcount = 0
        postg = iter(())
        for b in range(3):
            parts = parts_of(b)
            slot = b % 2
            for fc in range(44):
                next(postg, None)
                wb = wbuf[wcount % 3]
                wk = ("wbuf", wcount % 3)
                self.dma("pool", wb, W["fwin"][i][fc], [], [wk], wk)
                wcount += 1
                st = fc % 2
                banks = (st * 3, st * 3 + 1, st * 3 + 2)
                for half in range(2):
                    for kc in range(16):
                        self.mm(self.bank(banks[half]), wb[:, half, kc, :], uT[:, kc, 0:512], kc == 0, kc == 15,
                                [wk, ("uT", kc)], [("ps", banks[half])])
                if parts[1][3]:
                    for half in range(2):
                        for kc in range(16):
                            self.mm(self.bank(banks[2], 256, half * 256), wb[:, half, kc, :], uT[:, kc, 512:768],
                                    kc == 0, kc == 15, [wk, ("uT", kc)], [("ps", banks[2])])
                self.act(sa[0], self.bank(banks[0]), AF.Silu, [("ps", banks[0])], [("sa", 0)])
                self.tt(gT[:, fc, 0:512], sa[0], self.bank(banks[1]), ALU.mult, [("sa", 0), ("ps", banks[1])],
                        [("gT", fc)])
                if parts[1][3]:
                    self.act(sa[1][:, :256], self.bank(banks[2], 256, 0), AF.Silu, [("ps", banks[2])], [("sa", 1)])
                    self.tt(gT[:, fc, 512:768], sa[1][:, :256], self.bank(banks[2], 256, 256), ALU.mult,
                            [("sa", 1), ("ps", banks[2])], [("gT", fc)])
            pg = prep_for(b + 1, 5) if b < 2 else iter(())
            for oc in range(16):
                for _ in range(5):
                    next(pg, None)
                wo = wobuf[oc % 2]
                wok = ("wobuf", oc % 2)
                self.dma("pool", wo, W["fwout"][i][oc], [], [wok], wok)
                st = oc % 2
                ybanks = (st * 2, st * 2 + 1)
                for pi, (c0, n, j, on) in enumerate(parts):
                    if not on:
                        continue
                    u0 = 0 if pi == 0 else 512
                    for fc in range(44):
                        self.mm(self.bank(ybanks[pi], n), wo[:, fc, :], gT[:, fc, u0:u0 + n], fc == 0, fc == 43,
                                [wok, ("gT", fc)], [("ps", ybanks[pi])])
                self.post_evac(pb, b, oc, parts, ybanks, (6, 7), slot)
            for _ in pg:
                pass
            for _ in postg:
                pass
            postg = self.post_gen(pb, l, sidx, b, parts, (6, 7), slot, h_in, h_out, final)
        for _ in postg:
            pass
        self.rstd_valid = not final
        ar.pop()

    def mixer_inproj(self, l):
        ar, S, W = self.ar, self.S, self.W[l]
        ar.push()
        uT = ar.alloc([16, T], BF16)
        wbuf = [ar.alloc([8192], BF16) for _ in range(2)]
        cbuf = [ar.alloc([T], BF16) for _ in range(4)]
        vbuf = [ar.alloc([512], BF16) for _ in range(3)]
        cosS = ar.alloc([2048], F32)
        sinS = ar.alloc([2048], F32)
        pm = ar.alloc([128], F32)
        tsb = [ar.alloc([512], F32) for _ in range(2)]
        ra = [ar.alloc([512], F32) for _ in range(2)]
        rb = [ar.alloc([512], F32) for _ in range(2)]
        pbufs = self.alloc_prep_bufs()
        self.dma("sp", cosS, self.cosT, [], ["cos"], "ld0")
        self.dma("sp", sinS, self.sinT, [], ["sin"], "ld1")
        self.dma("sp", pm, self.permM, [], ["pm"], "ld2")
        allparts = [(0, 512, 0), (512, 256, 1), (768, 512, 0), (1280, 512, 0), (1792, 512, 0)]
        for _ in self.prep_gen(l, 1, self.hT, allparts, uT, [p[0] for p in allparts], 7, pbufs):
            pass
        fparts = [(0, 512, 0), (512, 256, None), (768, 512, 4), (1280, 512, 8), (1792, 512, 12)]
        ccount = 0
        ecount = 0
        vcount = 0
        rcount = 0
        deferred = []
        for g in range(30):
            wb = wbuf[g % 2]
            wk = ("wbuf", g % 2)
            self.dma("pool", wb, W["win"][g], [], [wk], wk)
            if g in (4, 5, 10, 11, 14, 15):
                dst = {4: self.va, 5: self.va, 10: self.vb, 11: self.vb, 14: self.vc, 15: self.vc}[g]
                hoff = (g % 2) * 512
                wv = wb.rearrange("p (k c) -> p k c", c=512)
                for tt_ in range(NT):
                    bk = ecount % 4
                    ecount += 1
                    for kc in range(16):
                        self.mm(self.bank(bk), uT[:, kc, tt_ * 128:(tt_ + 1) * 128], wv[:, kc, :], kc == 0, kc == 15,
                                [wk, ("uT", kc)], [("ps", bk)])
                    vb_ = vbuf[vcount % 3]
                    vk = ("vbuf", vcount % 3)
                    vcount += 1
                    self.act(vb_, self.bank(bk), AF.Copy, [("ps", bk)], [vk])
                    self.dma("sp", dst[tt_][:, hoff:hoff + 512], vb_, [vk], [("vdst", g, tt_)], vk)
                continue
            wf = wb.rearrange("p (c k f) -> p c k f", k=16, f=128)
            for ci in range(4):
                cc = g * 4 + ci
                if g < 2:
                    dst, mode, sc = self.qaT[cc], "rope", 1.0
                elif g < 4:
                    dst, mode, sc = self.kaT[cc - 8], "rope", 1.0
                elif g < 8:
                    dst, mode, sc = self.qbT[cc - 24], "copy", 1.0
                elif g < 10:
                    dst, mode, sc = self.kbT[cc - 32], "copy", 1.0
                elif g == 12:
                    dst, mode, sc = self.qcT[cc - 48], "rope", 1.0
                elif g == 13:
                    dst, mode, sc = self.kcT[cc - 52], "rope", 0.125
                elif g < 18:
                    dst, mode, sc = self.cgT[cc - 64], "silu", 1.0
                else:
                    dst, mode, sc = self.gT[cc - 72], "sigmoid", 1.0
                cb = cbuf[ccount % 4]
                ck = ("cbuf", ccount % 4)
                ccount += 1
                for (c0, n, lt0) in fparts:
                    bk = ecount % 4
                    ecount += 1
                    for kc in range(16):
                        self.mm(self.bank(bk, n), wf[:, ci, kc, :], uT[:, kc, c0:c0 + n], kc == 0, kc == 15,
                                [wk, ("uT", kc)], [("ps", bk)])
                    for fnc in deferred:
                        fnc()
                    deferred = []
                    if mode == "rope" and lt0 is not None:
                        r = rcount % 2
                        rcount += 1
                        self.act(tsb[r], self.bank(bk), AF.Copy, [("ps", bk)], [("tsb", r)], scale=sc)

                        def tail(r=r, t0=lt0 * 128, cb=cb, c0=c0, ck=ck):
                            rbk = 4 + r
                            self.mm(self.bank(rbk), pm, tsb[r], True, True, [("tsb", r), "pm"], [("ps", rbk)])
                            self.tt(ra[r], tsb[r], cosS[:, t0:t0 + 512], ALU.mult, [("tsb", r), "cos"], [("ra", r)])
                            self.tt(rb[r], self.bank(rbk), sinS[:, t0:t0 + 512], ALU.mult, [("ps", rbk), "sin"],
                                    [("rb", r)])
                            self.tt(cb[:, c0:c0 + 512], ra[r], rb[r], ALU.add, [("ra", r), ("rb", r)], [ck])
                        deferred.append(tail)
                    else:
                        fn = {"rope": AF.Copy, "copy": AF.Copy, "silu": AF.Silu, "sigmoid": AF.Sigmoid}[mode]
                        self.act(cb[:, c0:c0 + n], self.bank(bk, n), fn, [("ps", bk)], [ck], scale=sc)
                deferred.append(lambda dst=dst, cb=cb, ck=ck, cc=cc: self.dma("sp", dst, cb, [ck], [("fdst", cc)], ck))
            for fnc in deferred:
                fnc()
            deferred = []
        ar.pop()

    def attn_a(self, l, with_ctx):
        ar, S, W = self.ar, self.S, self.W[l]
        ar.push()
        q1 = [ar.alloc([T], BF16) for _ in range(2)]
        q2 = [ar.alloc([T], BF16) for _ in range(2)]
        kS = [ar.alloc([T], BF16) for _ in range(2)]
        vS = [ar.alloc([NT, 129], BF16) for _ in range(2)]
        E12 = [ar.alloc([2, 512], BF16) for _ in range(3)]
        E1 = [e[:, 0, :] for e in E12]
        E2 = [e[:, 1, :] for e in E12]
        cacc = [ar.alloc([3, 512], F32) for _ in range(2)]
        ident = ar.alloc([128], BF16)
        dnrow = ar.alloc([128], F32)
        sm = [ar.alloc([8], F32) for _ in range(2)]
        t1 = [ar.alloc([128], F32) for _ in range(2)]
        o_ = [ar.alloc([128], F32) for _ in range(2)]
        junkf = [ar.alloc([128], F32) for _ in range(2)]
        onb = [ar.alloc([128], BF16) for _ in range(2)]
        obuf = [ar.alloc([T], BF16) for _ in range(2)]
        self.dma("pool", ident, self.identM, [], ["ident"], "ld0")
        self.dma("sp", dnrow, W["dnrow"], [], ["dnrow"], "ld1")
        self.ts(dnrow, dnrow, 1.0 - (0.8 - 0.6 * math.exp(-0.3 * l)), ALU.mult, ["dnrow"], ["dnrow"])
        for i in range(2):
            S.add("dve", (lambda e, a=q1[i][64:128, :]: e.memset(a, 0.0)), writes=[("q1z", i)])
            S.add("dve", (lambda e, a=q2[i][0:64, :]: e.memset(a, 0.0)), writes=[("q2z", i)])
            S.add("dve", (lambda e, a=vS[i][:, :, 128:129]: e.memset(a, 1.0)), writes=[("vSo", i)])
        qblocks = [(c0, n, list(range(NT))) for (c0, n, _) in QBLK]
        if with_ctx:
            qblocks.append((CTX_COL0, 256, list(CTX_TILES)))
        slots1 = [(4, 0), (4, 129), (4, 258), (5, 0)]
        slots2 = [(5, 129), (5, 258), (6, 0), (6, 129)]
        scnt = 0
        ecnt = 0
        fcnt = 0
        tcnt = 0
        ps7b = self.bank(7).bitcast(BF16)
        tcnt_box = [0]

        def fin_gen(cb, c0, nq, ob, hb):
            ca = cacc[cb]
            ck = [("cacc", cb, 0), ("cacc", cb, 1), ("cacc", cb, 2)]
            for qt in range(nq):
                tb = tcnt_box[0] % 2
                tcnt_box[0] += 1
                b1, c1 = slots1[qt]
                b2, c2 = slots2[qt]
                A1 = ca[:, b1 - 4, c1:c1 + 129]
                A2 = ca[:, b2 - 4, c2:c2 + 129]
                smt = sm[tb]
                smk = ("sm", tb)
                self.recip(smt[:, 0:1], A1[:, 128:129], ck, [smk])
                self.recip(smt[:, 1:2], A2[:, 128:129], ck, [smk])
                self.tt(smt[:, 2:3], smt[:, 1:2], self.lamS[:, 0:1], ALU.mult, [smk, "lamS"], [smk])
                yield
                self.ts(t1[tb], A1[:, 0:128], smt[:, 0:1], ALU.mult, ck + [smk], [("t1", tb)])
                self.stt(o_[tb], A2[:, 0:128], smt[:, 2:3], t1[tb], ALU.mult, ALU.add, ck + [smk, ("t1", tb)],
                         [("o_", tb)])
                yield
                self.S.add("dve", (lambda e, o=junkf[tb], i=o_[tb], a=smt[:, 3:4]:
                                   e.scalar_tensor_tensor(out=o, in0=i, scalar=1.0, in1=i, op0=ALU.mult, op1=ALU.mult,
                                                          accum_out=a)),
                           [("o_", tb)], [("junk", tb), smk])
                yield
                self.act(smt[:, 4:5], smt[:, 3:4], AF.Ln, [smk, "eps"], [smk], scale=1.0 / 128,
                         bias=self.epsT[:, 0:1])
                self.act(smt[:, 5:6], smt[:, 4:5], AF.Exp, [smk], [smk], scale=-0.5)
                self.stt(onb[tb], o_[tb], smt[:, 5:6], dnrow, ALU.mult, ALU.mult, [("o_", tb), smk, "dnrow"],
                         [("onb", tb)])
                yield
                tv = ps7b[:, tb * 128:(tb + 1) * 128]
                self.S.add("pe", (lambda e, o=tv, i=onb[tb]: e.transpose(o, i, ident)), [("onb", tb), "ident"],
                           [("ps", 7)])
                yield
                qc = c0 + qt * 128
                self.S.add("dve", (lambda e, o=ob[:, qc:qc + 128], i=tv: e.tensor_copy(out=o, in_=i)),
                           [("ps", 7)], [("obuf", hb)])
                yield

        fin = iter(())
        modg = iter(())
        if l + 1 < self.nl and self.mod_overlap:
            mwb = [ar.alloc([4, 16, 128], BF16) for _ in range(3)]
            modg = self.mod_overlap_gen(l + 1, mwb)
        stepc = 0
        for h in range(8):
            hb = h % 2
            self.dma("sp", q1[hb][0:64, :], self.qaT[h][0:64, :], [], [("q1", hb)], ("q1", hb))
            self.dma("sp", q2[hb][64:128, :], self.qaT[h][64:128, :], [], [("q2", hb)], ("q2", hb))
            self.dma("sp", kS[hb], self.kaT[h], [], [("kS", hb)], ("kS", hb))
            self.dma("sp", vS[hb][:, :, 0:128], self.va[:, :, h * 128:(h + 1) * 128].rearrange("t p d -> p t d"),
                     [], [("vS", hb)], ("vS", hb))
            ob = obuf[hb]
            for (c0, n, ktl) in qblocks:
                nq = n // 128
                nk = len(ktl)
                pend = None
                first_in_bank = {}
                for idx in range(nk + 1):
                    cur = None
                    if idx < nk:
                        kt = ktl[idx]
                        sb = scnt % 2
                        scnt += 1
                        b1_, b2_ = 2 * sb, 2 * sb + 1
                        self.mm(self.bank(b1_, n), kS[hb][:, kt * 128:(kt + 1) * 128], q1[hb][:, c0:c0 + n],
                                True, True, [("kS", hb), ("q1", hb), ("q1z", hb)], [("ps", b1_)])
                        self.mm(self.bank(b2_, n), kS[hb][:, kt * 128:(kt + 1) * 128], q2[hb][:, c0:c0 + n],
                                True, True, [("kS", hb), ("q2", hb), ("q2z", hb)], [("ps", b2_)])
                        eb = ecnt % 3
                        ecnt += 1
                        src2 = self.ps[:, b1_ * 512:(b1_ + 2) * 512].rearrange("p (b c) -> p b c", c=512)[:, :, 0:n]
                        self.act(E12[eb][:, :, 0:n], src2, AF.Exp, [("ps", b1_), ("ps", b2_)],
                                 [("E1", eb), ("E2", eb)], scale=0.125)
                        cur = (kt, eb, idx)
                    if pend is not None:
                        kt_, eb_, i_ = pend
                        for (Eb, slots, ek) in ((E1[eb_], slots1, ("E1", eb_)), (E2[eb_], slots2, ("E2", eb_))):
                            for qt in range(nq):
                                bk, co = slots[qt]
                                st = (i_ == 0) and (bk not in first_in_bank)
                                if i_ == 0:
                                    first_in_bank[bk] = True
                                self.S.add("pe", (lambda e, o=self.bank(bk, 129, co), lt=Eb[:, qt * 128:(qt + 1) * 128],
                                                  r=vS[hb][:, kt_, :], st=st, sp=(i_ == nk - 1):
                                                  e.matmul(o, lt, r, start=st, stop=sp, skip_group_check=True)),
                                           [("vS", hb), ("vSo", hb), ek], [("ps", bk)])
                    pend = cur
                    next(fin, None)
                    next(fin, None)
                    stepc += 1
                    if stepc % 14 == 0:
                        next(modg, None)
                for _ in fin:
                    pass
                cb = fcnt % 2
                fcnt += 1
                ca = cacc[cb]
                for bi, bk in enumerate((4, 5, 6)):
                    if bi == 2 and nq < 3:
                        continue
                    self.S.add("dve", (lambda e, o=ca[:, bi, :], i=self.bank(bk): e.tensor_copy(out=o, in_=i)),
                               [("ps", bk)], [("cacc", cb, bi)])
                fin = fin_gen(cb, c0, nq, ob, hb)
            for _ in fin:
                pass
            fin = iter(())
            if with_ctx:
                self.dma("sp", self.oaT[h], ob, [("obuf", hb)], [("oaT", h)], ("obuf", hb))
            else:
                self.dma("sp", self.oaT[h][:, 0:512], ob[:, 0:512], [("obuf", hb)], [("oaT", h)], ("obuf", hb))
                self.dma("sp", self.oaT[h][:, 768:T], ob[:, 768:T], [("obuf", hb)], [("oaT", h)], ("obuf", hb))
        for _ in modg:
            pass
        ar.pop()

    def attn_b(self, l, with_ctx):
        ar, S, W = self.ar, self.S, self.W[l]
        ar.push()
        qS = [ar.alloc([T], BF16) for _ in range(2)]
        kS = [ar.alloc([T], BF16) for _ in range(2)]
        vS = [ar.alloc([NT, 128], BF16) for _ in range(2)]
        bias = [ar.alloc([25, 128], F32) for _ in range(2)]
        pre = [ar.alloc([5, 128], F32) for _ in range(2)]
        E = [ar.alloc([7, 128], BF16) for _ in range(2)]
        rr = [ar.alloc([256], F32) for _ in range(2)]
        rsc = [ar.alloc([256], F32) for _ in range(2)]
        obuf = [ar.alloc([T], BF16) for _ in range(2)]
        scale = 128 ** -0.5
        steps = []

        def loads(h):
            hb = h % 2
            self.dma("sp", qS[hb], self.qbT[h], [], [("qS", hb)], ("qS", hb))
            self.dma("sp", kS[hb], self.kbT[h], [], [("kS", hb)], ("kS", hb))
            self.dma("sp", vS[hb], self.vb[:, :, h * 128:(h + 1) * 128].rearrange("t p d -> p t d"),
                     [], [("vS", hb)], ("vS", hb))
            self.dma("sp", bias[hb], W["rpbx"][h], [], [("bias", hb)], ("bias", hb))

        def front(h, i, s2):
            hb = h % 2
            if i == 0:
                loads(h)
            qc0 = lat_T(i) * 128
            jl = nb_jlist(i)
            cls = nb_cls(i)
            nl_ = len(jl)
            tiles = [lat_T(j) for j in jl] + list(CTX_TILES)
            for sl, kt in enumerate(tiles):
                bk = s2 * 2 + (0 if sl < 4 else 1)
                cc = (sl if sl < 4 else sl - 4) * 128
                self.mm(self.bank(bk, 128, cc), kS[hb][:, kt * 128:(kt + 1) * 128], qS[hb][:, qc0:qc0 + 128],
                        True, True, [("kS", hb), ("qS", hb)], [("ps", bk)])
            b0 = self.bank(s2 * 2, 512).rearrange("p (a b) -> p a b", b=128)
            b1 = self.bank(s2 * 2 + 1, 512).rearrange("p (a b) -> p a b", b=128)
            n0 = min(nl_, 4)
            self.stt(pre[s2][:, 0:n0, :], b0[:, 0:n0, :], scale, bias[hb][:, cls * 5:cls * 5 + n0, :], ALU.mult, ALU.add,
                     [("ps", s2 * 2), ("bias", hb)], [("pre", s2)])
            if nl_ > 4:
                self.stt(pre[s2][:, 4:5, :], b1[:, 0:1, :], scale, bias[hb][:, cls * 5 + 4:cls * 5 + 5, :], ALU.mult,
                         ALU.add, [("ps", s2 * 2 + 1), ("bias", hb)], [("pre", s2)])
            self.act(E[s2][:, 0:nl_, :], pre[s2][:, 0:nl_, :], AF.Exp, [("pre", s2)], [("E", s2)])
            if nl_ <= 4:
                for ci in range(2):
                    sl = nl_ + ci
                    src = b0[:, sl:sl + 1, :] if sl < 4 else b1[:, sl - 4:sl - 3, :]
                    bkk = s2 * 2 + (0 if sl < 4 else 1)
                    self.act(E[s2][:, sl:sl + 1, :], src, AF.Exp, [("ps", bkk)], [("E", s2)], scale=scale)
            else:
                self.act(E[s2][:, 5:7, :], b1[:, 1:3, :], AF.Exp, [("ps", s2 * 2 + 1)], [("E", s2)], scale=scale)

        def back(h, i, s2):
            hb = h % 2
            ob = obuf[hb]
            qc0 = lat_T(i) * 128
            tiles = [lat_T(j) for j in nb_jlist(i)] + list(CTX_TILES)
            ns = len(tiles)
            ob_ = 4 + s2
            db_ = 6 + s2
            for sl, kt in enumerate(tiles):
                self.mm(self.bank(ob_, 128), vS[hb][:, kt, :], E[s2][:, sl, :], sl == 0, sl == ns - 1,
                        [("vS", hb), ("E", s2)], [("ps", ob_)])
            for sl, kt in enumerate(tiles):
                self.mm(self.bank(db_, 128), self.ones, E[s2][:, sl, :], sl == 0, sl == ns - 1,
                        ["ones", ("E", s2)], [("ps", db_)])
            self.rcp(rr[s2][:, :128], self.bank(db_, 128), rsc[s2][:, :128], [("ps", db_)], [("rr", s2)])
            self.tt(ob[:, qc0:qc0 + 128], self.bank(ob_, 128), rr[s2][:, :128], ALU.mult, [("ps", ob_), ("rr", s2)],
                    [("obuf", hb)])
            if i == 15 and not with_ctx:
                self.dma("sp", self.obT[h][:, 0:512], ob[:, 0:512], [("obuf", hb)], [("obT", h)], ("obuf", hb))
                self.dma("sp", self.obT[h][:, 768:T], ob[:, 768:T], [("obuf", hb)], [("obT", h)], ("obuf", hb))

        def front_c(h, s2):
            hb = h % 2
            n = 256
            for ci, kt in enumerate(CTX_TILES):
                self.mm(self.bank(s2 * 2 + ci, n), kS[hb][:, kt * 128:(kt + 1) * 128], qS[hb][:, CTX_COL0:CTX_COL0 + n],
                        True, True, [("kS", hb), ("qS", hb)], [("ps", s2 * 2 + ci)])
            Ev = E[s2].rearrange("p a b -> p (a b)")
            for ci in range(2):
                self.act(Ev[:, ci * 256:(ci + 1) * 256], self.bank(s2 * 2 + ci, n), AF.Exp, [("ps", s2 * 2 + ci)],
                         [("E", s2)], scale=scale)

        def back_c(h, s2):
            hb = h % 2
            ob = obuf[hb]
            n = 256
            Ev = E[s2].rearrange("p a b -> p (a b)")
            ob_ = 4 + s2
            db_ = 6 + s2
            for ci, kt in enumerate(CTX_TILES):
                self.mm(self.bank(ob_, n), vS[hb][:, kt, :], Ev[:, ci * 256:(ci + 1) * 256], ci == 0, ci == 1,
                        [("vS", hb), ("E", s2)], [("ps", ob_)])
            for ci, kt in enumerate(CTX_TILES):
                self.mm(self.bank(db_, n), self.ones, Ev[:, ci * 256:(ci + 1) * 256], ci == 0, ci == 1,
                        ["ones", ("E", s2)], [("ps", db_)])
            self.rcp(rr[s2][:, :n], self.bank(db_, n), rsc[s2][:, :n], [("ps", db_)], [("rr", s2)])
            self.tt(ob[:, CTX_COL0:CTX_COL0 + n], self.bank(ob_, n), rr[s2][:, :n], ALU.mult,
                    [("ps", ob_), ("rr", s2)], [("obuf", hb)])
            self.dma("sp", self.obT[h], ob, [("obuf", hb)], [("obT", h)], ("obuf", hb))

        cnt = 0
        for h in range(8):
            for i in range(16):
                s2 = cnt % 2
                cnt += 1
                steps.append((lambda h=h, i=i, s2=s2: front(h, i, s2), lambda h=h, i=i, s2=s2: back(h, i, s2)))
            if with_ctx:
                s2 = cnt % 2
                cnt += 1
                steps.append((lambda h=h, s2=s2: front_c(h, s2), lambda h=h, s2=s2: back_c(h, s2)))
        prev = None
        for fr, bk in steps:
            fr()
            if prev is not None:
                prev()
            prev = bk
        prev()
        ar.pop()

    def retention(self, l, with_ctx):
        ar, S = self.ar, self.S
        ar.push()
        qP = [[ar.alloc([T], BF16) for _ in range(2)] for _ in range(2)]
        kS = [ar.alloc([T], BF16) for _ in range(2)]
        vS = [ar.alloc([NT, 256], BF16) for _ in range(2)]
        for i in range(2):
            S.add("dve", (lambda e, a=qP[0][i][64:128, :]: e.memset(a, 0.0)), writes=[("qz", 0, i)])
            S.add("dve", (lambda e, a=qP[1][i][0:64, :]: e.memset(a, 0.0)), writes=[("qz", 1, i)])
        cg = [ar.alloc([T], BF16) for _ in range(2)]
        rz = ar.alloc([10, 512], F32)
        Dm = [ar.alloc([6, 512], F32) for _ in range(2)]
        dtmp = ar.alloc([512], F32)
        dctx = [ar.alloc([512], F32) for _ in range(2)]
        dctx2 = [ar.alloc([512], F32) for _ in range(2)]
        ssb = [ar.alloc([512], F32) for _ in range(2)]
        rs2 = ar.alloc([512], F32)
        rsc = ar.alloc([512], F32)
        att = [ar.alloc([512], BF16) for _ in range(4)]
        osb = ar.alloc([512], F32)
        osq = ar.alloc([512], BF16)
        rs = ar.alloc([512], F32)
        ybuf = [ar.alloc([T], BF16) for _ in range(2)]
        self.dma("sp", rz, self.retz, [], ["rz"], "ld0")
        qblocks = [(c0, n, j0, False) for (c0, n, j0) in QBLK]
        if with_ctx:
            qblocks.append((CTX_COL0, 256, 0, True))
        scnt = 0
        acnt = 0
        hcnt = 0
        osbs = [ar.alloc([512], F32) for _ in range(2)]
        osqs = [ar.alloc([512], BF16) for _ in range(2)]
        rfc = [0]

        def ret_fin(fb, n, hh, yb, c0, hb):
            yield
            yield
            self.mm(self.bank(6 + hh, n), self.ones, osqs[fb][:, :n], True, True, [("osq", fb), "ones"], [("ps", 6 + hh)])
            yield
            self.act(rs[:, :n], self.bank(6 + hh, n), AF.Ln, [("ps", 6 + hh), "eps"], ["rs"], scale=1.0 / 128,
                     bias=self.epsT[:, 0:1])
            self.act(rs2[:, :n], rs[:, :n], AF.Exp, ["rs"], ["rs2"], scale=-0.5)
            yield
            yield
            self.tt(osbs[fb][:, :n], osbs[fb][:, :n], rs2[:, :n], ALU.mult, [("osb", fb), "rs2"], [("osb", fb)])
            yield
            self.tt(yb[:, c0:c0 + n], osbs[fb][:, :n], cg[hb][:, c0:c0 + n], ALU.mult, [("osb", fb), ("cg", hb)],
                    [("ybuf", hb)])
            yield

        rfin = iter(())
        for m in range(4):
            mb = m % 2
            self.dma("sp", qP[0][mb][0:64, :], self.qcT[m][0:64, :], [], [("qS", 0, mb)], ("qS", 0, mb))
            self.dma("sp", qP[1][mb][64:128, :], self.qcT[m][64:128, :], [], [("qS", 1, mb)], ("qS", 1, mb))
            self.dma("sp", kS[mb], self.kcT[m], [], [("kS", mb)], ("kS", mb))
            self.dma("sp", vS[mb], self.vc[:, :, m * 256:(m + 1) * 256].rearrange("t p d -> p t d"),
                     [], [("vS", mb)], ("vS", mb))
            for hh in range(2):
                h = 2 * m + hh
                pb0 = hh * 64
                hb = hcnt % 2
                hcnt += 1
                self.dma("sp", cg[hb], self.cgT[h], [], [("cg", hb)], ("cg", hb))
                lgf = self.lgS[:, h:h + 1]
                lgb = self.lgS[:, 8 + h:9 + h]
                dm = Dm[hb]
                dk = ("Dm", hb)
                self.act(dm[:, 0, :], rz[:, 0, :], AF.Exp, ["rz", "lgS"], [dk], scale=lgf)
                self.act(dm[:, 1, :], rz[:, 1, :], AF.Exp, ["rz", "lgS"], [dk], scale=lgb)
                for a in range(4):
                    self.act(dm[:, 2 + a, :], rz[:, 2 + a, :], AF.Exp, ["rz", "lgS"], [dk], scale=lgf)
                    self.act(dtmp, rz[:, 6 + a, :], AF.Exp, ["rz", "lgS"], ["dtmp"], scale=lgb)
                    self.tt(dm[:, 2 + a, :], dm[:, 2 + a, :], dtmp, ALU.add, [dk, "dtmp"], [dk])
                yb = ybuf[hb]
                for (c0, n, j0, isctx) in qblocks:
                    if isctx:
                        ktl = [("cc", c) for c in range(2)]
                    else:
                        ktl = [("l", j) for j in range(16)] + [("c", c) for c in range(2)]
                    nk = len(ktl)
                    ob_ = 4 + (scnt // 100000) % 1
                    obank = 4 + hh
                    pendq = []
                    LAG = 3
                    for idx in range(nk + LAG):
                        cur = None
                        if idx < nk:
                            kind, jk = ktl[idx]
                            kt = lat_T(jk) if kind == "l" else CTX_TILES[jk]
                            sb = scnt % 4
                            scnt += 1
                            self.mm(self.bank(sb, n), kS[mb][:, kt * 128:(kt + 1) * 128],
                                    qP[hh][mb][:, c0:c0 + n], True, True,
                                    [("kS", mb), ("qS", hh, mb), ("qz", hh, mb)], [("ps", sb)])
                            ab = acnt % 4
                            acnt += 1
                            ak = ("att", ab)
                            sps = self.bank(sb, n)
                            alt = (kind == "l") and (acnt % 2 == 1)
                            if alt:
                                sf = ssb[(acnt // 2) % 2]
                                sfk = ("ssb", (acnt // 2) % 2)
                                if jk < j0:
                                    self.act(sf[:, :n], sps, AF.Copy, [("ps", sb), "cft"], [sfk],
                                             scale=self.cft[:, h, j0 - jk:j0 - jk + 1])
                                    dsel = dm[:, 0, :n]
                                elif jk > j0 + 3:
                                    self.act(sf[:, :n], sps, AF.Copy, [("ps", sb), "cft"], [sfk],
                                             scale=self.cft[:, 8 + h, jk - j0 - 4:jk - j0 - 3])
                                    dsel = dm[:, 1, :n]
                                else:
                                    self.act(sf[:, :n], sps, AF.Copy, [("ps", sb)], [sfk])
                                    dsel = dm[:, 2 + jk - j0, :n]
                                self.tt(att[ab][:, :n], sf[:, :n], dsel, ALU.mult, [sfk, dk], [ak], eng="pool")
                            elif kind == "l":
                                if jk < j0:
                                    self.stt(att[ab][:, :n], sps, self.cft[:, h, j0 - jk:j0 - jk + 1], dm[:, 0, :n],
                                             ALU.mult, ALU.mult, [("ps", sb), dk, "cft"], [ak])
                                elif jk > j0 + 3:
                                    self.stt(att[ab][:, :n], sps, self.cft[:, 8 + h, jk - j0 - 4:jk - j0 - 3], dm[:, 1, :n],
                                             ALU.mult, ALU.mult, [("ps", sb), dk, "cft"], [ak])
                                else:
                                    self.tt(att[ab][:, :n], sps, dm[:, 2 + jk - j0, :n], ALU.mult, [("ps", sb), dk], [ak])
                            elif kind == "c":
                                dc = dctx[acnt % 2]
                                dck = ("dctx", acnt % 2)
                                i1 = 2 + j0 - jk
                                i2 = 12 + jk - j0
                                dc2 = dctx2[acnt % 2]
                                self.act(dc, dm[:, 0, :], AF.Copy, [dk, "cft"], [dck], scale=self.cft[:, h, i1:i1 + 1])
                                self.act(dc2, dm[:, 1, :], AF.Copy, [dk, "cft"], [("dctx2", acnt % 2)],
                                         scale=self.cft[:, 8 + h, i2:i2 + 1])
                                self.tt(dc, dc, dc2, ALU.add, [dck, ("dctx2", acnt % 2)], [dck], eng="pool")
                                self.tt(att[ab][:, :n], sps, dc[:, :n], ALU.mult, [("ps", sb), dck], [ak])
                            else:
                                self.tt(att[ab][:, :n], sps, dm[:, 2 + jk, :n], ALU.mult, [("ps", sb), dk], [ak])
                            cur = (kt, ab, idx)
                        next(rfin, None)
                        if cur is not None:
                            pendq.append(cur)
                        if idx >= LAG and pendq:
                            kt_, ab_, i_ = pendq.pop(0)
                            self.mm(self.bank(obank, n), vS[mb][:, kt_, hh * 128:(hh + 1) * 128], att[ab_][:, :n],
                                    i_ == 0, i_ == nk - 1, [("vS", mb), ("att", ab_)], [("ps", obank)])
                    for _ in rfin:
                        pass
                    fb = rfc[0] % 2
                    rfc[0] += 1
                    self.act(osbs[fb][:, :n], self.bank(obank, n), AF.Copy, [("ps", obank), "dnS"], [("osb", fb)],
                             scale=self.dnS[:, 1:2])
                    self.act(osqs[fb][:, :n], self.bank(obank, n), AF.Square, [("ps", obank)], [("osq", fb)])
                    rfin = ret_fin(fb, n, hh, yb, c0, hb)
                for _ in rfin:
                    pass
                rfin = iter(())
                if with_ctx:
                    self.dma("sp", self.yrT[h], yb, [("ybuf", hb)], [("yrT", h)], ("ybuf", hb))
                else:
                    self.dma("sp", self.yrT[h][:, 0:512], yb[:, 0:512], [("ybuf", hb)], [("yrT", h)], ("ybuf", hb))
                    self.dma("sp", self.yrT[h][:, 768:T], yb[:, 768:T], [("ybuf", hb)], [("yrT", h)], ("ybuf", hb))
        ar.pop()

    def merge(self, l, skip_ctx):
        ar, S, W = self.ar, self.S, self.W[l]
        ar.push()
        obr = [ar.alloc([8, 768], BF16) for _ in range(3)]
        mT = ar.alloc([16, 768], BF16)
        wbr = [ar.alloc([3, 8, 128], BF16) for _ in range(3)]
        wo = [ar.alloc([16, 128], BF16) for _ in range(2)]
        gb = [ar.alloc([3, 768], BF16) for _ in range(2)]
        m1 = [ar.alloc([512], F32) for _ in range(2)]
        m2 = [ar.alloc([512], F32) for _ in range(2)]
        m3 = [ar.alloc([512], F32) for _ in range(2)]
        pb = self.alloc_post_bufs()
        srcs = (self.oaT, self.obT, self.yrT)
        wc = 0
        postg = iter(())
        for b in range(3):
            parts = [(c0, n, 1 if part_is_ctx(b, p) else 0, not (skip_ctx and part_is_ctx(b, p)))
                     for p, (c0, n) in enumerate(PARTS[b])]
            bc0 = b * 768
            slot = b % 2
            for br in range(3):
                self.dma("sp", obr[br], srcs[br][:, :, bc0:bc0 + 768].rearrange("h p t -> p h t"), [], [("obr", br)],
                         ("obr", br))
            for oc in range(16):
                next(postg, None)
                next(postg, None)
                wb_ = wbr[wc % 3]
                wk = ("wbr", wc % 3)
                wc += 1
                self.dma("pool", wb_, W["wbr"][oc], [], [wk], wk)
                g_ = gb[oc % 2]
                gk = ("gb", oc % 2)
                for br in range(3):
                    self.dma("sp", g_[:, br, :], self.gT[br * 16 + oc][:, bc0:bc0 + 768], [], [(gk, br)], (gk, br))
                for pi, (c0, n, j, on) in enumerate(parts):
                    if not on:
                        continue
                    u0 = 0 if pi == 0 else 512
                    st = (oc * 2 + pi) % 2
                    for br in range(3):
                        bk = st * 3 + br
                        for kc in range(8):
                            self.mm(self.bank(bk, n), wb_[:, br, kc, :], obr[br][:, kc, u0:u0 + n], kc == 0, kc == 7,
                                    [wk, ("obr", br)], [("ps", bk)])
                    a_, b_, c_ = m1[st], m2[st], m3[st]
                    self.tt(a_[:, :n], self.bank(st * 3, n), g_[:, 0, u0:u0 + n], ALU.mult, [("ps", st * 3), (gk, 0)],
                            [("m1", st)])
                    self.tt(b_[:, :n], self.bank(st * 3 + 1, n), g_[:, 1, u0:u0 + n], ALU.mult,
                            [("ps", st * 3 + 1), (gk, 1)], [("m2", st)])
                    self.tt(c_[:, :n], self.bank(st * 3 + 2, n), g_[:, 2, u0:u0 + n], ALU.mult,
                            [("ps", st * 3 + 2), (gk, 2)], [("m3", st)])
                    self.tt(a_[:, :n], a_[:, :n], b_[:, :n], ALU.add, [("m1", st), ("m2", st)], [("m1", st)])
                    self.tt(mT[:, oc, u0:u0 + n], a_[:, :n], c_[:, :n], ALU.add, [("m1", st), ("m3", st)], [("mT", oc)])
            for oc2 in range(16):
                w_ = wo[oc2 % 2]
                wk = ("wo", oc2 % 2)
                self.dma("pool", w_, W["wo"][oc2], [], [wk], wk)
                st = oc2 % 2
                ybanks = (st * 2, st * 2 + 1)
                for pi, (c0, n, j, on) in enumerate(parts):
                    if not on:
                        continue
                    u0 = 0 if pi == 0 else 512
                    for oc in range(16):
                        self.mm(self.bank(ybanks[pi], n), w_[:, oc, :], mT[:, oc, u0:u0 + n], oc == 0, oc == 15,
                                [wk, ("mT", oc)], [("ps", ybanks[pi])])
                self.post_evac(pb, b, oc2, parts, ybanks, (6, 7), slot)
            for _ in postg:
                pass
            postg = self.post_gen(pb, l, 1, b, parts, (6, 7), slot, self.hT, self.hT, False)
        for _ in postg:
            pass
        self.rstd_valid = True
        ar.pop()


def _rope_tables():
    half = 16
    inv = (10000.0 ** (-np.arange(half, dtype=np.float32) / half)).astype(np.float32)
    pos = np.arange(2048)
    prow = (pos // 64).astype(np.float32)
    pcol = (pos % 64).astype(np.float32)
    cosT = np.zeros((128, 2048), np.float32)
    sinT = np.zeros((128, 2048), np.float32)
    perm = np.zeros((128, 128), np.float32)
    for p in range(128):
        dd = p % 64
        axis = dd // 32
        jj = dd % 32
        i = jj % 16
        first = jj < 16
        ang = (prow if axis == 0 else pcol) * inv[i]
        cosT[p] = np.cos(ang.astype(np.float32))
        s = np.sin(ang.astype(np.float32))
        sinT[p] = -s if first else s
        partner = p + 16 if first else p - 16
        perm[partner, p] = 1.0
    return cosT, sinT, perm


def _ret_tables():
    kl = np.arange(128, dtype=np.float64)[:, None]
    x = np.arange(512, dtype=np.float64)[None, :]
    tabs = np.zeros((128, 10, 512), np.float32)
    tabs[:, 0] = x - kl
    tabs[:, 1] = 512 - x + kl
    for a in range(4):
        z = x - 128 * a - kl
        tabs[:, 2 + a] = np.where(z >= 0, z, BIGZ)
        tabs[:, 6 + a] = np.where(z <= 0, -z, BIGZ)
    rows = np.tile((128.0 * np.arange(20, dtype=np.float32))[None, :], (128, 1)).astype(np.float32)
    return tabs, rows


def _rpb_expand(rpb):
    WIN_R, WIN_C, GW, ROWS = 8, 16, 64, 32
    out = np.zeros((8, 128, 25, 128), np.float32)
    seen = {}
    for i in range(16):
        cls = nb_cls(i)
        jl = nb_jlist(i)
        q = np.arange(128)
        qr = 2 * i + q // 64
        qc = q % 64
        rs = np.clip(qr - 4, 0, ROWS - WIN_R)
        cstart = np.clip(qc - WIN_C // 2, 0, GW - WIN_C)
        for slot, j in enumerate(jl):
            k = np.arange(128)
            kr = 2 * j + k // 64
            kc = k % 64
            row_ok = (kr[:, None] >= rs[None, :]) & (kr[:, None] < rs[None, :] + WIN_R)
            col_ok = (kc[:, None] >= cstart[None, :]) & (kc[:, None] < cstart[None, :] + WIN_C)
            dr = np.clip(kr[:, None] - qr[None, :] + WIN_R - 1, 0, 2 * WIN_R - 2)
            dc = np.clip(kc[:, None] - qc[None, :] + WIN_C - 1, 0, 2 * WIN_C - 2)
            ok = row_ok & col_ok
            key = (cls, slot)
            sig = (ok.tobytes(), dr.tobytes(), dc.tobytes(), j - i)
            if key in seen:
                assert seen[key] == sig, f"class structure mismatch {i} {slot}"
                continue
            seen[key] = sig
            vals = rpb[:, dr, dc]
            out[:, :, cls * 5 + slot, :] = np.where(ok[None], vals, np.float32(-30000.0))
    return out


def _fm(v):
    return np.ascontiguousarray(v.reshape(16, 128).T)


def host_weights(inp, L=2):
    m = {}
    cosT, sinT, perm = _rope_tables()
    retz, rows = _ret_tables()
    m["cosT"], m["sinT"], m["permM"], m["retz"], m["rows128"] = cosT, sinT, perm, retz, rows
    m["identM"] = np.eye(128, dtype=np.float32)
    for l in range(L):
        wm = inp["w_mod"][l].reshape(16, 128, 36, 4, 128)
        m[f"wmod{l}"] = np.ascontiguousarray(wm.transpose(2, 1, 3, 0, 4))
        m[f"bmod{l}"] = np.ascontiguousarray(inp["b_mod"][l].reshape(144, 128).T)
        m[f"pre{l}"] = np.ascontiguousarray(inp["pre_norm"][l].reshape(3, 16, 128).transpose(2, 0, 1))
        m[f"post{l}"] = np.ascontiguousarray(inp["post_norm"][l].reshape(3, 16, 128).transpose(2, 0, 1))
        for i in range(2):
            wi = inp["ffn_w_in"][l, i].reshape(16, 128, 2, 44, 128)
            m[f"fwin{l}_{i}"] = np.ascontiguousarray(wi.transpose(3, 1, 2, 0, 4))
            wo = inp["ffn_w_out"][l, i].reshape(44, 128, 16, 128)
            m[f"fwout{l}_{i}"] = np.ascontiguousarray(wo.transpose(2, 1, 0, 3))
        w = inp["w_in"][l].reshape(16, 128, 30, 4, 128)
        fm = w.transpose(2, 1, 3, 0, 4).reshape(30, 128, 8192)
        tm = w.transpose(2, 1, 0, 3, 4).reshape(30, 128, 8192)
        win = np.ascontiguousarray(fm)
        for g in (4, 5, 10, 11, 14, 15):
            win[g] = tm[g]
        m[f"win{l}"] = win
        wb = inp["w_branch"][l].reshape(3, 8, 128, 16, 128)
        m[f"wbr{l}"] = np.ascontiguousarray(wb.transpose(3, 2, 0, 1, 4))
        wo_ = inp["w_out"][l].reshape(16, 128, 16, 128)
        m[f"wo{l}"] = np.ascontiguousarray(wo_.transpose(2, 1, 0, 3))
        m[f"dlam{l}"] = np.ascontiguousarray(np.tile(inp["diff_lambda"][l].reshape(1, 256), (128, 1)))
        m[f"dnorm{l}"] = np.ascontiguousarray(inp["diff_norm"][l].reshape(128, 1))
        m[f"dnrow{l}"] = np.ascontiguousarray(np.tile(inp["diff_norm"][l].reshape(1, 128), (128, 1)))
        m[f"rnorm{l}"] = np.ascontiguousarray(inp["ret_norm"][l].reshape(128, 1))
        m[f"rdecay{l}"] = np.ascontiguousarray(np.tile(inp["ret_decay"][l].reshape(1, 16), (128, 1)))
        m[f"rpbx{l}"] = _rpb_expand(inp["na_rpb"][l])
    return m


def host_core_inputs(inp, b):
    x = inp["x"][b]
    ctx = inp["ctx"][b]
    tok = np.concatenate([x[:512], ctx, x[512:]], axis=0)
    xT = np.ascontiguousarray(tok.T.reshape(16, 128, T))
    cv = np.stack([_fm(inp["c"][b]), _fm(inp["c_ctx"])], axis=-1)
    return {"xT": xT, "cvec": np.ascontiguousarray(cv)}


_CACHE = {}


def kernel(**inputs):
    inp = {k: np.asarray(v, dtype=np.float32) for k, v in inputs.items()}
    if "prog" not in _CACHE:
        p = Prog(2)
        p.build()
        _CACHE["prog"] = p
    p = _CACHE["prog"]
    wm = host_weights(inp)
    in_maps = []
    for b in range(8):
        d = dict(wm)
        d.update(host_core_inputs(inp, b))
        in_maps.append(d)
    res = run_bass_kernel_spmd(p.nc, in_maps, core_ids=list(range(8)))
    out = np.empty((8, 2048, 2048), np.float32)
    for b in range(8):
        oT = res.results[b]["outT"]
        out[b] = oT.reshape(2048, 2048).T
    return out
```
